# Optimizing a Trainium2 kernel written in Bass

```python
import jax, jax.numpy as jnp
from jax import lax
import numpy as np

D_MODEL = 1024
BATCH = 32
SEQ = 256
DEPTH = 4
DEC_BATCH = 4
DEC_SEQ = 1024
PAST_LEN = 256

GRID_W = 64
N_HEADS = 16
HEAD_DIM = D_MODEL // N_HEADS
WIN_ROWS_MAX = 8
WIN_COLS = 16
Q_COL_BLOCK = WIN_COLS
K_COL_BLOCK = 2 * WIN_COLS
CONV_WIDTH = 31
D_FF = 4 * D_MODEL
N_MIXERS = 2
N_ATTN = (DEPTH + 1) // 2
N_CONV = DEPTH // 2
RMS_EPS = 1e-6
LN_EPS = 1e-5

kernel_name = "hybrid_natten_conformer_flow_step"


def rms_norm(x, g):
    xf = x.astype(jnp.float32)
    y = xf * lax.rsqrt(jnp.mean(xf * xf, axis=-1, keepdims=True) + RMS_EPS)
    return (y * g.astype(jnp.float32)).astype(x.dtype)


def layer_norm(x, g, b):
    xf = x.astype(jnp.float32)
    mu = jnp.mean(xf, axis=-1, keepdims=True)
    xc = xf - mu
    y = xc * lax.rsqrt(jnp.mean(xc * xc, axis=-1, keepdims=True) + LN_EPS)
    return (y * g.astype(jnp.float32) + b.astype(jnp.float32)).astype(x.dtype)


def adaln(cond, w, b):
    m = jax.nn.silu(cond) @ w + b
    return jnp.split(m, 6, axis=-1)


def modulate(h, shift, scale):
    return h * (1 + scale) + shift


def split_heads(x):
    return x.reshape(x.shape[0], x.shape[1], N_HEADS, HEAD_DIM)


def _na_indices(rows):
    kr = min(WIN_ROWS_MAX, rows)
    r = np.arange(rows)
    row_start = np.clip(r - kr // 2, 0, rows - kr)
    row_idx = row_start[:, None] + np.arange(kr)[None, :]
    n_cb = GRID_W // Q_COL_BLOCK
    cb = np.arange(n_cb)
    kcol_start = np.clip(cb * Q_COL_BLOCK - WIN_COLS // 2, 0, GRID_W - K_COL_BLOCK)
    col_idx = kcol_start[:, None] + np.arange(K_COL_BLOCK)[None, :]
    qcol = cb[:, None] * Q_COL_BLOCK + np.arange(Q_COL_BLOCK)[None, :]
    win_start = np.clip(qcol - WIN_COLS // 2, 0, GRID_W - WIN_COLS)
    kc = col_idx[:, None, :]
    col_valid = (kc >= win_start[..., None]) & (kc < win_start[..., None] + WIN_COLS)
    mask = np.broadcast_to(col_valid[:, :, None, :], (n_cb, Q_COL_BLOCK, kr, K_COL_BLOCK))
    mask = mask.reshape(n_cb, Q_COL_BLOCK, kr * K_COL_BLOCK)
    d_row = row_idx - r[:, None] + (WIN_ROWS_MAX - 1)
    d_col = np.clip(kc - qcol[..., None] + (WIN_COLS - 1), 0, 2 * WIN_COLS - 2)
    return kr, n_cb, row_idx, col_idx, mask, d_row, d_col


def na_context(h, w_qkv, w_o):
    q, k, v = jnp.split(h @ w_qkv, 3, axis=-1)
    q, k, v = split_heads(q), split_heads(k), split_heads(v)
    s = jnp.einsum('bqhd,bkhd->bhqk', q, k).astype(jnp.float32) * (HEAD_DIM ** -0.5)
    p = jax.nn.softmax(s, axis=-1).astype(v.dtype)
    o = jnp.einsum('bhqk,bkhd->bqhd', p, v).reshape(h.shape[0], h.shape[1], D_MODEL)
    return o @ w_o, k, v


def na_latent(h, k_ctx, v_ctx, w_qkv, w_o, rpb):
    b, t = h.shape[0], h.shape[1]
    rows = t // GRID_W
    kr, n_cb, row_idx, col_idx, mask, d_row, d_col = _na_indices(rows)
    q, k, v = jnp.split(h @ w_qkv, 3, axis=-1)
    qg = q.reshape(b, rows, n_cb, Q_COL_BLOCK, N_HEADS, HEAD_DIM)
    kg = k.reshape(b, rows, GRID_W, N_HEADS, HEAD_DIM)
    vg = v.reshape(b, rows, GRID_W, N_HEADS, HEAD_DIM)
    ri = row_idx[:, None, :, None]
    ci = col_idx[None, :, None, :]
    n_loc = kr * K_COL_BLOCK
    kb = kg[:, ri, ci].reshape(b, rows, n_cb, n_loc, N_HEADS, HEAD_DIM)
    vb = vg[:, ri, ci].reshape(b, rows, n_cb, n_loc, N_HEADS, HEAD_DIM)
    rel = rpb[:, d_row[:, None, None, :, None], d_col[None, :, :, None, :]]
    rel = rel.reshape(N_HEADS, rows, n_cb, Q_COL_BLOCK, n_loc).astype(jnp.float32)
    bias = jnp.where(mask, rel, -jnp.inf)
    scale = HEAD_DIM ** -0.5
    s_loc = jnp.einsum('brjqhd,brjkhd->bhrjqk', qg, kb).astype(jnp.float32) * scale + bias[None]
    s_ctx = jnp.einsum('brjqhd,blhd->bhrjql', qg, k_ctx).astype(jnp.float32) * scale
    p = jax.nn.softmax(jnp.concatenate([s_loc, s_ctx], axis=-1), axis=-1).astype(v.dtype)
    o = (jnp.einsum('bhrjqk,brjkhd->brjqhd', p[..., :n_loc], vb)
         + jnp.einsum('bhrjql,blhd->brjqhd', p[..., n_loc:], v_ctx))
    return o.reshape(b, t, D_MODEL) @ w_o


def conv_module(h, w_pw1, w_dw, b_dw, ln_g, ln_b, w_pw2):
    a, g = jnp.split(h @ w_pw1, 2, axis=-1)
    u = a * jax.nn.sigmoid(g)
    u = lax.conv_general_dilated(
        u, w_dw[:, None, :].astype(u.dtype), window_strides=(1,),
        padding=[(CONV_WIDTH // 2, CONV_WIDTH // 2)],
        dimension_numbers=('NWC', 'WIO', 'NWC'), feature_group_count=D_MODEL) + b_dw
    u = jax.nn.silu(layer_norm(u, ln_g, ln_b))
    return u @ w_pw2


def sq_relu_mlp(h, w_up, w_down):
    return jnp.square(jax.nn.relu(h @ w_up)) @ w_down


def setup_inputs(seed: int = 0) -> dict:
    key = jax.random.key(seed)
    ks = jax.random.split(key, 21)
    d = D_MODEL

    def nrm(k, shape, s):
        return jax.random.normal(k, shape, jnp.float32) * s

    return {
        "x_prompt": nrm(ks[0], (BATCH, SEQ, d), 1.0),
        "x_sample": nrm(ks[1], (DEC_BATCH, DEC_SEQ, d), 1.0),
        "cache_k": nrm(ks[2], (DEC_BATCH, N_ATTN, PAST_LEN, N_HEADS, HEAD_DIM), 1.0),
        "cache_v": nrm(ks[3], (DEC_BATCH, N_ATTN, PAST_LEN, N_HEADS, HEAD_DIM), 1.0),
        "c": nrm(ks[4], (DEC_BATCH, d), 1.0),
        "c_ctx": nrm(ks[5], (d,), 1.0),
        "norm_g": 1.0 + nrm(ks[6], (DEPTH, 2, d), 0.02),
        "w_ada": nrm(ks[7], (DEPTH, d, 6 * d), 0.5 * d ** -0.5),
        "b_ada": nrm(ks[8], (DEPTH, 6 * d), 0.01),
        "w_qkv": nrm(ks[9], (N_ATTN, d, 3 * d), d ** -0.5),
        "w_o": nrm(ks[10], (N_ATTN, d, d), d ** -0.5),
        "rpb": nrm(ks[11], (N_ATTN, N_HEADS, 2 * WIN_ROWS_MAX - 1, 2 * WIN_COLS - 1), 0.1),
        "w_pw1": nrm(ks[12], (N_CONV, d, 2 * d), d ** -0.5),
        "w_dw": nrm(ks[13], (N_CONV, CONV_WIDTH, d), CONV_WIDTH ** -0.5),
        "b_dw": nrm(ks[14], (N_CONV, d), 0.01),
        "conv_ln_g": 1.0 + nrm(ks[15], (N_CONV, d), 0.02),
        "conv_ln_b": nrm(ks[16], (N_CONV, d), 0.01),
        "w_pw2": nrm(ks[17], (N_CONV, d, d), d ** -0.5),
        "w_up": nrm(ks[18], (DEPTH, d, D_FF), d ** -0.5),
        "w_down": nrm(ks[19], (DEPTH, D_FF, d), D_FF ** -0.5),
        "final_g": 1.0 + nrm(ks[20], (d,), 0.02),
    }


def reference(x_prompt, x_sample, cache_k, cache_v, c, c_ctx, norm_g, w_ada, b_ada,
              w_qkv, w_o, rpb, w_pw1, w_dw, b_dw, conv_ln_g, conv_ln_b, w_pw2,
              w_up, w_down, final_g):
    xp, xs = x_prompt, x_sample
    new_k, new_v = [], []
    for l in range(DEPTH):
        i = l // N_MIXERS
        sh1c, sc1c, g1c, sh2c, sc2c, g2c = adaln(c_ctx, w_ada[l], b_ada[l])
        sh1, sc1, g1, sh2, sc2, g2 = [m[:, None, :] for m in adaln(c, w_ada[l], b_ada[l])]
        hp = modulate(rms_norm(xp, norm_g[l, 0]), sh1c, sc1c)
        hs = modulate(rms_norm(xs, norm_g[l, 0]), sh1, sc1)
        if l % N_MIXERS == 0:
            yp, kp, vp = na_context(hp, w_qkv[i], w_o[i])
            new_k.append(kp)
            new_v.append(vp)
            ys = na_latent(hs, cache_k[:, i], cache_v[:, i], w_qkv[i], w_o[i], rpb[i])
        else:
            yp = conv_module(hp, w_pw1[i], w_dw[i], b_dw[i], conv_ln_g[i], conv_ln_b[i], w_pw2[i])
            ys = conv_module(hs, w_pw1[i], w_dw[i], b_dw[i], conv_ln_g[i], conv_ln_b[i], w_pw2[i])
        xp = xp + g1c * yp
        xs = xs + g1 * ys
        hp = modulate(rms_norm(xp, norm_g[l, 1]), sh2c, sc2c)
        hs = modulate(rms_norm(xs, norm_g[l, 1]), sh2, sc2)
        xp = xp + g2c * sq_relu_mlp(hp, w_up[l], w_down[l])
        xs = xs + g2 * sq_relu_mlp(hs, w_up[l], w_down[l])
    y_prompt = rms_norm(xp, final_g)
    y_sample = rms_norm(xs, final_g)
    new_cache_k = jnp.stack(new_k, axis=1)
    new_cache_v = jnp.stack(new_v, axis=1)
    return (y_prompt, y_sample, new_cache_k, new_cache_v)
```

```python
import numpy as np
import concourse.bass as bass
import concourse.mybir as mybir
from concourse.bass_utils import run_bass_kernel_spmd

F32 = mybir.dt.float32
F32R = mybir.dt.float32r
BF16 = mybir.dt.bfloat16
U8 = mybir.dt.uint8
AF = mybir.ActivationFunctionType
ALU = mybir.AluOpType

D = 1024
TOK = 1536
NT = 3
DEPTH = 4
NEG = -30000.0
SEGP = 286


def piece_plan():
    plan = []
    for l in range(DEPTH):
        for j in range(12):
            plan.append(("ada", l, j))
        if l % 2 == 0:
            for j in range(4):
                plan.append(("qk", l, j))
            for j in range(2):
                plan.append(("v", l, j))
            for j in range(2):
                plan.append(("o", l, j))
        else:
            for j in range(4):
                plan.append(("pw1", l, j))
            for j in range(2):
                plan.append(("pw2", l, j))
        for hf in range(2):
            for j in range(4):
                plan.append(("up", l, hf * 4 + j))
            for j in range(4):
                plan.append(("down", l, hf * 4 + j))
    return plan


NPIECE = len(piece_plan())
PLAN = []


def tile_kf(w, f0, nf):
    return np.ascontiguousarray(
        w[:, f0:f0 + nf].reshape(8, 128, nf).transpose(1, 0, 2)).reshape(128, 8 * nf)


def build_wall(inp):
    wall = np.empty((NPIECE, 128, 4096), np.float32)
    for n, (kind, l, j) in enumerate(PLAN):
        i = l // 2
        if kind == "ada":
            wall[n] = tile_kf(inp["w_ada"][l], j * 512, 512)
        elif kind == "qk":
            wall[n] = tile_kf(inp["w_qkv"][i], j * 512, 512)
        elif kind == "v":
            wall[n] = tile_kf(inp["w_qkv"][i], 2048 + j * 512, 512)
        elif kind == "o":
            wall[n] = tile_kf(inp["w_o"][i], j * 512, 512)
        elif kind == "pw1":
            w = inp["w_pw1"][i]
            cols = np.concatenate([np.arange(256 * j, 256 * j + 256),
                                   1024 + np.arange(256 * j, 256 * j + 256)])
            wall[n] = tile_kf(w[:, cols], 0, 512)
        elif kind == "pw2":
            wall[n] = tile_kf(inp["w_pw2"][i], j * 512, 512)
        elif kind == "up":
            wall[n] = tile_kf(inp["w_up"][l], j * 512, 512)
        elif kind == "down":
            hf, jj = divmod(j, 4)
            w = inp["w_down"][l][hf * 2048:(hf + 1) * 2048, jj * 256:(jj + 1) * 256]
            wall[n] = np.ascontiguousarray(
                w.reshape(16, 128, 256).transpose(1, 0, 2)).reshape(128, 4096)
    return wall


VOFF = {}
_nv = 0
for _name, _n in [("cond", 16), ("bada", 192), ("ng", 64), ("fg", 8), ("bdw", 16), ("lng", 16),
                  ("lnb", 16), ("wdw", 496), ("gate", 128), ("ctxg", 1), ("flag", 1),
                  ("cmask", 64), ("epsr", 1), ("epsl", 1), ("J", 64), ("ident", 128)]:
    VOFF[_name] = _nv
    _nv += _n
NV = _nv


def fm(v):
    return np.ascontiguousarray(np.asarray(v, np.float32).reshape(8, 128).T)


def row_start(r):
    return int(np.clip(r - 4, 0, 8))


def chunk_rows(c):
    out = []
    for rq in range(16):
        rs = row_start(rq)
        if rs <= 2 * c + 1 and rs + 7 >= 2 * c:
            out.append(rq)
    return out


def build_gtab(is_s):
    g = np.zeros((16, 2048), np.float32)
    for rq in range(16):
        for c in range(8):
            for rl in range(2):
                rk = 2 * c + rl
                if is_s:
                    rs = row_start(rq)
                    ok = rs <= rk < rs + 8
                else:
                    ok = (rk // 4) == (rq // 4)
                g[rq, c * 128 + rl * 64:c * 128 + (rl + 1) * 64] = 0.0 if ok else NEG
        g[rq, 1024 + rq * 64:1024 + (rq + 1) * 64] = 1.0
    return g


def build_vecs(inp, sample_b):
    v = np.zeros((128, NV), np.float32)
    is_s = sample_b is not None
    condA = inp["c"][sample_b] if is_s else inp["c_ctx"]
    cd = np.stack([fm(condA), fm(inp["c_ctx"])], axis=-1)
    v[:, VOFF["cond"]:VOFF["cond"] + 16] = cd.reshape(128, 16)
    for l in range(DEPTH):
        b = np.asarray(inp["b_ada"][l], np.float32).reshape(48, 128).T
        v[:, VOFF["bada"] + l * 48:VOFF["bada"] + (l + 1) * 48] = b
        for s in range(2):
            o = VOFF["ng"] + (l * 2 + s) * 8
            v[:, o:o + 8] = fm(inp["norm_g"][l, s])
    v[:, VOFF["fg"]:VOFF["fg"] + 8] = fm(inp["final_g"])
    for i in range(2):
        v[:, VOFF["bdw"] + i * 8:VOFF["bdw"] + (i + 1) * 8] = fm(inp["b_dw"][i])
        v[:, VOFF["lng"] + i * 8:VOFF["lng"] + (i + 1) * 8] = fm(inp["conv_ln_g"][i])
        v[:, VOFF["lnb"] + i * 8:VOFF["lnb"] + (i + 1) * 8] = fm(inp["conv_ln_b"][i])
        w = np.asarray(inp["w_dw"][i], np.float32)
        wf = w.reshape(31, 8, 128).transpose(2, 1, 0)
        v[:, VOFF["wdw"] + i * 248:VOFF["wdw"] + (i + 1) * 248] = wf.reshape(128, 248)
    g = np.zeros((128, 8, 16), np.float32)
    for c in range(8):
        for rq in range(16):
            for rl in range(2):
                rk = 2 * c + rl
                if is_s:
                    rs = row_start(rq)
                    ok = rs <= rk < rs + 8
                else:
                    ok = (rk // 4) == (rq // 4)
                g[rl * 64:(rl + 1) * 64, c, rq] = 0.0 if ok else NEG
    v[:, VOFF["gate"]:VOFF["gate"] + 128] = g.reshape(128, 128)
    v[:, VOFF["ctxg"]] = 0.0 if is_s else NEG
    v[:, VOFF["flag"]] = 1.0 if is_s else 0.0
    cm = np.zeros((64, 64), np.float32)
    if is_s:
        for cq in range(64):
            ws = int(np.clip(cq - 8, 0, 48))
            for ck in range(64):
                if not (ws <= ck < ws + 16):
                    cm[ck, cq] = NEG
    v[0:64, VOFF["cmask"]:VOFF["cmask"] + 64] = cm
    v[64:128, VOFF["cmask"]:VOFF["cmask"] + 64] = cm
    v[:, VOFF["epsr"]] = 1e-6
    v[:, VOFF["epsl"]] = 1e-5
    v[0:64, VOFF["J"]:VOFF["J"] + 64] = np.eye(64, dtype=np.float32)[::-1]
    v[:, VOFF["ident"]:VOFF["ident"] + 128] = np.eye(128, dtype=np.float32)
    return v


class Sem:
    def __init__(self, h, name):
        self.h = h
        self.name = name
        self.count = 0


class Res:
    __slots__ = ("name", "w", "r")

    def __init__(self, name, init=None):
        self.name = name
        self.w = None
        self.r = dict(init) if init else {}

    def events(self):
        ev = dict(self.r)
        if self.w is not None:
            s, v = self.w
            if ev.get(s, 0) < v:
                ev[s] = v
        return ev


class Eng:
    def __init__(self, name, sem):
        self.name = name
        self.sem = sem
        self.items = []
        self.waited = {}

    def need(self, sem, val):
        if self.waited.get(sem, 0) >= val:
            return
        self.waited[sem] = val
        self.items.append(("wait", sem, val))


class Prog:
    def __init__(self, nc):
        self.nc = nc
        self.sems = []
        self.eng = {}

    def new_sem(self, name):
        s = Sem(None, name)
        self.sems.append(s)
        return s

    def add_engine(self, name):
        self.eng[name] = Eng(name, self.new_sem("e_" + name))

    def _deps(self, eng, reads, writes, extra):
        for r in reads:
            if r.w is not None:
                s, v = r.w
                if s is eng.sem and eng.name == "pe":
                    continue
                eng.need(s, v)
        for w in writes:
            if w.w is not None:
                s, v = w.w
                if not (s is eng.sem and eng.name == "pe"):
                    eng.need(s, v)
            for s, v in w.r.items():
                if s is eng.sem and eng.name == "pe":
                    continue
                eng.need(s, v)
        for s, v in extra:
            if s is eng.sem:
                continue
            eng.need(s, v)

    def op(self, ename, fn, reads=(), writes=(), extra=()):
        eng = self.eng[ename]
        self._deps(eng, reads, writes, extra)
        eng.sem.count += 1
        ev = (eng.sem, eng.sem.count)
        eng.items.append(("op", fn, eng.sem, 1))
        for r in reads:
            if r.r.get(ev[0], 0) < ev[1]:
                r.r[ev[0]] = ev[1]
        for w in writes:
            w.w = ev
            w.r = {}
        return ev

    def dma(self, qname, fn, sem, reads=(), writes=(), extra=()):
        eng = self.eng[qname]
        self._deps(eng, reads, writes, extra)
        sem.count += 16
        ev = (sem, sem.count)
        eng.items.append(("op", fn, sem, 16))
        for r in reads:
            if r.r.get(sem, 0) < ev[1]:
                r.r[sem] = ev[1]
        for w in writes:
            w.w = ev
            w.r = {}
        return ev

    def emit(self):
        nc = self.nc
        from contextlib import ExitStack
        with ExitStack() as st:
            for s in self.sems:
                s.h = st.enter_context(nc.semaphore(s.name))
            block = st.enter_context(nc.Block())

            def runner(eng):
                def body(e):
                    for it in eng.items:
                        if it[0] == "wait":
                            e.wait_ge(it[1].h, it[2])
                        else:
                            ins = it[1](e)
                            ins.then_inc(it[2].h, it[3])
                return body

            block.tensor(runner(self.eng["pe"]))
            block.scalar(runner(self.eng["act"]))
            block.vector(runner(self.eng["dve"]))
            block.gpsimd(runner(self.eng["pool"]))
            block.sync(runner(self.eng["sp"]))


class Region:
    def __init__(self, nc, name, nbytes):
        self.t = nc.alloc_sbuf_tensor(name, [128, nbytes], U8)
        self.nbytes = nbytes
        self.name = name
        self.live = []
        self.inherit = {}
        self.off = 0
        self.n = 0

    def reset(self):
        for r in self.live:
            for s, v in r.events().items():
                if self.inherit.get(s, 0) < v:
                    self.inherit[s] = v
        self.live = []
        self.off = 0

    def view(self, off, nbytes, dtype):
        return self.t[:, off:off + nbytes].bitcast(dtype)

    def alloc(self, nbytes, dtype, name=None):
        nbytes = (nbytes + 31) // 32 * 32
        assert self.off + nbytes <= self.nbytes, (self.name, name, self.off, nbytes, self.nbytes)
        ap = self.t[:, self.off:self.off + nbytes].bitcast(dtype)
        self.last_off = self.off
        self.off += nbytes
        self.n += 1
        r = Res(f"{self.name}_{name}_{self.n}", init=self.inherit)
        self.live.append(r)
        return ap, r


def build_program():
    nc = bass.Bass("TRN2", target_bir_lowering=False)
    xT_d = nc.dram_tensor("xT", [D, TOK], F32, kind="ExternalInput").ap()
    wall_d = nc.dram_tensor("wall", [NPIECE, 128, 4096], F32, kind="ExternalInput").ap()
    vec_d = nc.dram_tensor("vecs", [128, NV], F32, kind="ExternalInput").ap()
    rpb_h = nc.dram_tensor("rpbp", [2, 7696], F32, kind="ExternalInput")
    ckT_d = nc.dram_tensor("ckT", [2, D, 256], F32, kind="ExternalInput").ap()
    cv_d = nc.dram_tensor("cv", [2, 256, D], F32, kind="ExternalInput").ap()
    gt_d = nc.dram_tensor("gtab", [16, 2048], F32, kind="ExternalInput").ap()
    yT_d = nc.dram_tensor("yT", [D, TOK], F32, kind="ExternalOutput").ap()
    kT_d = nc.dram_tensor("kTo", [2, D, TOK], F32, kind="ExternalOutput").ap()
    vo_d = nc.dram_tensor("vo", [2, TOK, D], F32, kind="ExternalOutput").ap()

    P = Prog(nc)
    for e in ("pe", "act", "dve", "pool", "sp"):
        P.add_engine(e)

    xT = nc.alloc_sbuf_tensor("xTs", [128, 8, TOK], F32)
    xT_r = [[Res(f"x{c}_{t}") for t in range(NT)] for c in range(8)]
    hT = nc.alloc_sbuf_tensor("hTs", [128, 8, TOK], BF16)
    hT_r = [Res(f"h{t}") for t in range(NT)]
    vecs = nc.alloc_sbuf_tensor("vecs_s", [128, NV], F32)
    vecs_r = Res("vecs")
    mods = [nc.alloc_sbuf_tensor(f"mod{n}", [128, 48, 2], F32) for n in range(2)]
    mods_r = [[Res(f"mod{n}_{j}") for j in range(12)] for n in range(2)]
    coefs = [nc.alloc_sbuf_tensor(f"coef{n}", [128, 2, 8, 2], F32) for n in range(2)]
    coefs_r = [[Res(f"coef{n}_{k}") for k in range(2)] for n in range(2)]
    cur = {"mod": mods[0], "mod_r": mods_r[0], "coef": coefs[0], "coef_r": coefs_r[0]}
    scond = nc.alloc_sbuf_tensor("scond", [128, 8, 2], BF16)
    scond_r = Res("scond")
    ident_b = nc.alloc_sbuf_tensor("identb", [128, 128], BF16)
    ones_b = nc.alloc_sbuf_tensor("onesb", [128, 128], BF16)
    ones_f = nc.alloc_sbuf_tensor("onesf", [128, 128], F32)
    Jb = nc.alloc_sbuf_tensor("Jb", [128, 64], BF16)
    const_r = Res("const")
    NSLOT = 3
    ring = [nc.alloc_sbuf_tensor(f"ring{i}", [128, 4096], BF16) for i in range(NSLOT)]
    ring_r = [Res(f"ring{i}") for i in range(NSLOT)]
    ring_sem = [P.new_sem(f"ringsem{i}") for i in range(NSLOT)]
    big = Region(nc, "big", 77824)
    scr = Region(nc, "scr", 29184)
    ps = [nc.alloc_psum_tensor(f"ps{i}", [128, 512], F32) for i in range(8)]
    ps_r = [Res(f"ps{i}") for i in range(8)]

    def V(name, n=1, j=0):
        o = VOFF[name] + j
        return vecs[:, o:o + n]

    st = {"next_load": 0, "next_use": 0, "psi": 0}

    def load_piece():
        n = st["next_load"]
        if n >= NPIECE:
            return
        st["next_load"] += 1
        s = n % NSLOT
        P.dma("pool", lambda e, n=n, s=s: e.dma_start(out=ring[s][:, :], in_=wall_d[n],
                                                      max_dma_last_dim=8192),
              ring_sem[s], writes=[ring_r[s]])

    def use_piece(kind, l, j):
        n = st["next_use"]
        PLAN.append((kind, l, j))
        st["next_use"] += 1
        return n % NSLOT

    def done_piece():
        load_piece()

    for _ in range(NSLOT):
        load_piece()

    def next_ps(lo=0, hi=8):
        i = st["psi"]
        if i < lo or i >= hi:
            i = lo
        st["psi"] = i + 1
        return i

    sem_v = P.new_sem("ld_vecs")
    P.dma("sp", lambda e: e.dma_start(out=vecs[:, :], in_=vec_d), sem_v, writes=[vecs_r])
    sem_x = [P.new_sem(f"ld_x{t}") for t in range(NT)]
    for t in range(NT):
        P.dma("sp", lambda e, t=t: e.dma_start(
            out=xT[:, :, t * 512:(t + 1) * 512],
            in_=xT_d.rearrange("(c p) t -> p c t", p=128)[:, :, t * 512:(t + 1) * 512]),
            sem_x[t], writes=[xT_r[c][t] for c in range(8)])

    P.op("dve", lambda e: e.tensor_copy(out=ident_b[:, :], in_=V("ident", 128)),
         reads=[vecs_r], writes=[const_r])
    P.op("dve", lambda e: e.tensor_copy(out=Jb[:, :], in_=V("J", 64)), reads=[vecs_r], writes=[const_r])
    P.op("pool", lambda e: e.memset(ones_b[:, :], 1.0), writes=[const_r])
    P.op("pool", lambda e: e.memset(ones_f[:, :], 1.0), writes=[const_r])
    P.op("act", lambda e: e.activation(out=scond[:, :, :].rearrange("p c s -> p (c s)"),
                                       in_=V("cond", 16), func=AF.Silu),
         reads=[vecs_r], writes=[scond_r])

    out_sems = []

    ada = {}

    ada_q = []

    def adaln_begin(l):
        ada_q.extend((l, j) for j in range(12))

    def adaln_piece():
        if not ada_q:
            return
        l, j = ada_q.pop(0)
        pi = next_ps(0, 4) if st.get("in_attn") else next_ps()
        s = use_piece("ada", l, j)

        def fn(e):
            ins = None
            for fc in range(4):
                col = fc * 2
                for kc in range(8):
                    ins = e.matmul(ps[pi][:, col:col + 2],
                                   lhsT=ring[s][:, kc * 512 + fc * 128:kc * 512 + fc * 128 + 128],
                                   rhs=scond[:, kc, :], start=(kc == 0), stop=(kc == 7))
            return ins
        P.op("pe", fn, reads=[ring_r[s], scond_r], writes=[ps_r[pi]])
        done_piece()
        mod, mod_r = mods[l % 2], mods_r[l % 2]
        coef, coef_r = coefs[l % 2], coefs_r[l % 2]
        P.op("dve", lambda e: e.tensor_tensor(
            out=mod[:, j * 4:(j + 1) * 4, :],
            in0=ps[pi][:, 0:8].rearrange("p (j s) -> p j s", s=2),
            in1=V("bada", 4, l * 48 + j * 4).unsqueeze(2).to_broadcast([128, 4, 2]), op=ALU.add),
            reads=[ps_r[pi], vecs_r], writes=[mod_r[j]])
        for k, jj in ((0, 3), (1, 9)):
            if j == jj:
                P.op("dve", lambda e, k=k: e.scalar_tensor_tensor(
                    out=coef[:, k, :, :], in0=mod[:, 8 + 24 * k:16 + 24 * k, :], scalar=1.0,
                    in1=V("ng", 8, (l * 2 + k) * 8).unsqueeze(2).to_broadcast([128, 8, 2]),
                    op0=ALU.add, op1=ALU.mult),
                    reads=[mod_r[jj - 1], mod_r[jj], vecs_r], writes=[coef_r[k]])

    def set_layer(l):
        cur["mod"], cur["mod_r"] = mods[l % 2], mods_r[l % 2]
        cur["coef"], cur["coef_r"] = coefs[l % 2], coefs_r[l % 2]

    def slot_of(t):
        return 0 if t < 2 else 1

    pend = []

    def tick():
        for it in pend:
            it[0] -= 1
        while pend and pend[0][0] <= 0:
            pend.pop(0)[1]()

    def defer(k, fn):
        pend.append([k, fn])

    def flush_deferred():
        while pend:
            pend.pop(0)[1]()

    def norm_begin(kind, l=None, k=None):
        flush_deferred()
        scr.reset()
        ctx = {"kind": kind, "l": l, "k": k, "n": 0}
        ctx["sqb"], ctx["sqb_r"] = [], []
        for n in range(2):
            sqb, r = scr.alloc(8 * 512 * 2, BF16, f"sq{n}")
            ctx["sqb"].append(sqb.rearrange("p (c t) -> p c t", c=8))
            ctx["sqb_r"].append(r)
        ctx["pendB"] = None
        ctx["rstd"], ctx["rstd_r"] = scr.alloc(2048, F32, "rstd")
        ctx["tmp"] = [scr.alloc(2048, F32, f"tmp{i}") for i in range(2)]
        if kind == "final":
            ctx["stg"] = [scr.alloc(2048, F32, f"stg{i}") for i in range(2)]
            ctx["stg_sem"] = [P.new_sem(f"fin_st{i}") for i in range(2)]
            out_sems.extend(ctx["stg_sem"])
        else:
            ctx["coef"], ctx["coef_r"] = coefs[l % 2], coefs_r[l % 2]
            ctx["mod"], ctx["mod_r"] = mods[l % 2], mods_r[l % 2]
        return ctx

    def norm_A(ctx, t):
        sqb = ctx["sqb"][t % 2]
        P.op("act", lambda e: e.activation(out=sqb[:, :, :], in_=xT[:, :, t * 512:(t + 1) * 512],
                                           func=AF.Square),
             reads=[xT_r[c][t] for c in range(8)], writes=[ctx["sqb_r"][t % 2]])

    def norm_B(ctx, t):
        rps, rps_r = norm_B1(ctx, t)
        norm_B2(ctx, t, rps, rps_r)

    def norm_B1(ctx, t):
        sqb, rstd, rstd_r = ctx["sqb"][t % 2], ctx["rstd"], ctx["rstd_r"]
        sqb_r = ctx["sqb_r"][t % 2]
        pi = next_ps()

        def fn(e):
            ins = None
            for kc in range(8):
                ins = e.matmul(ps[pi][:, :], lhsT=ones_b[:, :], rhs=sqb[:, kc, :],
                               start=(kc == 0), stop=(kc == 7))
            return ins
        P.op("pe", fn, reads=[sqb_r, const_r], writes=[ps_r[pi]])
        P.op("act", lambda e: e.activation(out=rstd[:, :], in_=ps[pi][:, :], func=AF.Ln,
                                           bias=V("epsr"), scale=1.0 / D),
             reads=[ps_r[pi], vecs_r], writes=[rstd_r])
        P.op("act", lambda e: e.activation(out=ps[pi][:, :], in_=rstd[:, :], func=AF.Exp, scale=-0.5),
             reads=[rstd_r], writes=[ps_r[pi]])
        return ps[pi], ps_r[pi]

    def norm_B2(ctx, t, rps, rps_r):
        tsl = slice(t * 512, (t + 1) * 512)
        for c in range(8):
            ta, ta_r = ctx["tmp"][ctx["n"] % 2]
            P.op("dve", lambda e, c=c, ta=ta: e.tensor_tensor(
                out=ta[:, :], in0=xT[:, c, tsl], in1=rps[:, :], op=ALU.mult),
                reads=[xT_r[c][t], rps_r], writes=[ta_r])
            if ctx["kind"] == "final":
                sg, sg_r = ctx["stg"][ctx["n"] % 2]
                ssem = ctx["stg_sem"][ctx["n"] % 2]
                P.op("act", lambda e, c=c, ta=ta, sg=sg: e.activation(
                    out=sg[:, :], in_=ta[:, :], func=AF.Identity, scale=V("fg", 1, c)),
                    reads=[ta_r, vecs_r], writes=[sg_r])
                P.dma("sp", lambda e, c=c, sg=sg: e.dma_start(
                    out=yT_d[c * 128:(c + 1) * 128, tsl], in_=sg[:, :]), ssem, reads=[sg_r])
            else:
                k, s_ = ctx["k"], slot_of(t)
                sh0 = 0 if k == 0 else 24
                cf, md = ctx["coef"], ctx["mod"]
                P.op("act", lambda e, c=c, ta=ta: e.activation(
                    out=hT[:, c, tsl], in_=ta[:, :], func=AF.Identity,
                    scale=cf[:, k, c, s_:s_ + 1], bias=md[:, sh0 + c, s_:s_ + 1]),
                    reads=[ta_r, ctx["coef_r"][k], ctx["mod_r"][(sh0 + c) // 4]], writes=[hT_r[t]])
            ctx["n"] += 1

    def norm_tile_done(ctx, lag=3):
        def cb(t):
            norm_A(ctx, t)
            if ctx["pendB"] is not None:
                tp = ctx["pendB"]
                norm_B(ctx, tp)
            ctx["pendB"] = t
            if t == NT - 1:
                defer(lag, lambda: norm_B(ctx, t))
        return cb

    def norm_mod(l, k):
        ctx = norm_begin("mod", l, k)
        for t in range(NT):
            norm_A(ctx, t)
            norm_B(ctx, t)

    def linear_fm(kind, l, npieces, nk, src, src_r, evac, fc_per_piece=4, j0=0, hook=None, pshi=8,
                  last_tile_done=None):
        for j in range(npieces):
            s = use_piece(kind, l, j0 + j)
            for t in range(NT):
                for fc in range(fc_per_piece):
                    pi = next_ps(0, pshi)

                    def fn(e, s=s, fc=fc, t=t, pi=pi):
                        ins = None
                        w = ring[s][:, :].rearrange("p (k f) -> p k f", k=nk)
                        fw = 4096 // nk // fc_per_piece
                        for kc in range(nk):
                            ins = e.matmul(ps[pi][:, :], lhsT=w[:, kc, fc * fw:(fc + 1) * fw],
                                           rhs=src[:, kc, t * 512:(t + 1) * 512],
                                           start=(kc == 0), stop=(kc == nk - 1))
                        return ins
                    P.op("pe", fn, reads=[ring_r[s], src_r[t]], writes=[ps_r[pi]])
                    evac(j, fc, t, pi)
                    tick()
                if last_tile_done is not None and j == npieces - 1:
                    last_tile_done(t)
            done_piece()
            if hook is not None:
                hook()

    def linear_fm_t(kind, l, npieces, src, src_r, evac, tile_done=None):
        slots = [use_piece(kind, l, j) for j in range(npieces)]
        for t in range(NT):
            for j in range(npieces):
                s = slots[j]
                for fc in range(4):
                    pi = next_ps()

                    def fn(e, s=s, fc=fc, t=t, pi=pi):
                        ins = None
                        for kc in range(8):
                            ins = e.matmul(ps[pi][:, :],
                                           lhsT=ring[s][:, kc * 512 + fc * 128:kc * 512 + fc * 128 + 128],
                                           rhs=src[:, kc, t * 512:(t + 1) * 512],
                                           start=(kc == 0), stop=(kc == 7))
                        return ins
                    P.op("pe", fn, reads=[ring_r[s], src_r[t]], writes=[ps_r[pi]])
                    evac(j, fc, t, pi)
                    tick()
            if tile_done is not None:
                tile_done(t)
        for j in range(npieces):
            done_piece()

    def resid_evac(gate_chunk0):
        def evac(j, fc, t, pi, per_piece=4):
            fch = j * per_piece + fc
            s = slot_of(t)
            md = cur["mod"]
            P.op("dve", lambda e: e.scalar_tensor_tensor(
                out=xT[:, fch, t * 512:(t + 1) * 512], in0=ps[pi][:, :],
                scalar=md[:, gate_chunk0 + fch, s:s + 1], in1=xT[:, fch, t * 512:(t + 1) * 512],
                op0=ALU.mult, op1=ALU.add),
                reads=[ps_r[pi], cur["mod_r"][(gate_chunk0 + fch) // 4]], writes=[xT_r[fch][t]])
        return evac

    def mlp(l):
        big.reset()
        hid, _ = big.alloc(16 * TOK * 2, BF16, "hid")
        hid = hid.rearrange("p (c t) -> p c t", c=16)
        hid_r = [Res(f"hid{t}", init=big.inherit) for t in range(NT)]
        big.live.extend(hid_r)
        rt = [big.alloc(2048, F32, f"relu{i}") for i in range(3)]
        cnt = {"n": 0}
        hook, pshi = adaln_piece, 8
        if l + 1 < DEPTH:
            adaln_begin(l + 1)
        for hf in range(2):
            def up_evac(j, fc, t, pi):
                hc = j * 4 + fc
                ta, ta_r = rt[cnt["n"] % 3]
                cnt["n"] += 1
                P.op("act", lambda e: e.activation(out=ta[:, :], in_=ps[pi][:, :], func=AF.Relu),
                     reads=[ps_r[pi]], writes=[ta_r])
                P.op("dve", lambda e: e.tensor_tensor(out=hid[:, hc, t * 512:(t + 1) * 512],
                                                      in0=ta[:, :], in1=ta[:, :], op=ALU.mult),
                     reads=[ta_r], writes=[hid_r[t]])
            linear_fm("up", l, 4, 8, hT, hT_r, up_evac, j0=hf * 4, hook=hook, pshi=pshi)
            g2 = resid_evac(40)

            def down_evac(j, fc, t, pi):
                g2(j, fc, t, pi, per_piece=2)
            ltd = None
            if hf == 1:
                nctx = norm_begin("mod", l + 1, 0) if l + 1 < DEPTH else norm_begin("final")
                ltd = norm_tile_done(nctx, lag=2)
            linear_fm("down", l, 4, 16, hid, hid_r, down_evac, fc_per_piece=2, j0=hf * 4, hook=hook,
                      pshi=pshi, last_tile_done=ltd)

    def final_norm():
        flush_deferred()

    def attn_layer(l):
        i = l // 2
        big.reset()
        qkT, _ = big.alloc(16 * TOK * 2, BF16, "qkT")
        qkT = qkT.rearrange("p (c t) -> p c t", c=16)
        qk_r = [[Res(f"qk{c}_{t}", init=big.inherit) for t in range(NT)] for c in range(16)]
        Vb, _ = big.alloc(12 * 1024 * 2, BF16, "Vb")
        Vb = Vb.rearrange("p (b f) -> p b f", b=12)
        V_r = [Res(f"V{b}", init=big.inherit) for b in range(12)]
        for c in range(16):
            big.live.extend(qk_r[c])
        big.live.extend(V_r)
        gt, gt_r = big.alloc(2048 * 2, BF16, "gtab")
        sem_gt = P.new_sem(f"gt{l}")
        P.dma("pool", lambda e: e.dma_start(out=gt[0:16, :], in_=gt_d), sem_gt, writes=[gt_r])
        P.dma("pool", lambda e: e.dma_start(out=gt[64:80, :], in_=gt_d), sem_gt, writes=[gt_r])

        scr.reset()
        G = []
        for n in range(4):
            ap, r = scr.alloc(2048, F32, f"G{n}")
            G.append((ap, r, scr.last_off))
        zb = [(G[0][0], G[0][1]), (G[1][0], G[1][1])]
        pT = []
        for n in (2, 3):
            bfv = scr.view(G[n][2], 2048, BF16)
            for hh in range(2):
                r = Res(f"pT{n}_{hh}", init=scr.inherit)
                scr.live.append(r)
                pT.append((bfv[:, hh * 512:(hh + 1) * 512], r))
        stg = [(G[0][0], [G[0][1]]), (G[1][0], [G[1][1]]),
               (G[2][0], [pT[0][1], pT[1][1]]), (G[3][0], [pT[2][1], pT[3][1]])]
        stg_sem = [P.new_sem(f"st{l}_{n}") for n in range(4)]
        out_sems.extend(stg_sem)
        cnt = {"n": 0}
        Tb = [scr.alloc(16 * 64 * 2, BF16, f"T{n}") for n in range(4)]
        Hks = []
        for n in range(2):
            hk, hk_r = scr.alloc(17 * 64 * 2, BF16, f"hank{n}")
            Hks.append((hk.rearrange("p (a b) -> p a b", a=17), hk_r, P.new_sem(f"hk{l}_{n}")))
        ckT, ckT_r = scr.alloc(8 * 256 * 2, BF16, "ckT")
        ckT = ckT.rearrange("p (c k) -> p c k", c=8)
        cvb, cvb_r = scr.alloc(2 * 1024 * 2, BF16, "cvb")
        cvb = cvb.rearrange("p (c f) -> p c f", c=2)
        sem_ck = P.new_sem(f"ck{l}")
        sem_cv = P.new_sem(f"cv{l}")
        P.dma("pool", lambda e: e.dma_start(
            out=ckT[:, :, :], in_=ckT_d[i].rearrange("(c p) k -> p c k", p=128)),
            sem_ck, writes=[ckT_r])
        P.dma("pool", lambda e: e.dma_start(
            out=cvb[:, :, :], in_=cv_d[i].rearrange("(c p) f -> p c f", p=128)),
            sem_cv, writes=[cvb_r])

        cn = {"z": 0, "p": 0}

        def hankel_load(h):
            src = bass.AP(rpb_h, i * 7696 + 128 + (h * 15 - 1) * 31 - 48, [[1, 64], [31, 17], [1, 64]])
            Hk, Hk_r, hk_sem = Hks[h % 2]
            P.dma("pool", lambda e: e.dma_start(out=Hk[0:64, :, :], in_=src), hk_sem, writes=[Hk_r])

        def build_T(h, slot, load=True):
            Tt, Tt_r = Tb[slot]
            Tt = Tt.rearrange("p (o q) -> p o q", o=16)
            Hk, Hk_r, hk_sem = Hks[h % 2]
            if load:
                hankel_load(h)
            for half in range(2):
                pi = next_ps(0, 4)

                def fn(e, half=half, pi=pi):
                    ins = None
                    for oo in range(8):
                        a0 = half * 8 + oo
                        ins = e.matmul(ps[pi][:, (7 - oo) * 64:(8 - oo) * 64],
                                       lhsT=Hk[0:64, a0:a0 + 2, :].rearrange("p a b -> p (a b)"),
                                       rhs=Jb[0:64, :], start=True, stop=True)
                    return ins
                P.op("pe", fn, reads=[Hk_r, const_r], writes=[ps_r[pi]])
                P.op("dve", lambda e, half=half, pi=pi: e.tensor_tensor(
                    out=Tt[:, (1 - half) * 8:(2 - half) * 8, :],
                    in0=ps[pi][:, :].rearrange("p (o q) -> p o q", o=8),
                    in1=V("cmask", 64).unsqueeze(1).to_broadcast([128, 8, 64]), op=ALU.add),
                    reads=[ps_r[pi], vecs_r], writes=[Tt_r])
            return Tt, Tt_r

        hankel_load(0)
        hankel_load(1)

        def qk_evac(j, fc, t, pi):
            fch = j * 4 + fc
            if fch < 8:
                P.op("act", lambda e: e.activation(out=qkT[:, fch, t * 512:(t + 1) * 512],
                                                   in_=ps[pi][:, :], func=AF.Copy, scale=0.125),
                     reads=[ps_r[pi]], writes=[qk_r[fch][t]])
            else:
                n = cnt["n"] % 4
                cnt["n"] += 1
                sg, sg_r = stg[n]
                P.op("dve", lambda e: e.tensor_copy(out=sg[:, :], in_=ps[pi][:, :]),
                     reads=[ps_r[pi]], writes=sg_r)
                P.op("act", lambda e: e.activation(out=qkT[:, fch, t * 512:(t + 1) * 512],
                                                   in_=sg[:, :], func=AF.Copy),
                     reads=sg_r, writes=[qk_r[fch][t]])
                kc = fch - 8
                P.dma("sp", lambda e: e.dma_start(
                    out=kT_d[i, kc * 128:(kc + 1) * 128, t * 512:(t + 1) * 512], in_=sg[:, :]),
                    stg_sem[n], reads=sg_r)
        pshi = 8
        linear_fm("qk", l, 4, 8, hT, hT_r, qk_evac, hook=adaln_piece, pshi=pshi)

        T_first = [build_T(0, 0, load=False), build_T(1, 1, load=False)]

        for j in range(2):
            s = use_piece("v", l, j)
            for b in range(12):
                pi = next_ps(0, pshi)
                t = b // 4

                def fn(e, s=s, b=b, pi=pi):
                    ins = None
                    for kc in range(8):
                        ins = e.matmul(ps[pi][:, :], lhsT=hT[:, kc, b * 128:(b + 1) * 128],
                                       rhs=ring[s][:, kc * 512:(kc + 1) * 512],
                                       start=(kc == 0), stop=(kc == 7))
                    return ins
                P.op("pe", fn, reads=[ring_r[s], hT_r[t]], writes=[ps_r[pi]])
                n = cnt["n"] % 4
                cnt["n"] += 1
                sg, sg_r = stg[n]
                P.op("dve", lambda e, sg=sg, pi=pi: e.tensor_copy(out=sg[:, :], in_=ps[pi][:, :]),
                     reads=[ps_r[pi]], writes=sg_r)
                P.op("act", lambda e, sg=sg, b=b, j=j: e.activation(
                    out=Vb[:, b, j * 512:(j + 1) * 512], in_=sg[:, :], func=AF.Copy),
                    reads=sg_r, writes=[V_r[b]])
                P.dma("sp", lambda e, sg=sg, b=b, j=j: e.dma_start(
                    out=vo_d[i, b * 128:(b + 1) * 128, j * 512:(j + 1) * 512], in_=sg[:, :]),
                    stg_sem[n], reads=sg_r)
            done_piece()
            adaln_piece()

        OA = [4, 5]
        SA = [6, 7]

        def finish_pair(o_i, s_i, hp, col0, ncol, tq):
            rc, rc_r = zb[cn["z"] % 2]
            cn["z"] += 1
            P.op("act", lambda e: e.activation(out=rc[:, 0:ncol], in_=ps[s_i][:, 0:ncol], func=AF.Ln),
                 reads=[ps_r[s_i]], writes=[rc_r])
            P.op("act", lambda e: e.activation(out=rc[:, 0:ncol], in_=rc[:, 0:ncol], func=AF.Exp,
                                               scale=-1.0),
                 reads=[rc_r], writes=[rc_r])
            P.op("dve", lambda e: e.tensor_tensor(
                out=hT[:, hp, col0:col0 + ncol], in0=ps[o_i][:, 0:ncol], in1=rc[:, 0:ncol],
                op=ALU.mult),
                reads=[ps_r[o_i], rc_r], writes=[hT_r[tq]])

        def pv_pair(o_i, s_i, args):
            def fn(e):
                ins = None
                for (pb, vsrc, vsrc_r, ptile, ptile_r, c0, n, first) in args:
                    ins = e.matmul(ps[o_i][pb:pb + 64, c0:c0 + n], lhsT=vsrc, rhs=ptile[:, 0:n],
                                   start=first, stop=True, skip_group_check=True)
                for (pb, vsrc, vsrc_r, ptile, ptile_r, c0, n, first) in args:
                    ins = e.matmul(ps[s_i][pb:pb + 64, c0:c0 + n], lhsT=ones_b[:, 0:64],
                                   rhs=ptile[:, 0:n], start=first, stop=True, skip_group_check=True)
                return ins
            reads = [const_r]
            for a in args:
                reads += [a[2], a[4]]
            P.op("pe", fn, reads=reads, writes=[ps_r[o_i], ps_r[s_i]])

        class Blk:
            pass

        def ctx_block(o_i, s_i, hp, qh, par, lc, first):
            b = Blk()
            pb = 64 * par
            h = 2 * hp + par

            def prep():
                b.pi = next_ps(0, 4)
                b.reads = [ckT_r, qk_r[hp][qh]]
            b.prep = prep
            b.mm_s = lambda e: e.matmul(
                ps[b.pi][:, :], lhsT=ckT[pb:pb + 64, hp, lc * 128:(lc + 1) * 128],
                rhs=qkT[pb:pb + 64, hp, qh * 512:(qh + 1) * 512], start=True, stop=True)
            b.mm_g = None

            def post():
                pi = b.pi
                b.pt, b.pt_r = pT[cn["p"] % 4]
                cn["p"] += 1
                pt = b.pt
                P.op("act", lambda e: e.activation(
                    out=pt[:, :], in_=ps[pi][:, :], func=AF.Exp, bias=V("ctxg")),
                    reads=[ps_r[pi], vecs_r], writes=[b.pt_r])
            b.post = post
            b.pvargs = lambda: (pb, cvb[:, lc, h * 64:(h + 1) * 64], cvb_r, b.pt, b.pt_r, 0, 512, first)
            return b

        def own_block(o_i, s_i, hp, qh, par, c, rows, Tt, Tt_r):
            b = Blk()
            pb = 64 * par
            h = 2 * hp + par
            r0, n = rows[0], len(rows) * 64
            c0 = (r0 - qh * 8) * 64
            tk = c // 4

            def prep():
                b.pi = next_ps(0, 4)
                b.reads = [qk_r[8 + hp][tk], qk_r[hp][qh], gt_r]
            b.prep = prep
            b.mm_s = lambda e: e.matmul(
                ps[b.pi][:, 0:n], lhsT=qkT[pb:pb + 64, 8 + hp, c * 128:(c + 1) * 128],
                rhs=qkT[pb:pb + 64, hp, r0 * 64:r0 * 64 + n], start=True, stop=False,
                skip_group_check=True)
            b.mm_g = lambda e: e.matmul(
                ps[b.pi][:, 0:n], lhsT=gt[pb:pb + 16, c * 128:(c + 1) * 128],
                rhs=gt[pb:pb + 16, 1024 + r0 * 64:1024 + r0 * 64 + n], start=False, stop=True,
                skip_group_check=True)

            def post():
                pi = b.pi
                z, z_r = zb[cn["z"] % 2]
                cn["z"] += 1
                k = len(rows)
                o0 = 7 - 2 * c + r0
                assert 0 <= o0 and o0 + k <= 16, (c, r0, k)
                P.op("dve", lambda e: e.tensor_tensor(
                    out=z[:, 0:n], in0=ps[pi][:, 0:n],
                    in1=Tt[:, o0:o0 + k, :].rearrange("p o q -> p (o q)"), op=ALU.add),
                    reads=[ps_r[pi], Tt_r], writes=[z_r])
                b.pt, b.pt_r = pT[cn["p"] % 4]
                cn["p"] += 1
                pt = b.pt
                P.op("act", lambda e: e.activation(out=pt[:, 0:n], in_=z[:, 0:n], func=AF.Exp),
                     reads=[z_r], writes=[b.pt_r])
            b.post = post
            b.pvargs = lambda: (pb, Vb[:, c, h * 64:(h + 1) * 64], V_r[c], b.pt, b.pt_r, c0, n, False)
            return b

        def slotb_block(o_i, s_i, hp, par, sq, kc):
            b = Blk()
            pb = 64 * par
            h = 2 * hp + par
            q0 = 1024 + sq * 256
            k0 = q0 + kc * 128
            vb_i = k0 // 128

            def prep():
                b.pi = next_ps(0, 4)
                b.reads = [qk_r[8 + hp][2], qk_r[hp][2]]
            b.prep = prep
            b.mm_s = lambda e: e.matmul(
                ps[b.pi][:, 0:256], lhsT=qkT[pb:pb + 64, 8 + hp, k0:k0 + 128],
                rhs=qkT[pb:pb + 64, hp, q0:q0 + 256], start=True, stop=True)
            b.mm_g = None

            def post():
                pi = b.pi
                b.pt, b.pt_r = pT[cn["p"] % 4]
                cn["p"] += 1
                pt = b.pt
                P.op("act", lambda e: e.activation(
                    out=pt[:, 0:256], in_=ps[pi][:, 0:256], func=AF.Exp),
                    reads=[ps_r[pi]], writes=[b.pt_r])
            b.post = post
            b.pvargs = lambda: (pb, Vb[:, vb_i, h * 64:(h + 1) * 64], V_r[vb_i], b.pt, b.pt_r,
                                sq * 256, 256, kc == 0)
            return b

        def emit_scores(blks):
            for b in blks:
                b.prep()

            def fn(e):
                ins = None
                for b in blks:
                    ins = b.mm_s(e)
                for b in blks:
                    if b.mm_g is not None:
                        ins = b.mm_g(e)
                return ins
            reads = []
            for b in blks:
                reads += b.reads
            P.op("pe", fn, reads=reads, writes=[ps_r[b.pi] for b in blks])
            for b in blks:
                b.post()

        sched = []
        npair = 0
        for hp in range(8):
            sched.append(("preload", hp))
            for qh in range(2):
                if qh == 1:
                    sched.append(("pre", hp))
                o_i, s_i = OA[npair % 2], SA[npair % 2]
                npair += 1
                for lc in range(2):
                    sched.append(("step", o_i, s_i, [("ctx", o_i, s_i, hp, qh, par, lc, lc == 0)
                                                     for par in range(2)]))
                for c in range(8):
                    rows = [r for r in chunk_rows(c) if qh * 8 <= r < qh * 8 + 8]
                    if rows:
                        sched.append(("step", o_i, s_i, [("own", o_i, s_i, hp, qh, par, c, rows)
                                                         for par in range(2)]))
                sched.append(("post", lambda o_i=o_i, s_i=s_i, hp=hp, qh=qh: finish_pair(
                    o_i, s_i, hp, qh * 512, 512, qh)))
            sched.append(("swap", None))
        for hp in range(8):
            o_i, s_i = OA[npair % 2], SA[npair % 2]
            npair += 1
            for sq in range(2):
                for kc in range(2):
                    sched.append(("step", o_i, s_i, [("sb", o_i, s_i, hp, par, sq, kc) for par in range(2)]))
            sched.append(("post", lambda o_i=o_i, s_i=s_i, hp=hp: finish_pair(o_i, s_i, hp, 1024, 512, 2)))

        LAG = 1
        pending = []
        Tstate = {"cur": T_first, "nxt": None}

        def flush_one():
            ent = pending.pop(0)
            if ent[0] == "step":
                _, o_i, s_i, blks = ent
                pv_pair(o_i, s_i, [b.pvargs() for b in blks])
            else:
                ent[1]()

        def nsteps():
            return sum(1 for e_ in pending if e_[0] == "step")

        for ent in sched:
            kind = ent[0]
            if kind == "preload":
                hpn = ent[1]
                if hpn + 1 < 8:
                    hankel_load(2 * hpn + 2)
                    hankel_load(2 * hpn + 3)
            elif kind == "pre":
                hpn = ent[1]
                if hpn + 1 < 8:
                    sl = ((hpn + 1) % 2) * 2
                    Tstate["nxt"] = [build_T(2 * hpn + 2, sl, load=False),
                                     build_T(2 * hpn + 3, sl + 1, load=False)]
            elif kind == "swap":
                if Tstate["nxt"] is not None:
                    Tstate["cur"] = Tstate["nxt"]
                    Tstate["nxt"] = None
            elif kind == "post":
                pending.append(("post", ent[1]))
            else:
                _, o_i, s_i, specs = ent
                blks = []
                for obj in specs:
                    tag = obj[0]
                    if tag == "ctx":
                        b = ctx_block(*obj[1:])
                    elif tag == "own":
                        _, oo, ss, hp, qh, par, c, rows = obj
                        Tt, Tt_r = Tstate["cur"][par]
                        b = own_block(oo, ss, hp, qh, par, c, rows, Tt, Tt_r)
                    else:
                        b = slotb_block(*obj[1:])
                    blks.append(b)
                emit_scores(blks)
                pending.append(("step", o_i, s_i, blks))
                while nsteps() > LAG:
                    flush_one()
                    while pending and pending[0][0] == "post":
                        flush_one()
        while pending:
            flush_one()

        nctx = norm_begin("mod", l, 1)
        linear_fm_t("o", l, 2, hT, hT_r, resid_evac(16), tile_done=norm_tile_done(nctx, lag=3))

    def conv_layer(l):
        i = l // 2
        big.reset()
        up, up_r0 = big.alloc(8 * 6 * SEGP * 2, BF16, "upad")
        up = up.rearrange("p (c s w) -> p c s w", c=8, s=6)
        up_r = [Res(f"up{c}", init=big.inherit) for c in range(8)]
        big.live.extend(up_r)
        vv, _ = big.alloc(8 * TOK * 4, F32, "v")
        vv = vv.rearrange("p (c t) -> p c t", c=8)
        v_r = [[Res(f"v{c}_{t}", init=big.inherit) for t in range(NT)] for c in range(8)]
        for c in range(8):
            big.live.extend(v_r[c])
        scr.reset()
        dg = [scr.alloc(31 * 128 * 2, BF16, f"dg{n}") for n in range(2)]
        sg = [scr.alloc(2048, F32, f"sig{n}") for n in range(2)]
        for c in range(8):
            P.op("pool", lambda e, c=c: e.memset(up[:, c, :, 0:15], 0.0), writes=[up_r[c]])
            P.op("pool", lambda e, c=c: e.memset(up[:, c, :, 271:286], 0.0), writes=[up_r[c]])
        cnt = {"n": 0}
        gbank = {}

        def pw1_evac(j, fc, t, pi):
            if fc < 2:
                gbank[(fc, t)] = pi
                return
            c = 2 * j + (fc - 2)
            pa = gbank[(fc - 2, t)]
            sgt, sgt_r = sg[cnt["n"] % 2]
            cnt["n"] += 1
            P.op("act", lambda e: e.activation(out=sgt[:, :], in_=ps[pi][:, :], func=AF.Sigmoid),
                 reads=[ps_r[pi]], writes=[sgt_r])
            P.op("dve", lambda e: e.tensor_tensor(
                out=up[:, c, 2 * t:2 * t + 2, 15:271],
                in0=ps[pa][:, :].rearrange("p (s w) -> p s w", s=2),
                in1=sgt[:, :].rearrange("p (s w) -> p s w", s=2), op=ALU.mult),
                reads=[ps_r[pa], sgt_r], writes=[up_r[c]])
        sqc = [scr.alloc(1024, BF16, f"sqc{n}") for n in range(2)]
        mean, mean_r = scr.alloc(2048, F32, "mean")
        rstd, rstd_r = scr.alloc(2048, F32, "rstd")
        tmp = sg
        nd = {"n": 0, "q": 0, "t": 0}
        dgB = [Res("dgB0", init=scr.inherit), Res("dgB1", init=scr.inherit)]
        scr.live.extend(dgB)
        diag_q = []
        pending_tail = []
        pending_stats = []
        NPOOL = 22

        def build_diag(c):
            dgt, dgt_r = dg[nd["n"] % 2]
            dgB_r = dgB[nd["n"] % 2]
            nd["n"] += 1
            dgt = dgt.rearrange("p (j m) -> p j m", j=31)
            P.op("pool", lambda e: e.tensor_tensor(
                out=dgt[:, 0:NPOOL, :],
                in0=ident_b[:, :].unsqueeze(1).to_broadcast([128, NPOOL, 128]),
                in1=V("wdw", NPOOL, i * 248 + c * 31).unsqueeze(2).to_broadcast([128, NPOOL, 128]),
                op=ALU.mult),
                reads=[const_r, vecs_r], writes=[dgt_r])
            P.op("dve", lambda e: e.tensor_tensor(
                out=dgt[:, NPOOL:31, :],
                in0=ident_b[:, :].unsqueeze(1).to_broadcast([128, 31 - NPOOL, 128]),
                in1=V("wdw", 31 - NPOOL, i * 248 + c * 31 + NPOOL).unsqueeze(2).to_broadcast(
                    [128, 31 - NPOOL, 128]),
                op=ALU.mult),
                reads=[const_r, vecs_r], writes=[dgB_r])
            diag_q.append((dgt, dgt_r, dgB_r))

        build_diag(0)
        for j in range(4):
            s = use_piece("pw1", l, j)
            for t in range(NT):
                banks = []
                for fc in range(4):
                    pi = next_ps()
                    banks.append(pi)

                    def fn(e, s=s, fc=fc, t=t, pi=pi):
                        ins = None
                        for kc in range(8):
                            ins = e.matmul(ps[pi][:, :],
                                           lhsT=ring[s][:, kc * 512 + fc * 128:kc * 512 + fc * 128 + 128],
                                           rhs=hT[:, kc, t * 512:(t + 1) * 512],
                                           start=(kc == 0), stop=(kc == 7))
                        return ins
                    P.op("pe", fn, reads=[ring_r[s], hT_r[t]], writes=[ps_r[pi]])
                    pw1_evac(j, fc, t, pi)
            done_piece()
        for c in range(8):
            P.op("dve", lambda e, c=c: e.tensor_scalar(
                out=up[:, c, 1:4, 0:15], in0=up[:, c, 0:3, 256:271], scalar1=V("flag"), scalar2=None,
                op0=ALU.mult), reads=[up_r[c], vecs_r], writes=[up_r[c]])
            P.op("dve", lambda e, c=c: e.tensor_scalar(
                out=up[:, c, 0:3, 271:286], in0=up[:, c, 1:4, 15:30], scalar1=V("flag"), scalar2=None,
                op0=ALU.mult), reads=[up_r[c], vecs_r], writes=[up_r[c]])
        for t in range(NT):
            tsl = slice(t * 512, (t + 1) * 512)
            pm, pq = (4, 5) if t % 2 == 0 else (6, 7)

            def stats(c, t=t, tsl=tsl, pm=pm, pq=pq, sq=None, sq_r=None):
                P.op("pe", lambda e: e.matmul(ps[pm][:, :], lhsT=ones_f[:, :], rhs=vv[:, c, tsl],
                                              start=(c == 0), stop=(c == 7)),
                     reads=[v_r[c][t], const_r], writes=[ps_r[pm]])
                P.op("pe", lambda e: e.matmul(ps[pq][:, :], lhsT=ones_b[:, :], rhs=sq[:, :],
                                              start=(c == 0), stop=(c == 7)),
                     reads=[sq_r, const_r], writes=[ps_r[pq]])
            prev = None
            for c in range(8):
                dgt, dgt_r, dgB_r = diag_q.pop(0)
                if not (t == NT - 1 and c == 7):
                    build_diag((c + 1) % 8)
                pi = next_ps(0, 4)

                def fn(e, c=c, t=t, pi=pi, dgt=dgt):
                    ins = None
                    for sgm in range(2):
                        for jt in range(31):
                            ins = e.matmul(ps[pi][:, sgm * 256:(sgm + 1) * 256], lhsT=dgt[:, jt, :],
                                           rhs=up[:, c, 2 * t + sgm, jt:jt + 256],
                                           start=(jt == 0), stop=(jt == 30))
                    return ins
                P.op("pe", fn, reads=[dgt_r, dgB_r, up_r[c]], writes=[ps_r[pi]])
                P.op("act", lambda e, c=c, tsl=tsl, pi=pi: e.activation(
                    out=vv[:, c, tsl], in_=ps[pi][:, :], func=AF.Identity,
                    bias=V("bdw", 1, i * 8 + c)),
                    reads=[ps_r[pi], vecs_r], writes=[v_r[c][t]])
                sq, sq_r = sqc[nd["q"] % 2]
                nd["q"] += 1
                P.op("act", lambda e, c=c, tsl=tsl, sq=sq: e.activation(
                    out=sq[:, :], in_=vv[:, c, tsl], func=AF.Square),
                    reads=[v_r[c][t]], writes=[sq_r])
                if prev is not None:
                    stats(*prev[:1], sq=prev[1], sq_r=prev[2])
                prev = (c, sq, sq_r)
                if c == 0 and pending_stats:
                    pending_stats.pop(0)()
                if 1 <= c <= 5 and pending_tail:
                    pending_tail.pop(0)()
            def tail_stats(prev=prev, stats=stats):
                stats(*prev[:1], sq=prev[1], sq_r=prev[2])

            def tail(part, t=t, tsl=tsl, pm=pm, pq=pq):
                if part > 0:
                    tail_chunks(part, t, tsl, pm, pq)
                    return
                ta, ta_r = tmp[0]
                P.op("act", lambda e, pm=pm, ta=ta: e.activation(out=ta[:, :], in_=ps[pm][:, :], func=AF.Square,
                                                                 scale=1.0 / D),
                     reads=[ps_r[pm]], writes=[ta_r])
                P.op("act", lambda e, pm=pm: e.activation(out=ps[pm][:, :], in_=ps[pm][:, :], func=AF.Copy,
                                                          scale=1.0 / D),
                     reads=[ps_r[pm]], writes=[ps_r[pm]])
                P.op("dve", lambda e, ta=ta, pq=pq: e.scalar_tensor_tensor(
                    out=rstd[:, :], in0=ps[pq][:, :], scalar=1.0 / D, in1=ta[:, :],
                    op0=ALU.mult, op1=ALU.subtract),
                    reads=[ps_r[pq], ta_r], writes=[rstd_r])
                P.op("act", lambda e: e.activation(out=rstd[:, :], in_=rstd[:, :], func=AF.Ln,
                                                   bias=V("epsl")),
                     reads=[rstd_r, vecs_r], writes=[rstd_r])
                P.op("act", lambda e, pq=pq: e.activation(out=ps[pq][:, :], in_=rstd[:, :], func=AF.Exp,
                                                          scale=-0.5),
                     reads=[rstd_r], writes=[ps_r[pq]])

            def tail_chunks(part, t, tsl, pm, pq):
                for c in range(2 * (part - 1), 2 * part):
                    ta, ta_r = tmp[nd["t"] % 2]
                    nd["t"] += 1
                    P.op("dve", lambda e, c=c, ta=ta, tsl=tsl, pm=pm: e.tensor_tensor(
                        out=ta[:, :], in0=vv[:, c, tsl], in1=ps[pm][:, :], op=ALU.subtract),
                        reads=[v_r[c][t], ps_r[pm]], writes=[ta_r])
                    P.op("dve", lambda e, ta=ta, pq=pq: e.tensor_tensor(
                        out=ta[:, :], in0=ta[:, :], in1=ps[pq][:, :], op=ALU.mult),
                        reads=[ta_r, ps_r[pq]], writes=[ta_r])
                    P.op("act", lambda e, c=c, ta=ta, tsl=tsl: e.activation(
                        out=hT[:, c, tsl], in_=ta[:, :], func=AF.Silu,
                        scale=V("lng", 1, i * 8 + c), bias=V("lnb", 1, i * 8 + c)),
                        reads=[ta_r, vecs_r], writes=[hT_r[t]])
            pending_stats.append(tail_stats)
            for part in range(5):
                pending_tail.append(lambda part=part, tail=tail: tail(part))
        pending_stats.pop(0)()
        while pending_tail:
            pending_tail.pop(0)()
        nctx = norm_begin("mod", l, 1)
        st["psi"] = 6
        linear_fm_t("pw2", l, 2, hT, hT_r, resid_evac(16), tile_done=norm_tile_done(nctx, lag=3))

    del PLAN[:]
    ctx0 = norm_begin("mod", 0, 0)
    rp0 = []
    for t in range(NT):
        norm_A(ctx0, t)
        rp0.append(norm_B1(ctx0, t))
    adaln_begin(0)
    for _ in range(4):
        adaln_piece()
    for t in range(NT):
        norm_B2(ctx0, t, *rp0[t])
    for l in range(DEPTH):
        set_layer(l)
        flush_deferred()
        if l % 2 == 0:
            attn_layer(l)
        else:
            conv_layer(l)
        mlp(l)
    final_norm()
    assert st["next_use"] == NPIECE, (st, NPIECE)
    sp = P.eng["sp"]
    for s in out_sems:
        if s.count:
            sp.need(s, s.count)
    P.emit()
    return nc


_CACHE = {}


def kernel(**inp):
    inp = {k: np.asarray(v) for k, v in inp.items()}
    if "nc" not in _CACHE:
        _CACHE["nc"] = build_program()
    nc = _CACHE["nc"]
    wall = build_wall(inp)
    xp = inp["x_prompt"].astype(np.float32, copy=False)
    xs = inp["x_sample"].astype(np.float32, copy=False)
    roles = []
    for core in range(8):
        if core < 4:
            roles.append((core, [2 * core, 2 * core + 1], None))
        else:
            base = 8 + (core - 4) * 6
            roles.append((None, [base + 4, base + 5], [base, base + 1, base + 2, base + 3]))
    in_maps = []
    zeros_ck = np.zeros((2, D, 256), np.float32)
    zeros_cv = np.zeros((2, 256, D), np.float32)
    zeros_rp = np.zeros((2, 7696), np.float32)
    for core in range(8):
        sb, pB, pA = roles[core]
        if sb is not None:
            xa = xs[sb]
        else:
            xa = np.concatenate([xp[p] for p in pA], axis=0)
        xb = np.concatenate([xp[p] for p in pB], axis=0)
        x = np.concatenate([xa, xb], axis=0)
        m = {"xT": np.ascontiguousarray(x.T), "wall": wall, "vecs": build_vecs(inp, sb),
             "gtab": build_gtab(sb is not None)}
        if sb is not None:
            ck = inp["cache_k"][sb].reshape(2, 256, D)
            m["ckT"] = np.ascontiguousarray(ck.transpose(0, 2, 1)).astype(np.float32)
            m["cv"] = np.ascontiguousarray(inp["cache_v"][sb].reshape(2, 256, D)).astype(np.float32)
            rp = np.zeros((2, 7696), np.float32)
            rp[:, 128:128 + 7440] = inp["rpb"].reshape(2, 7440)
            m["rpbp"] = rp
        else:
            m["ckT"], m["cv"], m["rpbp"] = zeros_ck, zeros_cv, zeros_rp
        in_maps.append(m)
    res = run_bass_kernel_spmd(nc, in_maps, core_ids=list(range(8)))
    outs = res.results
    y_prompt = np.empty((32, 256, D), np.float32)
    y_sample = np.empty((4, 1024, D), np.float32)
    nk = np.empty((32, 2, 256, 16, 64), np.float32)
    nv = np.empty((32, 2, 256, 16, 64), np.float32)
    for core in range(8):
        sb, pB, pA = roles[core]
        y = outs[core]["yT"].T
        kt = outs[core]["kTo"].transpose(0, 2, 1)
        vo = outs[core]["vo"]
        if sb is not None:
            y_sample[sb] = y[0:1024]
        else:
            for n, p in enumerate(pA):
                y_prompt[p] = y[n * 256:(n + 1) * 256]
                nk[p] = kt[:, n * 256:(n + 1) * 256].reshape(2, 256, 16, 64)
                nv[p] = vo[:, n * 256:(n + 1) * 256].reshape(2, 256, 16, 64)
        for n, p in enumerate(pB):
            y_prompt[p] = y[1024 + n * 256:1024 + (n + 1) * 256]
            nk[p] = kt[:, 1024 + n * 256:1024 + (n + 1) * 256].reshape(2, 256, 16, 64)
            nv[p] = vo[:, 1024 + n * 256:1024 + (n + 1) * 256].reshape(2, 256, 16, 64)
    return (y_prompt, y_sample, nk, nv)
```

```python
import numpy as np
import concourse.bass as bass
import concourse.mybir as mybir
from concourse.bass_utils import run_bass_kernel_spmd

F32 = mybir.dt.float32
F32R = mybir.dt.float32r
BF16 = mybir.dt.bfloat16
U8 = mybir.dt.uint8
AF = mybir.ActivationFunctionType
ALU = mybir.AluOpType

D = 1024
TOK = 1536
NT = 3
DEPTH = 4
NEG = -30000.0
SEGP = 286


def piece_plan():
    plan = []
    for l in range(DEPTH):
        for j in range(12):
            plan.append(("ada", l, j))
        if l % 2 == 0:
            for j in range(4):
                plan.append(("qk", l, j))
            for j in range(2):
                plan.append(("v", l, j))
            for j in range(2):
                plan.append(("o", l, j))
        else:
            for j in range(4):
                plan.append(("pw1", l, j))
            for j in range(2):
                plan.append(("pw2", l, j))
        for hf in range(2):
            for j in range(4):
                plan.append(("up", l, hf * 4 + j))
            for j in range(4):
                plan.append(("down", l, hf * 4 + j))
    return plan


NPIECE = len(piece_plan())
PLAN = []


def tile_kf(w, f0, nf):
    return np.ascontiguousarray(
        w[:, f0:f0 + nf].reshape(8, 128, nf).transpose(1, 0, 2)).reshape(128, 8 * nf)


def build_wall(inp):
    wall = np.empty((NPIECE, 128, 4096), np.float32)
    for n, (kind, l, j) in enumerate(PLAN):
        i = l // 2
        if kind == "ada":
            wall[n] = tile_kf(inp["w_ada"][l], j * 512, 512)
        elif kind == "qk":
            wall[n] = tile_kf(inp["w_qkv"][i], j * 512, 512)
        elif kind == "v":
            wall[n] = tile_kf(inp["w_qkv"][i], 2048 + j * 512, 512)
        elif kind == "o":
            wall[n] = tile_kf(inp["w_o"][i], j * 512, 512)
        elif kind == "pw1":
            w = inp["w_pw1"][i]
            cols = np.concatenate([np.arange(256 * j, 256 * j + 256),
                                   1024 + np.arange(256 * j, 256 * j + 256)])
            wall[n] = tile_kf(w[:, cols], 0, 512)
        elif kind == "pw2":
            wall[n] = tile_kf(inp["w_pw2"][i], j * 512, 512)
        elif kind == "up":
            wall[n] = tile_kf(inp["w_up"][l], j * 512, 512)
        elif kind == "down":
            hf, jj = divmod(j, 4)
            w = inp["w_down"][l][hf * 2048:(hf + 1) * 2048, jj * 256:(jj + 1) * 256]
            wall[n] = np.ascontiguousarray(
                w.reshape(16, 128, 256).transpose(1, 0, 2)).reshape(128, 4096)
    return wall


VOFF = {}
_nv = 0
for _name, _n in [("cond", 16), ("bada", 192), ("ng", 64), ("fg", 8), ("bdw", 16), ("lng", 16),
                  ("lnb", 16), ("wdw", 496), ("gate", 128), ("ctxg", 1), ("flag", 1),
                  ("cmask", 64), ("epsr", 1), ("epsl", 1), ("J", 64), ("ident", 128)]:
    VOFF[_name] = _nv
    _nv += _n
NV = _nv


def fm(v):
    return np.ascontiguousarray(np.asarray(v, np.float32).reshape(8, 128).T)


def row_start(r):
    return int(np.clip(r - 4, 0, 8))


def chunk_rows(c):
    out = []
    for rq in range(16):
        rs = row_start(rq)
        if rs <= 2 * c + 1 and rs + 7 >= 2 * c:
            out.append(rq)
    return out


def build_gtab(is_s):
    g = np.zeros((16, 2048), np.float32)
    for rq in range(16):
        for c in range(8):
            for rl in range(2):
                rk = 2 * c + rl
                if is_s:
                    rs = row_start(rq)
                    ok = rs <= rk < rs + 8
                else:
                    ok = (rk // 4) == (rq // 4)
                g[rq, c * 128 + rl * 64:c * 128 + (rl + 1) * 64] = 0.0 if ok else NEG
        g[rq, 1024 + rq * 64:1024 + (rq + 1) * 64] = 1.0
    return g


def build_vecs(inp, sample_b):
    v = np.zeros((128, NV), np.float32)
    is_s = sample_b is not None
    condA = inp["c"][sample_b] if is_s else inp["c_ctx"]
    cd = np.stack([fm(condA), fm(inp["c_ctx"])], axis=-1)
    v[:, VOFF["cond"]:VOFF["cond"] + 16] = cd.reshape(128, 16)
    for l in range(DEPTH):
        b = np.asarray(inp["b_ada"][l], np.float32).reshape(48, 128).T
        v[:, VOFF["bada"] + l * 48:VOFF["bada"] + (l + 1) * 48] = b
        for s in range(2):
            o = VOFF["ng"] + (l * 2 + s) * 8
            v[:, o:o + 8] = fm(inp["norm_g"][l, s])
    v[:, VOFF["fg"]:VOFF["fg"] + 8] = fm(inp["final_g"])
    for i in range(2):
        v[:, VOFF["bdw"] + i * 8:VOFF["bdw"] + (i + 1) * 8] = fm(inp["b_dw"][i])
        v[:, VOFF["lng"] + i * 8:VOFF["lng"] + (i + 1) * 8] = fm(inp["conv_ln_g"][i])
        v[:, VOFF["lnb"] + i * 8:VOFF["lnb"] + (i + 1) * 8] = fm(inp["conv_ln_b"][i])
        w = np.asarray(inp["w_dw"][i], np.float32)
        wf = w.reshape(31, 8, 128).transpose(2, 1, 0)
        v[:, VOFF["wdw"] + i * 248:VOFF["wdw"] + (i + 1) * 248] = wf.reshape(128, 248)
    g = np.zeros((128, 8, 16), np.float32)
    for c in range(8):
        for rq in range(16):
            for rl in range(2):
                rk = 2 * c + rl
                if is_s:
                    rs = row_start(rq)
                    ok = rs <= rk < rs + 8
                else:
                    ok = (rk // 4) == (rq // 4)
                g[rl * 64:(rl + 1) * 64, c, rq] = 0.0 if ok else NEG
    v[:, VOFF["gate"]:VOFF["gate"] + 128] = g.reshape(128, 128)
    v[:, VOFF["ctxg"]] = 0.0 if is_s else NEG
    v[:, VOFF["flag"]] = 1.0 if is_s else 0.0
    cm = np.zeros((64, 64), np.float32)
    if is_s:
        for cq in range(64):
            ws = int(np.clip(cq - 8, 0, 48))
            for ck in range(64):
                if not (ws <= ck < ws + 16):
                    cm[ck, cq] = NEG
    v[0:64, VOFF["cmask"]:VOFF["cmask"] + 64] = cm
    v[64:128, VOFF["cmask"]:VOFF["cmask"] + 64] = cm
    v[:, VOFF["epsr"]] = 1e-6
    v[:, VOFF["epsl"]] = 1e-5
    v[0:64, VOFF["J"]:VOFF["J"] + 64] = np.eye(64, dtype=np.float32)[::-1]
    v[:, VOFF["ident"]:VOFF["ident"] + 128] = np.eye(128, dtype=np.float32)
    return v


class Sem:
    def __init__(self, h, name):
        self.h = h
        self.name = name
        self.count = 0


class Res:
    __slots__ = ("name", "w", "r")

    def __init__(self, name, init=None):
        self.name = name
        self.w = None
        self.r = dict(init) if init else {}

    def events(self):
        ev = dict(self.r)
        if self.w is not None:
            s, v = self.w
            if ev.get(s, 0) < v:
                ev[s] = v
        return ev


class Eng:
    def __init__(self, name, sem):
        self.name = name
        self.sem = sem
        self.items = []
        self.waited = {}

    def need(self, sem, val):
        if self.waited.get(sem, 0) >= val:
            return
        self.waited[sem] = val
        self.items.append(("wait", sem, val))


class Prog:
    def __init__(self, nc):
        self.nc = nc
        self.sems = []
        self.eng = {}

    def new_sem(self, name):
        s = Sem(None, name)
        self.sems.append(s)
        return s

    def add_engine(self, name):
        self.eng[name] = Eng(name, self.new_sem("e_" + name))

    def _deps(self, eng, reads, writes, extra):
        for r in reads:
            if r.w is not None:
                s, v = r.w
                if s is eng.sem and eng.name == "pe":
                    continue
                eng.need(s, v)
        for w in writes:
            if w.w is not None:
                s, v = w.w
                if not (s is eng.sem and eng.name == "pe"):
                    eng.need(s, v)
            for s, v in w.r.items():
                if s is eng.sem and eng.name == "pe":
                    continue
                eng.need(s, v)
        for s, v in extra:
            if s is eng.sem:
                continue
            eng.need(s, v)

    def op(self, ename, fn, reads=(), writes=(), extra=()):
        eng = self.eng[ename]
        self._deps(eng, reads, writes, extra)
        eng.sem.count += 1
        ev = (eng.sem, eng.sem.count)
        eng.items.append(("op", fn, eng.sem, 1))
        for r in reads:
            if r.r.get(ev[0], 0) < ev[1]:
                r.r[ev[0]] = ev[1]
        for w in writes:
            w.w = ev
            w.r = {}
        return ev

    def dma(self, qname, fn, sem, reads=(), writes=(), extra=()):
        eng = self.eng[qname]
        self._deps(eng, reads, writes, extra)
        sem.count += 16
        ev = (sem, sem.count)
        eng.items.append(("op", fn, sem, 16))
        for r in reads:
            if r.r.get(sem, 0) < ev[1]:
                r.r[sem] = ev[1]
        for w in writes:
            w.w = ev
            w.r = {}
        return ev

    def emit(self):
        nc = self.nc
        from contextlib import ExitStack
        with ExitStack() as st:
            for s in self.sems:
                s.h = st.enter_context(nc.semaphore(s.name))
            block = st.enter_context(nc.Block())

            def runner(eng):
                def body(e):
                    for it in eng.items:
                        if it[0] == "wait":
                            e.wait_ge(it[1].h, it[2])
                        else:
                            ins = it[1](e)
                            ins.then_inc(it[2].h, it[3])
                return body

            block.tensor(runner(self.eng["pe"]))
            block.scalar(runner(self.eng["act"]))
            block.vector(runner(self.eng["dve"]))
            block.gpsimd(runner(self.eng["pool"]))
            block.sync(runner(self.eng["sp"]))


class Region:
    def __init__(self, nc, name, nbytes):
        self.t = nc.alloc_sbuf_tensor(name, [128, nbytes], U8)
        self.nbytes = nbytes
        self.name = name
        self.live = []
        self.inherit = {}
        self.off = 0
        self.n = 0

    def reset(self):
        for r in self.live:
            for s, v in r.events().items():
                if self.inherit.get(s, 0) < v:
                    self.inherit[s] = v
        self.live = []
        self.off = 0

    def view(self, off, nbytes, dtype):
        return self.t[:, off:off + nbytes].bitcast(dtype)

    def alloc(self, nbytes, dtype, name=None):
        nbytes = (nbytes + 31) // 32 * 32
        assert self.off + nbytes <= self.nbytes, (self.name, name, self.off, nbytes, self.nbytes)
        ap = self.t[:, self.off:self.off + nbytes].bitcast(dtype)
        self.last_off = self.off
        self.off += nbytes
        self.n += 1
        r = Res(f"{self.name}_{name}_{self.n}", init=self.inherit)
        self.live.append(r)
        return ap, r


def build_program():
    nc = bass.Bass("TRN2", target_bir_lowering=False)
    xT_d = nc.dram_tensor("xT", [D, TOK], F32, kind="ExternalInput").ap()
    wall_d = nc.dram_tensor("wall", [NPIECE, 128, 4096], F32, kind="ExternalInput").ap()
    vec_d = nc.dram_tensor("vecs", [128, NV], F32, kind="ExternalInput").ap()
    rpb_h = nc.dram_tensor("rpbp", [2, 7696], F32, kind="ExternalInput")
    ckT_d = nc.dram_tensor("ckT", [2, D, 256], F32, kind="ExternalInput").ap()
    cv_d = nc.dram_tensor("cv", [2, 256, D], F32, kind="ExternalInput").ap()
    gt_d = nc.dram_tensor("gtab", [16, 2048], F32, kind="ExternalInput").ap()
    yT_d = nc.dram_tensor("yT", [D, TOK], F32, kind="ExternalOutput").ap()
    kT_d = nc.dram_tensor("kTo", [2, D, TOK], F32, kind="ExternalOutput").ap()
    vo_d = nc.dram_tensor("vo", [2, TOK, D], F32, kind="ExternalOutput").ap()

    P = Prog(nc)
    for e in ("pe", "act", "dve", "pool", "sp"):
        P.add_engine(e)

    xT = nc.alloc_sbuf_tensor("xTs", [128, 8, TOK], F32)
    xT_r = [[Res(f"x{c}_{t}") for t in range(NT)] for c in range(8)]
    hT = nc.alloc_sbuf_tensor("hTs", [128, 8, TOK], BF16)
    hT_r = [Res(f"h{t}") for t in range(NT)]
    vecs = nc.alloc_sbuf_tensor("vecs_s", [128, NV], F32)
    vecs_r = Res("vecs")
    mods = [nc.alloc_sbuf_tensor(f"mod{n}", [128, 48, 2], F32) for n in range(2)]
    mods_r = [[Res(f"mod{n}_{j}") for j in range(12)] for n in range(2)]
    coefs = [nc.alloc_sbuf_tensor(f"coef{n}", [128, 2, 8, 2], F32) for n in range(2)]
    coefs_r = [[Res(f"coef{n}_{k}") for k in range(2)] for n in range(2)]
    cur = {"mod": mods[0], "mod_r": mods_r[0], "coef": coefs[0], "coef_r": coefs_r[0]}
    scond = nc.alloc_sbuf_tensor("scond", [128, 8, 2], BF16)
    scond_r = Res("scond")
    ident_b = nc.alloc_sbuf_tensor("identb", [128, 128], BF16)
    ones_b = nc.alloc_sbuf_tensor("onesb", [128, 128], BF16)
    ones_f = nc.alloc_sbuf_tensor("onesf", [128, 128], F32)
    Jb = nc.alloc_sbuf_tensor("Jb", [128, 64], BF16)
    const_r = Res("const")
    NSLOT = 3
    ring = [nc.alloc_sbuf_tensor(f"ring{i}", [128, 4096], BF16) for i in range(NSLOT)]
    ring_r = [Res(f"ring{i}") for i in range(NSLOT)]
    ring_sem = [P.new_sem(f"ringsem{i}") for i in range(NSLOT)]
    big = Region(nc, "big", 77824)
    scr = Region(nc, "scr", 29184)
    ps = [nc.alloc_psum_tensor(f"ps{i}", [128, 512], F32) for i in range(8)]
    ps_r = [Res(f"ps{i}") for i in range(8)]

    def V(name, n=1, j=0):
        o = VOFF[name] + j
        return vecs[:, o:o + n]

    st = {"next_load": 0, "next_use": 0, "psi": 0}

    def load_piece():
        n = st["next_load"]
        if n >= NPIECE:
            return
        st["next_load"] += 1
        s = n % NSLOT
        P.dma("pool", lambda e, n=n, s=s: e.dma_start(out=ring[s][:, :], in_=wall_d[n],
                                                      max_dma_last_dim=8192),
              ring_sem[s], writes=[ring_r[s]])

    def use_piece(kind, l, j):
        n = st["next_use"]
        PLAN.append((kind, l, j))
        st["next_use"] += 1
        return n % NSLOT

    def done_piece():
        load_piece()

    for _ in range(NSLOT):
        load_piece()

    def next_ps(lo=0, hi=8):
        i = st["psi"]
        if i < lo or i >= hi:
            i = lo
        st["psi"] = i + 1
        return i

    sem_v = P.new_sem("ld_vecs")
    P.dma("sp", lambda e: e.dma_start(out=vecs[:, :], in_=vec_d), sem_v, writes=[vecs_r])
    sem_x = [P.new_sem(f"ld_x{t}") for t in range(NT)]
    for t in range(NT):
        P.dma("sp", lambda e, t=t: e.dma_start(
            out=xT[:, :, t * 512:(t + 1) * 512],
            in_=xT_d.rearrange("(c p) t -> p c t", p=128)[:, :, t * 512:(t + 1) * 512]),
            sem_x[t], writes=[xT_r[c][t] for c in range(8)])

    P.op("dve", lambda e: e.tensor_copy(out=ident_b[:, :], in_=V("ident", 128)),
         reads=[vecs_r], writes=[const_r])
    P.op("dve", lambda e: e.tensor_copy(out=Jb[:, :], in_=V("J", 64)), reads=[vecs_r], writes=[const_r])
    P.op("pool", lambda e: e.memset(ones_b[:, :], 1.0), writes=[const_r])
    P.op("pool", lambda e: e.memset(ones_f[:, :], 1.0), writes=[const_r])
    P.op("act", lambda e: e.activation(out=scond[:, :, :].rearrange("p c s -> p (c s)"),
                                       in_=V("cond", 16), func=AF.Silu),
         reads=[vecs_r], writes=[scond_r])

    out_sems = []

    ada = {}

    ada_q = []

    def adaln_begin(l):
        ada_q.extend((l, j) for j in range(12))

    def adaln_piece():
        if not ada_q:
            return
        l, j = ada_q.pop(0)
        pi = next_ps(0, 4) if st.get("in_attn") else next_ps()
        s = use_piece("ada", l, j)

        def fn(e):
            ins = None
            for fc in range(4):
                col = fc * 2
                for kc in range(8):
                    ins = e.matmul(ps[pi][:, col:col + 2],
                                   lhsT=ring[s][:, kc * 512 + fc * 128:kc * 512 + fc * 128 + 128],
                                   rhs=scond[:, kc, :], start=(kc == 0), stop=(kc == 7))
            return ins
        P.op("pe", fn, reads=[ring_r[s], scond_r], writes=[ps_r[pi]])
        done_piece()
        mod, mod_r = mods[l % 2], mods_r[l % 2]
        coef, coef_r = coefs[l % 2], coefs_r[l % 2]
        P.op("dve", lambda e: e.tensor_tensor(
            out=mod[:, j * 4:(j + 1) * 4, :],
            in0=ps[pi][:, 0:8].rearrange("p (j s) -> p j s", s=2),
            in1=V("bada", 4, l * 48 + j * 4).unsqueeze(2).to_broadcast([128, 4, 2]), op=ALU.add),
            reads=[ps_r[pi], vecs_r], writes=[mod_r[j]])
        for k, jj in ((0, 3), (1, 9)):
            if j == jj:
                P.op("dve", lambda e, k=k: e.scalar_tensor_tensor(
                    out=coef[:, k, :, :], in0=mod[:, 8 + 24 * k:16 + 24 * k, :], scalar=1.0,
                    in1=V("ng", 8, (l * 2 + k) * 8).unsqueeze(2).to_broadcast([128, 8, 2]),
                    op0=ALU.add, op1=ALU.mult),
                    reads=[mod_r[jj - 1], mod_r[jj], vecs_r], writes=[coef_r[k]])

    def set_layer(l):
        cur["mod"], cur["mod_r"] = mods[l % 2], mods_r[l % 2]
        cur["coef"], cur["coef_r"] = coefs[l % 2], coefs_r[l % 2]

    def slot_of(t):
        return 0 if t < 2 else 1

    pend = []

    def tick():
        for it in pend:
            it[0] -= 1
        while pend and pend[0][0] <= 0:
            pend.pop(0)[1]()

    def defer(k, fn):
        pend.append([k, fn])

    def flush_deferred():
        while pend:
            pend.pop(0)[1]()

    def norm_begin(kind, l=None, k=None):
        flush_deferred()
        scr.reset()
        ctx = {"kind": kind, "l": l, "k": k, "n": 0}
        ctx["sqb"], ctx["sqb_r"] = [], []
        for n in range(2):
            sqb, r = scr.alloc(8 * 512 * 2, BF16, f"sq{n}")
            ctx["sqb"].append(sqb.rearrange("p (c t) -> p c t", c=8))
            ctx["sqb_r"].append(r)
        ctx["pendB"] = None
        ctx["rstd"], ctx["rstd_r"] = scr.alloc(2048, F32, "rstd")
        ctx["tmp"] = [scr.alloc(2048, F32, f"tmp{i}") for i in range(2)]
        if kind == "final":
            ctx["stg"] = [scr.alloc(2048, F32, f"stg{i}") for i in range(2)]
            ctx["stg_sem"] = [P.new_sem(f"fin_st{i}") for i in range(2)]
            out_sems.extend(ctx["stg_sem"])
        else:
            ctx["coef"], ctx["coef_r"] = coefs[l % 2], coefs_r[l % 2]
            ctx["mod"], ctx["mod_r"] = mods[l % 2], mods_r[l % 2]
        return ctx

    def norm_A(ctx, t):
        sqb = ctx["sqb"][t % 2]
        P.op("act", lambda e: e.activation(out=sqb[:, :, :], in_=xT[:, :, t * 512:(t + 1) * 512],
                                           func=AF.Square),
             reads=[xT_r[c][t] for c in range(8)], writes=[ctx["sqb_r"][t % 2]])

    def norm_B(ctx, t):
        rps, rps_r = norm_B1(ctx, t)
        norm_B2(ctx, t, rps, rps_r)

    def norm_B1(ctx, t):
        sqb, rstd, rstd_r = ctx["sqb"][t % 2], ctx["rstd"], ctx["rstd_r"]
        sqb_r = ctx["sqb_r"][t % 2]
        pi = next_ps()

        def fn(e):
            ins = None
            for kc in range(8):
                ins = e.matmul(ps[pi][:, :], lhsT=ones_b[:, :], rhs=sqb[:, kc, :],
                               start=(kc == 0), stop=(kc == 7))
            return ins
        P.op("pe", fn, reads=[sqb_r, const_r], writes=[ps_r[pi]])
        P.op("act", lambda e: e.activation(out=rstd[:, :], in_=ps[pi][:, :], func=AF.Ln,
                                           bias=V("epsr"), scale=1.0 / D),
             reads=[ps_r[pi], vecs_r], writes=[rstd_r])
        P.op("act", lambda e: e.activation(out=ps[pi][:, :], in_=rstd[:, :], func=AF.Exp, scale=-0.5),
             reads=[rstd_r], writes=[ps_r[pi]])
        return ps[pi], ps_r[pi]

    def norm_B2(ctx, t, rps, rps_r):
        tsl = slice(t * 512, (t + 1) * 512)
        for c in range(8):
            ta, ta_r = ctx["tmp"][ctx["n"] % 2]
            P.op("dve", lambda e, c=c, ta=ta: e.tensor_tensor(
                out=ta[:, :], in0=xT[:, c, tsl], in1=rps[:, :], op=ALU.mult),
                reads=[xT_r[c][t], rps_r], writes=[ta_r])
            if ctx["kind"] == "final":
                sg, sg_r = ctx["stg"][ctx["n"] % 2]
                ssem = ctx["stg_sem"][ctx["n"] % 2]
                P.op("act", lambda e, c=c, ta=ta, sg=sg: e.activation(
                    out=sg[:, :], in_=ta[:, :], func=AF.Identity, scale=V("fg", 1, c)),
                    reads=[ta_r, vecs_r], writes=[sg_r])
                P.dma("sp", lambda e, c=c, sg=sg: e.dma_start(
                    out=yT_d[c * 128:(c + 1) * 128, tsl], in_=sg[:, :]), ssem, reads=[sg_r])
            else:
                k, s_ = ctx["k"], slot_of(t)
                sh0 = 0 if k == 0 else 24
                cf, md = ctx["coef"], ctx["mod"]
                P.op("act", lambda e, c=c, ta=ta: e.activation(
                    out=hT[:, c, tsl], in_=ta[:, :], func=AF.Identity,
                    scale=cf[:, k, c, s_:s_ + 1], bias=md[:, sh0 + c, s_:s_ + 1]),
                    reads=[ta_r, ctx["coef_r"][k], ctx["mod_r"][(sh0 + c) // 4]], writes=[hT_r[t]])
            ctx["n"] += 1

    def norm_tile_done(ctx, lag=3):
        def cb(t):
            norm_A(ctx, t)
            if ctx["pendB"] is not None:
                tp = ctx["pendB"]
                norm_B(ctx, tp)
            ctx["pendB"] = t
            if t == NT - 1:
                defer(lag, lambda: norm_B(ctx, t))
        return cb

    def norm_mod(l, k):
        ctx = norm_begin("mod", l, k)
        for t in range(NT):
            norm_A(ctx, t)
            norm_B(ctx, t)

    def linear_fm(kind, l, npieces, nk, src, src_r, evac, fc_per_piece=4, j0=0, hook=None, pshi=8,
                  last_tile_done=None):
        for j in range(npieces):
            s = use_piece(kind, l, j0 + j)
            for t in range(NT):
                for fc in range(fc_per_piece):
                    pi = next_ps(0, pshi)

                    def fn(e, s=s, fc=fc, t=t, pi=pi):
                        ins = None
                        w = ring[s][:, :].rearrange("p (k f) -> p k f", k=nk)
                        fw = 4096 // nk // fc_per_piece
                        for kc in range(nk):
                            ins = e.matmul(ps[pi][:, :], lhsT=w[:, kc, fc * fw:(fc + 1) * fw],
                                           rhs=src[:, kc, t * 512:(t + 1) * 512],
                                           start=(kc == 0), stop=(kc == nk - 1))
                        return ins
                    P.op("pe", fn, reads=[ring_r[s], src_r[t]], writes=[ps_r[pi]])
                    evac(j, fc, t, pi)
                    tick()
                if last_tile_done is not None and j == npieces - 1:
                    last_tile_done(t)
            done_piece()
            if hook is not None:
                hook()

    def linear_fm_t(kind, l, npieces, src, src_r, evac, tile_done=None):
        slots = [use_piece(kind, l, j) for j in range(npieces)]
        for t in range(NT):
            for j in range(npieces):
                s = slots[j]
                for fc in range(4):
                    pi = next_ps()

                    def fn(e, s=s, fc=fc, t=t, pi=pi):
                        ins = None
                        for kc in range(8):
                            ins = e.matmul(ps[pi][:, :],
                                           lhsT=ring[s][:, kc * 512 + fc * 128:kc * 512 + fc * 128 + 128],
                                           rhs=src[:, kc, t * 512:(t + 1) * 512],
                                           start=(kc == 0), stop=(kc == 7))
                        return ins
                    P.op("pe", fn, reads=[ring_r[s], src_r[t]], writes=[ps_r[pi]])
                    evac(j, fc, t, pi)
                    tick()
            if tile_done is not None:
                tile_done(t)
        for j in range(npieces):
            done_piece()

    def resid_evac(gate_chunk0):
        def evac(j, fc, t, pi, per_piece=4):
            fch = j * per_piece + fc
            s = slot_of(t)
            md = cur["mod"]
            P.op("dve", lambda e: e.scalar_tensor_tensor(
                out=xT[:, fch, t * 512:(t + 1) * 512], in0=ps[pi][:, :],
                scalar=md[:, gate_chunk0 + fch, s:s + 1], in1=xT[:, fch, t * 512:(t + 1) * 512],
                op0=ALU.mult, op1=ALU.add),
                reads=[ps_r[pi], cur["mod_r"][(gate_chunk0 + fch) // 4]], writes=[xT_r[fch][t]])
        return evac

    def mlp(l):
        big.reset()
        hid, _ = big.alloc(16 * TOK * 2, BF16, "hid")
        hid = hid.rearrange("p (c t) -> p c t", c=16)
        hid_r = [Res(f"hid{t}", init=big.inherit) for t in range(NT)]
        big.live.extend(hid_r)
        rt = [big.alloc(2048, F32, f"relu{i}") for i in range(3)]
        cnt = {"n": 0}
        hook, pshi = adaln_piece, 8
        if l + 1 < DEPTH:
            adaln_begin(l + 1)
        for hf in range(2):
            def up_evac(j, fc, t, pi):
                hc = j * 4 + fc
                ta, ta_r = rt[cnt["n"] % 3]
                cnt["n"] += 1
                P.op("act", lambda e: e.activation(out=ta[:, :], in_=ps[pi][:, :], func=AF.Relu),
                     reads=[ps_r[pi]], writes=[ta_r])
                P.op("dve", lambda e: e.tensor_tensor(out=hid[:, hc, t * 512:(t + 1) * 512],
                                                      in0=ta[:, :], in1=ta[:, :], op=ALU.mult),
                     reads=[ta_r], writes=[hid_r[t]])
            linear_fm("up", l, 4, 8, hT, hT_r, up_evac, j0=hf * 4, hook=hook, pshi=pshi)
            g2 = resid_evac(40)

            def down_evac(j, fc, t, pi):
                g2(j, fc, t, pi, per_piece=2)
            ltd = None
            if hf == 1:
                nctx = norm_begin("mod", l + 1, 0) if l + 1 < DEPTH else norm_begin("final")
                ltd = norm_tile_done(nctx, lag=2)
            linear_fm("down", l, 4, 16, hid, hid_r, down_evac, fc_per_piece=2, j0=hf * 4, hook=hook,
                      pshi=pshi, last_tile_done=ltd)

    def final_norm():
        flush_deferred()

    def attn_layer(l):
        i = l // 2
        big.reset()
        qkT, _ = big.alloc(16 * TOK * 2, BF16, "qkT")
        qkT = qkT.rearrange("p (c t) -> p c t", c=16)
        qk_r = [[Res(f"qk{c}_{t}", init=big.inherit) for t in range(NT)] for c in range(16)]
        Vb, _ = big.alloc(12 * 1024 * 2, BF16, "Vb")
        Vb = Vb.rearrange("p (b f) -> p b f", b=12)
        V_r = [Res(f"V{b}", init=big.inherit) for b in range(12)]
        for c in range(16):
            big.live.extend(qk_r[c])
        big.live.extend(V_r)
        gt, gt_r = big.alloc(2048 * 2, BF16, "gtab")
        sem_gt = P.new_sem(f"gt{l}")
        P.dma("pool", lambda e: e.dma_start(out=gt[0:16, :], in_=gt_d), sem_gt, writes=[gt_r])
        P.dma("pool", lambda e: e.dma_start(out=gt[64:80, :], in_=gt_d), sem_gt, writes=[gt_r])

        scr.reset()
        G = []
        for n in range(4):
            ap, r = scr.alloc(2048, F32, f"G{n}")
            G.append((ap, r, scr.last_off))
        zb = [(G[0][0], G[0][1]), (G[1][0], G[1][1])]
        pT = []
        for n in (2, 3):
            bfv = scr.view(G[n][2], 2048, BF16)
            for hh in range(2):
                r = Res(f"pT{n}_{hh}", init=scr.inherit)
                scr.live.append(r)
                pT.append((bfv[:, hh * 512:(hh + 1) * 512], r))
        stg = [(G[0][0], [G[0][1]]), (G[1][0], [G[1][1]]),
               (G[2][0], [pT[0][1], pT[1][1]]), (G[3][0], [pT[2][1], pT[3][1]])]
        stg_sem = [P.new_sem(f"st{l}_{n}") for n in range(4)]
        out_sems.extend(stg_sem)
        cnt = {"n": 0}
        Tb = [scr.alloc(16 * 64 * 2, BF16, f"T{n}") for n in range(4)]
        Hks = []
        for n in range(2):
            hk, hk_r = scr.alloc(17 * 64 * 2, BF16, f"hank{n}")
            Hks.append((hk.rearrange("p (a b) -> p a b", a=17), hk_r, P.new_sem(f"hk{l}_{n}")))
        ckT, ckT_r = scr.alloc(8 * 256 * 2, BF16, "ckT")
        ckT = ckT.rearrange("p (c k) -> p c k", c=8)
        cvb, cvb_r = scr.alloc(2 * 1024 * 2, BF16, "cvb")
        cvb = cvb.rearrange("p (c f) -> p c f", c=2)
        sem_ck = P.new_sem(f"ck{l}")
        sem_cv = P.new_sem(f"cv{l}")
        P.dma("pool", lambda e: e.dma_start(
            out=ckT[:, :, :], in_=ckT_d[i].rearrange("(c p) k -> p c k", p=128)),
            sem_ck, writes=[ckT_r])
        P.dma("pool", lambda e: e.dma_start(
            out=cvb[:, :, :], in_=cv_d[i].rearrange("(c p) f -> p c f", p=128)),
            sem_cv, writes=[cvb_r])

        cn = {"z": 0, "p": 0}

        def hankel_load(h):
            src = bass.AP(rpb_h, i * 7696 + 128 + (h * 15 - 1) * 31 - 48, [[1, 64], [31, 17], [1, 64]])
            Hk, Hk_r, hk_sem = Hks[h % 2]
            P.dma("pool", lambda e: e.dma_start(out=Hk[0:64, :, :], in_=src), hk_sem, writes=[Hk_r])

        def build_T(h, slot, load=True):
            Tt, Tt_r = Tb[slot]
            Tt = Tt.rearrange("p (o q) -> p o q", o=16)
            Hk, Hk_r, hk_sem = Hks[h % 2]
            if load:
                hankel_load(h)
            for half in range(2):
                pi = next_ps(0, 4)

                def fn(e, half=half, pi=pi):
                    ins = None
                    for oo in range(8):
                        a0 = half * 8 + oo
                        ins = e.matmul(ps[pi][:, (7 - oo) * 64:(8 - oo) * 64],
                                       lhsT=Hk[0:64, a0:a0 + 2, :].rearrange("p a b -> p (a b)"),
                                       rhs=Jb[0:64, :], start=True, stop=True)
                    return ins
                P.op("pe", fn, reads=[Hk_r, const_r], writes=[ps_r[pi]])
                P.op("dve", lambda e, half=half, pi=pi: e.tensor_tensor(
                    out=Tt[:, (1 - half) * 8:(2 - half) * 8, :],
                    in0=ps[pi][:, :].rearrange("p (o q) -> p o q", o=8),
                    in1=V("cmask", 64).unsqueeze(1).to_broadcast([128, 8, 64]), op=ALU.add),
                    reads=[ps_r[pi], vecs_r], writes=[Tt_r])
            return Tt, Tt_r

        hankel_load(0)
        hankel_load(1)

        def qk_evac(j, fc, t, pi):
            fch = j * 4 + fc
            if fch < 8:
                P.op("act", lambda e: e.activation(out=qkT[:, fch, t * 512:(t + 1) * 512],
                                                   in_=ps[pi][:, :], func=AF.Copy, scale=0.125),
                     reads=[ps_r[pi]], writes=[qk_r[fch][t]])
            else:
                n = cnt["n"] % 4
                cnt["n"] += 1
                sg, sg_r = stg[n]
                P.op("dve", lambda e: e.tensor_copy(out=sg[:, :], in_=ps[pi][:, :]),
                     reads=[ps_r[pi]], writes=sg_r)
                P.op("act", lambda e: e.activation(out=qkT[:, fch, t * 512:(t + 1) * 512],
                                                   in_=sg[:, :], func=AF.Copy),
                     reads=sg_r, writes=[qk_r[fch][t]])
                kc = fch - 8
                P.dma("sp", lambda e: e.dma_start(
                    out=kT_d[i, kc * 128:(kc + 1) * 128, t * 512:(t + 1) * 512], in_=sg[:, :]),
                    stg_sem[n], reads=sg_r)
        pshi = 8
        linear_fm("qk", l, 4, 8, hT, hT_r, qk_evac, hook=adaln_piece, pshi=pshi)

        T_first = [build_T(0, 0, load=False), build_T(1, 1, load=False)]

        for j in range(2):
            s = use_piece("v", l, j)
            for b in range(12):
                pi = next_ps(0, pshi)
                t = b // 4

                def fn(e, s=s, b=b, pi=pi):
                    ins = None
                    for kc in range(8):
                        ins = e.matmul(ps[pi][:, :], lhsT=hT[:, kc, b * 128:(b + 1) * 128],
                                       rhs=ring[s][:, kc * 512:(kc + 1) * 512],
                                       start=(kc == 0), stop=(kc == 7))
                    return ins
                P.op("pe", fn, reads=[ring_r[s], hT_r[t]], writes=[ps_r[pi]])
                n = cnt["n"] % 4
                cnt["n"] += 1
                sg, sg_r = stg[n]
                P.op("dve", lambda e, sg=sg, pi=pi: e.tensor_copy(out=sg[:, :], in_=ps[pi][:, :]),
                     reads=[ps_r[pi]], writes=sg_r)
                P.op("act", lambda e, sg=sg, b=b, j=j: e.activation(
                    out=Vb[:, b, j * 512:(j + 1) * 512], in_=sg[:, :], func=AF.Copy),
                    reads=sg_r, writes=[V_r[b]])
                P.dma("sp", lambda e, sg=sg, b=b, j=j: e.dma_start(
                    out=vo_d[i, b * 128:(b + 1) * 128, j * 512:(j + 1) * 512], in_=sg[:, :]),
                    stg_sem[n], reads=sg_r)
            done_piece()
            adaln_piece()

        OA = [4, 5]
        SA = [6, 7]

        def finish_pair(o_i, s_i, hp, col0, ncol, tq):
            rc, rc_r = zb[cn["z"] % 2]
            cn["z"] += 1
            P.op("act", lambda e: e.activation(out=rc[:, 0:ncol], in_=ps[s_i][:, 0:ncol], func=AF.Ln),
                 reads=[ps_r[s_i]], writes=[rc_r])
            P.op("act", lambda e: e.activation(out=rc[:, 0:ncol], in_=rc[:, 0:ncol], func=AF.Exp,
                                               scale=-1.0),
                 reads=[rc_r], writes=[rc_r])
            P.op("dve", lambda e: e.tensor_tensor(
                out=hT[:, hp, col0:col0 + ncol], in0=ps[o_i][:, 0:ncol], in1=rc[:, 0:ncol],
                op=ALU.mult),
                reads=[ps_r[o_i], rc_r], writes=[hT_r[tq]])

        def pv_pair(o_i, s_i, args):
            def fn(e):
                ins = None
                for (pb, vsrc, vsrc_r, ptile, ptile_r, c0, n, first) in args:
                    ins = e.matmul(ps[o_i][pb:pb + 64, c0:c0 + n], lhsT=vsrc, rhs=ptile[:, 0:n],
                                   start=first, stop=True, skip_group_check=True)
                for (pb, vsrc, vsrc_r, ptile, ptile_r, c0, n, first) in args:
                    ins = e.matmul(ps[s_i][pb:pb + 64, c0:c0 + n], lhsT=ones_b[:, 0:64],
                                   rhs=ptile[:, 0:n], start=first, stop=True, skip_group_check=True)
                return ins
            reads = [const_r]
            for a in args:
                reads += [a[2], a[4]]
            P.op("pe", fn, reads=reads, writes=[ps_r[o_i], ps_r[s_i]])

        class Blk:
            pass

        def ctx_block(o_i, s_i, hp, qh, par, lc, first):
            b = Blk()
            pb = 64 * par
            h = 2 * hp + par

            def prep():
                b.pi = next_ps(0, 4)
                b.reads = [ckT_r, qk_r[hp][qh]]
            b.prep = prep
            b.mm_s = lambda e: e.matmul(
                ps[b.pi][:, :], lhsT=ckT[pb:pb + 64, hp, lc * 128:(lc + 1) * 128],
                rhs=qkT[pb:pb + 64, hp, qh * 512:(qh + 1) * 512], start=True, stop=True)
            b.mm_g = None

            def post():
                pi = b.pi
                b.pt, b.pt_r = pT[cn["p"] % 4]
                cn["p"] += 1
                pt = b.pt
                P.op("act", lambda e: e.activation(
                    out=pt[:, :], in_=ps[pi][:, :], func=AF.Exp, bias=V("ctxg")),
                    reads=[ps_r[pi], vecs_r], writes=[b.pt_r])
            b.post = post
            b.pvargs = lambda: (pb, cvb[:, lc, h * 64:(h + 1) * 64], cvb_r, b.pt, b.pt_r, 0, 512, first)
            return b

        def own_block(o_i, s_i, hp, qh, par, c, rows, Tt, Tt_r):
            b = Blk()
            pb = 64 * par
            h = 2 * hp + par
            r0, n = rows[0], len(rows) * 64
            c0 = (r0 - qh * 8) * 64
            tk = c // 4

            def prep():
                b.pi = next_ps(0, 4)
                b.reads = [qk_r[8 + hp][tk], qk_r[hp][qh], gt_r]
            b.prep = prep
            b.mm_s = lambda e: e.matmul(
                ps[b.pi][:, 0:n], lhsT=qkT[pb:pb + 64, 8 + hp, c * 128:(c + 1) * 128],
                rhs=qkT[pb:pb + 64, hp, r0 * 64:r0 * 64 + n], start=True, stop=False,
                skip_group_check=True)
            gneed = []
            for rq in rows:
                rs_ = row_start(rq)
                s_ok = all(rs_ <= 2 * c + rl < rs_ + 8 for rl in range(2))
                gneed.append(not (s_ok and (c // 2) == (rq // 4)))
            gidx = [k_ for k_, x_ in enumerate(gneed) if x_]
            if gidx:
                g0, g1 = gidx[0], gidx[-1] + 1
                assert all(gneed[g0:g1])
                b.mm_g = lambda e: e.matmul(
                    ps[b.pi][:, g0 * 64:g1 * 64], lhsT=gt[pb:pb + 16, c * 128:(c + 1) * 128],
                    rhs=gt[pb:pb + 16, 1024 + (r0 + g0) * 64:1024 + (r0 + g1) * 64], start=False, stop=True,
                    skip_group_check=True)
            else:
                b.mm_g = None

            def post():
                pi = b.pi
                z, z_r = zb[cn["z"] % 2]
                cn["z"] += 1
                k = len(rows)
                o0 = 7 - 2 * c + r0
                assert 0 <= o0 and o0 + k <= 16, (c, r0, k)
                P.op("dve", lambda e: e.tensor_tensor(
                    out=z[:, 0:n], in0=ps[pi][:, 0:n],
                    in1=Tt[:, o0:o0 + k, :].rearrange("p o q -> p (o q)"), op=ALU.add),
                    reads=[ps_r[pi], Tt_r], writes=[z_r])
                b.pt, b.pt_r = pT[cn["p"] % 4]
                cn["p"] += 1
                pt = b.pt
                P.op("act", lambda e: e.activation(out=pt[:, 0:n], in_=z[:, 0:n], func=AF.Exp),
                     reads=[z_r], writes=[b.pt_r])
            b.post = post
            b.pvargs = lambda: (pb, Vb[:, c, h * 64:(h + 1) * 64], V_r[c], b.pt, b.pt_r, c0, n, False)
            return b

        def slotb_block(o_i, s_i, hp, par, sq, kc):
            b = Blk()
            pb = 64 * par
            h = 2 * hp + par
            q0 = 1024 + sq * 256
            k0 = q0 + kc * 128
            vb_i = k0 // 128

            def prep():
                b.pi = next_ps(0, 4)
                b.reads = [qk_r[8 + hp][2], qk_r[hp][2]]
            b.prep = prep
            b.mm_s = lambda e: e.matmul(
                ps[b.pi][:, 0:256], lhsT=qkT[pb:pb + 64, 8 + hp, k0:k0 + 128],
                rhs=qkT[pb:pb + 64, hp, q0:q0 + 256], start=True, stop=True)
            b.mm_g = None

            def post():
                pi = b.pi
                b.pt, b.pt_r = pT[cn["p"] % 4]
                cn["p"] += 1
                pt = b.pt
                P.op("act", lambda e: e.activation(
                    out=pt[:, 0:256], in_=ps[pi][:, 0:256], func=AF.Exp),
                    reads=[ps_r[pi]], writes=[b.pt_r])
            b.post = post
            b.pvargs = lambda: (pb, Vb[:, vb_i, h * 64:(h + 1) * 64], V_r[vb_i], b.pt, b.pt_r,
                                sq * 256, 256, kc == 0)
            return b

        def emit_scores(blks):
            for b in blks:
                b.prep()

            def fn(e):
                ins = None
                for b in blks:
                    ins = b.mm_s(e)
                for b in blks:
                    if b.mm_g is not None:
                        ins = b.mm_g(e)
                return ins
            reads = []
            for b in blks:
                reads += b.reads
            P.op("pe", fn, reads=reads, writes=[ps_r[b.pi] for b in blks])
            for b in blks:
                b.post()

        sched = []
        npair = 0
        for hp in range(8):
            sched.append(("preload", hp))
            for qh in range(2):
                if qh == 1:
                    sched.append(("pre", hp))
                o_i, s_i = OA[npair % 2], SA[npair % 2]
                npair += 1
                for lc in range(2):
                    sched.append(("step", o_i, s_i, [("ctx", o_i, s_i, hp, qh, par, lc, lc == 0)
                                                     for par in range(2)]))
                for c in range(8):
                    rows = [r for r in chunk_rows(c) if qh * 8 <= r < qh * 8 + 8]
                    if rows:
                        sched.append(("step", o_i, s_i, [("own", o_i, s_i, hp, qh, par, c, rows)
                                                         for par in range(2)]))
                sched.append(("post", lambda o_i=o_i, s_i=s_i, hp=hp, qh=qh: finish_pair(
                    o_i, s_i, hp, qh * 512, 512, qh)))
            sched.append(("swap", None))
        for hp in range(8):
            o_i, s_i = OA[npair % 2], SA[npair % 2]
            npair += 1
            for sq in range(2):
                for kc in range(2):
                    sched.append(("step", o_i, s_i, [("sb", o_i, s_i, hp, par, sq, kc) for par in range(2)]))
            sched.append(("post", lambda o_i=o_i, s_i=s_i, hp=hp: finish_pair(o_i, s_i, hp, 1024, 512, 2)))

        LAG = 1
        pending = []
        Tstate = {"cur": T_first, "nxt": None}

        def flush_one():
            ent = pending.pop(0)
            if ent[0] == "step":
                _, o_i, s_i, blks = ent
                pv_pair(o_i, s_i, [b.pvargs() for b in blks])
            else:
                ent[1]()

        def nsteps():
            return sum(1 for e_ in pending if e_[0] == "step")

        for ent in sched:
            kind = ent[0]
            if kind == "preload":
                hpn = ent[1]
                if hpn + 1 < 8:
                    hankel_load(2 * hpn + 2)
                    hankel_load(2 * hpn + 3)
            elif kind == "pre":
                hpn = ent[1]
                if hpn + 1 < 8:
                    sl = ((hpn + 1) % 2) * 2
                    Tstate["nxt"] = [build_T(2 * hpn + 2, sl, load=False),
                                     build_T(2 * hpn + 3, sl + 1, load=False)]
            elif kind == "swap":
                if Tstate["nxt"] is not None:
                    Tstate["cur"] = Tstate["nxt"]
                    Tstate["nxt"] = None
            elif kind == "post":
                pending.append(("post", ent[1]))
            else:
                _, o_i, s_i, specs = ent
                blks = []
                for obj in specs:
                    tag = obj[0]
                    if tag == "ctx":
                        b = ctx_block(*obj[1:])
                    elif tag == "own":
                        _, oo, ss, hp, qh, par, c, rows = obj
                        Tt, Tt_r = Tstate["cur"][par]
                        b = own_block(oo, ss, hp, qh, par, c, rows, Tt, Tt_r)
                    else:
                        b = slotb_block(*obj[1:])
                    blks.append(b)
                emit_scores(blks)
                pending.append(("step", o_i, s_i, blks))
                while nsteps() > LAG:
                    flush_one()
                    while pending and pending[0][0] == "post":
                        flush_one()
        while pending:
            flush_one()

        nctx = norm_begin("mod", l, 1)
        linear_fm_t("o", l, 2, hT, hT_r, resid_evac(16), tile_done=norm_tile_done(nctx, lag=3))

    def conv_layer(l):
        i = l // 2
        big.reset()
        up, up_r0 = big.alloc(8 * 6 * SEGP * 2, BF16, "upad")
        up = up.rearrange("p (c s w) -> p c s w", c=8, s=6)
        up_r = [Res(f"up{c}", init=big.inherit) for c in range(8)]
        big.live.extend(up_r)
        vv, _ = big.alloc(8 * TOK * 4, F32, "v")
        vv = vv.rearrange("p (c t) -> p c t", c=8)
        v_r = [[Res(f"v{c}_{t}", init=big.inherit) for t in range(NT)] for c in range(8)]
        for c in range(8):
            big.live.extend(v_r[c])
        scr.reset()
        dg = [scr.alloc(31 * 128 * 2, BF16, f"dg{n}") for n in range(2)]
        sg = [scr.alloc(2048, F32, f"sig{n}") for n in range(2)]
        for c in range(8):
            P.op("pool", lambda e, c=c: e.memset(up[:, c, :, 0:15], 0.0), writes=[up_r[c]])
            P.op("pool", lambda e, c=c: e.memset(up[:, c, :, 271:286], 0.0), writes=[up_r[c]])
        cnt = {"n": 0}
        gbank = {}

        def pw1_evac(j, fc, t, pi):
            if fc < 2:
                gbank[(fc, t)] = pi
                return
            c = 2 * j + (fc - 2)
            pa = gbank[(fc - 2, t)]
            sgt, sgt_r = sg[cnt["n"] % 2]
            cnt["n"] += 1
            P.op("act", lambda e: e.activation(out=sgt[:, :], in_=ps[pi][:, :], func=AF.Sigmoid),
                 reads=[ps_r[pi]], writes=[sgt_r])
            P.op("dve", lambda e: e.tensor_tensor(
                out=up[:, c, 2 * t:2 * t + 2, 15:271],
                in0=ps[pa][:, :].rearrange("p (s w) -> p s w", s=2),
                in1=sgt[:, :].rearrange("p (s w) -> p s w", s=2), op=ALU.mult),
                reads=[ps_r[pa], sgt_r], writes=[up_r[c]])
        sqc = [scr.alloc(1024, BF16, f"sqc{n}") for n in range(2)]
        mean, mean_r = scr.alloc(2048, F32, "mean")
        rstd, rstd_r = scr.alloc(2048, F32, "rstd")
        tmp = sg
        nd = {"n": 0, "q": 0, "t": 0}
        dgB = [Res("dgB0", init=scr.inherit), Res("dgB1", init=scr.inherit)]
        scr.live.extend(dgB)
        diag_q = []
        pending_tail = []
        pending_stats = []
        NPOOL = 22

        def build_diag(c):
            dgt, dgt_r = dg[nd["n"] % 2]
            dgB_r = dgB[nd["n"] % 2]
            nd["n"] += 1
            dgt = dgt.rearrange("p (j m) -> p j m", j=31)
            P.op("pool", lambda e: e.tensor_tensor(
                out=dgt[:, 0:NPOOL, :],
                in0=ident_b[:, :].unsqueeze(1).to_broadcast([128, NPOOL, 128]),
                in1=V("wdw", NPOOL, i * 248 + c * 31).unsqueeze(2).to_broadcast([128, NPOOL, 128]),
                op=ALU.mult),
                reads=[const_r, vecs_r], writes=[dgt_r])
            P.op("dve", lambda e: e.tensor_tensor(
                out=dgt[:, NPOOL:31, :],
                in0=ident_b[:, :].unsqueeze(1).to_broadcast([128, 31 - NPOOL, 128]),
                in1=V("wdw", 31 - NPOOL, i * 248 + c * 31 + NPOOL).unsqueeze(2).to_broadcast(
                    [128, 31 - NPOOL, 128]),
                op=ALU.mult),
                reads=[const_r, vecs_r], writes=[dgB_r])
            diag_q.append((dgt, dgt_r, dgB_r))

        build_diag(0)
        for j in range(4):
            s = use_piece("pw1", l, j)
            for t in range(NT):
                banks = []
                for fc in range(4):
                    pi = next_ps()
                    banks.append(pi)

                    def fn(e, s=s, fc=fc, t=t, pi=pi):
                        ins = None
                        for kc in range(8):
                            ins = e.matmul(ps[pi][:, :],
                                           lhsT=ring[s][:, kc * 512 + fc * 128:kc * 512 + fc * 128 + 128],
                                           rhs=hT[:, kc, t * 512:(t + 1) * 512],
                                           start=(kc == 0), stop=(kc == 7))
                        return ins
                    P.op("pe", fn, reads=[ring_r[s], hT_r[t]], writes=[ps_r[pi]])
                    pw1_evac(j, fc, t, pi)
            done_piece()
        for c in range(8):
            P.op("dve", lambda e, c=c: e.tensor_scalar(
                out=up[:, c, 1:4, 0:15], in0=up[:, c, 0:3, 256:271], scalar1=V("flag"), scalar2=None,
                op0=ALU.mult), reads=[up_r[c], vecs_r], writes=[up_r[c]])
            P.op("dve", lambda e, c=c: e.tensor_scalar(
                out=up[:, c, 0:3, 271:286], in0=up[:, c, 1:4, 15:30], scalar1=V("flag"), scalar2=None,
                op0=ALU.mult), reads=[up_r[c], vecs_r], writes=[up_r[c]])
        for t in range(NT):
            tsl = slice(t * 512, (t + 1) * 512)
            pm, pq = (4, 5) if t % 2 == 0 else (6, 7)

            def stats(c, t=t, tsl=tsl, pm=pm, pq=pq, sq=None, sq_r=None):
                P.op("pe", lambda e: e.matmul(ps[pm][:, :], lhsT=ones_f[:, :], rhs=vv[:, c, tsl],
                                              start=(c == 0), stop=(c == 7)),
                     reads=[v_r[c][t], const_r], writes=[ps_r[pm]])
                P.op("pe", lambda e: e.matmul(ps[pq][:, :], lhsT=ones_b[:, :], rhs=sq[:, :],
                                              start=(c == 0), stop=(c == 7)),
                     reads=[sq_r, const_r], writes=[ps_r[pq]])
            prev = None
            for c in range(8):
                dgt, dgt_r, dgB_r = diag_q.pop(0)
                if not (t == NT - 1 and c == 7):
                    build_diag((c + 1) % 8)
                pi = next_ps(0, 4)

                def fn(e, c=c, t=t, pi=pi, dgt=dgt):
                    ins = None
                    for sgm in range(2):
                        for jt in range(31):
                            ins = e.matmul(ps[pi][:, sgm * 256:(sgm + 1) * 256], lhsT=dgt[:, jt, :],
                                           rhs=up[:, c, 2 * t + sgm, jt:jt + 256],
                                           start=(jt == 0), stop=(jt == 30))
                    return ins
                P.op("pe", fn, reads=[dgt_r, dgB_r, up_r[c]], writes=[ps_r[pi]])
                P.op("act", lambda e, c=c, tsl=tsl, pi=pi: e.activation(
                    out=vv[:, c, tsl], in_=ps[pi][:, :], func=AF.Identity,
                    bias=V("bdw", 1, i * 8 + c)),
                    reads=[ps_r[pi], vecs_r], writes=[v_r[c][t]])
                sq, sq_r = sqc[nd["q"] % 2]
                nd["q"] += 1
                P.op("act", lambda e, c=c, tsl=tsl, sq=sq: e.activation(
                    out=sq[:, :], in_=vv[:, c, tsl], func=AF.Square),
                    reads=[v_r[c][t]], writes=[sq_r])
                if prev is not None:
                    stats(*prev[:1], sq=prev[1], sq_r=prev[2])
                prev = (c, sq, sq_r)
                if c == 0 and pending_stats:
                    pending_stats.pop(0)()
                if 1 <= c <= 5 and pending_tail:
                    pending_tail.pop(0)()
            def tail_stats(prev=prev, stats=stats):
                stats(*prev[:1], sq=prev[1], sq_r=prev[2])

            def tail(part, t=t, tsl=tsl, pm=pm, pq=pq):
                if part > 0:
                    tail_chunks(part, t, tsl, pm, pq)
                    return
                ta, ta_r = tmp[0]
                P.op("act", lambda e, pm=pm, ta=ta: e.activation(out=ta[:, :], in_=ps[pm][:, :], func=AF.Square,
                                                                 scale=1.0 / D),
                     reads=[ps_r[pm]], writes=[ta_r])
                P.op("act", lambda e, pm=pm: e.activation(out=ps[pm][:, :], in_=ps[pm][:, :], func=AF.Copy,
                                                          scale=1.0 / D),
                     reads=[ps_r[pm]], writes=[ps_r[pm]])
                P.op("dve", lambda e, ta=ta, pq=pq: e.scalar_tensor_tensor(
                    out=rstd[:, :], in0=ps[pq][:, :], scalar=1.0 / D, in1=ta[:, :],
                    op0=ALU.mult, op1=ALU.subtract),
                    reads=[ps_r[pq], ta_r], writes=[rstd_r])
                P.op("act", lambda e: e.activation(out=rstd[:, :], in_=rstd[:, :], func=AF.Ln,
                                                   bias=V("epsl")),
                     reads=[rstd_r, vecs_r], writes=[rstd_r])
                P.op("act", lambda e, pq=pq: e.activation(out=ps[pq][:, :], in_=rstd[:, :], func=AF.Exp,
                                                          scale=-0.5),
                     reads=[rstd_r], writes=[ps_r[pq]])

            def tail_chunks(part, t, tsl, pm, pq):
                for c in range(2 * (part - 1), 2 * part):
                    ta, ta_r = tmp[nd["t"] % 2]
                    nd["t"] += 1
                    P.op("dve", lambda e, c=c, ta=ta, tsl=tsl, pm=pm: e.tensor_tensor(
                        out=ta[:, :], in0=vv[:, c, tsl], in1=ps[pm][:, :], op=ALU.subtract),
                        reads=[v_r[c][t], ps_r[pm]], writes=[ta_r])
                    P.op("dve", lambda e, ta=ta, pq=pq: e.tensor_tensor(
                        out=ta[:, :], in0=ta[:, :], in1=ps[pq][:, :], op=ALU.mult),
                        reads=[ta_r, ps_r[pq]], writes=[ta_r])
                    P.op("act", lambda e, c=c, ta=ta, tsl=tsl: e.activation(
                        out=hT[:, c, tsl], in_=ta[:, :], func=AF.Silu,
                        scale=V("lng", 1, i * 8 + c), bias=V("lnb", 1, i * 8 + c)),
                        reads=[ta_r, vecs_r], writes=[hT_r[t]])
            pending_stats.append(tail_stats)
            for part in range(5):
                pending_tail.append(lambda part=part, tail=tail: tail(part))
        pending_stats.pop(0)()
        while pending_tail:
            pending_tail.pop(0)()
        nctx = norm_begin("mod", l, 1)
        st["psi"] = 6
        linear_fm_t("pw2", l, 2, hT, hT_r, resid_evac(16), tile_done=norm_tile_done(nctx, lag=3))

    del PLAN[:]
    ctx0 = norm_begin("mod", 0, 0)
    rp0 = []
    for t in range(NT):
        norm_A(ctx0, t)
        rp0.append(norm_B1(ctx0, t))
    adaln_begin(0)
    for _ in range(4):
        adaln_piece()
    for t in range(NT):
        norm_B2(ctx0, t, *rp0[t])
    for l in range(DEPTH):
        set_layer(l)
        flush_deferred()
        if l % 2 == 0:
            attn_layer(l)
        else:
            conv_layer(l)
        mlp(l)
    final_norm()
    assert st["next_use"] == NPIECE, (st, NPIECE)
    sp = P.eng["sp"]
    for s in out_sems:
        if s.count:
            sp.need(s, s.count)
    P.emit()
    return nc


_CACHE = {}


def kernel(**inp):
    inp = {k: np.asarray(v) for k, v in inp.items()}
    if "nc" not in _CACHE:
        _CACHE["nc"] = build_program()
    nc = _CACHE["nc"]
    wall = build_wall(inp)
    xp = inp["x_prompt"].astype(np.float32, copy=False)
    xs = inp["x_sample"].astype(np.float32, copy=False)
    roles = []
    for core in range(8):
        if core < 4:
            roles.append((core, [2 * core, 2 * core + 1], None))
        else:
            base = 8 + (core - 4) * 6
            roles.append((None, [base + 4, base + 5], [base, base + 1, base + 2, base + 3]))
    in_maps = []
    zeros_ck = np.zeros((2, D, 256), np.float32)
    zeros_cv = np.zeros((2, 256, D), np.float32)
    zeros_rp = np.zeros((2, 7696), np.float32)
    for core in range(8):
        sb, pB, pA = roles[core]
        if sb is not None:
            xa = xs[sb]
        else:
            xa = np.concatenate([xp[p] for p in pA], axis=0)
        xb = np.concatenate([xp[p] for p in pB], axis=0)
        x = np.concatenate([xa, xb], axis=0)
        m = {"xT": np.ascontiguousarray(x.T), "wall": wall, "vecs": build_vecs(inp, sb),
             "gtab": build_gtab(sb is not None)}
        if sb is not None:
            ck = inp["cache_k"][sb].reshape(2, 256, D)
            m["ckT"] = np.ascontiguousarray(ck.transpose(0, 2, 1)).astype(np.float32)
            m["cv"] = np.ascontiguousarray(inp["cache_v"][sb].reshape(2, 256, D)).astype(np.float32)
            rp = np.zeros((2, 7696), np.float32)
            rp[:, 128:128 + 7440] = inp["rpb"].reshape(2, 7440)
            m["rpbp"] = rp
        else:
            m["ckT"], m["cv"], m["rpbp"] = zeros_ck, zeros_cv, zeros_rp
        in_maps.append(m)
    res = run_bass_kernel_spmd(nc, in_maps, core_ids=list(range(8)))
    outs = res.results
    y_prompt = np.empty((32, 256, D), np.float32)
    y_sample = np.empty((4, 1024, D), np.float32)
    nk = np.empty((32, 2, 256, 16, 64), np.float32)
    nv = np.empty((32, 2, 256, 16, 64), np.float32)
    for core in range(8):
        sb, pB, pA = roles[core]
        y = outs[core]["yT"].T
        kt = outs[core]["kTo"].transpose(0, 2, 1)
        vo = outs[core]["vo"]
        if sb is not None:
            y_sample[sb] = y[0:1024]
        else:
            for n, p in enumerate(pA):
                y_prompt[p] = y[n * 256:(n + 1) * 256]
                nk[p] = kt[:, n * 256:(n + 1) * 256].reshape(2, 256, 16, 64)
                nv[p] = vo[:, n * 256:(n + 1) * 256].reshape(2, 256, 16, 64)
        for n, p in enumerate(pB):
            y_prompt[p] = y[1024 + n * 256:1024 + (n + 1) * 256]
            nk[p] = kt[:, 1024 + n * 256:1024 + (n + 1) * 256].reshape(2, 256, 16, 64)
            nv[p] = vo[:, 1024 + n * 256:1024 + (n + 1) * 256].reshape(2, 256, 16, 64)
    return (y_prompt, y_sample, nk, nv)
```

```python
import numpy as np
import concourse.bass as bass
import concourse.mybir as mybir
from concourse.bass_utils import run_bass_kernel_spmd

F32 = mybir.dt.float32
F32R = mybir.dt.float32r
BF16 = mybir.dt.bfloat16
U8 = mybir.dt.uint8
AF = mybir.ActivationFunctionType
ALU = mybir.AluOpType

D = 1024
TOK = 1536
NT = 3
DEPTH = 4
NEG = -30000.0
SEGP = 286


def piece_plan():
    plan = []
    for l in range(DEPTH):
        for j in range(12):
            plan.append(("ada", l, j))
        if l % 2 == 0:
            for j in range(4):
                plan.append(("qk", l, j))
            for j in range(2):
                plan.append(("v", l, j))
            for j in range(2):
                plan.append(("o", l, j))
        else:
            for j in range(4):
                plan.append(("pw1", l, j))
            for j in range(2):
                plan.append(("pw2", l, j))
        for hf in range(2):
            for j in range(4):
                plan.append(("up", l, hf * 4 + j))
            for j in range(4):
                plan.append(("down", l, hf * 4 + j))
    return plan


NPIECE = len(piece_plan())
PLAN = []


def tile_kf(w, f0, nf):
    return np.ascontiguousarray(
        w[:, f0:f0 + nf].reshape(8, 128, nf).transpose(1, 0, 2)).reshape(128, 8 * nf)


def build_wall(inp):
    wall = np.empty((NPIECE, 128, 4096), np.float32)
    for n, (kind, l, j) in enumerate(PLAN):
        i = l // 2
        if kind == "ada":
            wall[n] = tile_kf(inp["w_ada"][l], j * 512, 512)
        elif kind == "qk":
            wall[n] = tile_kf(inp["w_qkv"][i], j * 512, 512)
        elif kind == "v":
            wall[n] = tile_kf(inp["w_qkv"][i], 2048 + j * 512, 512)
        elif kind == "o":
            wall[n] = tile_kf(inp["w_o"][i], j * 512, 512)
        elif kind == "pw1":
            w = inp["w_pw1"][i]
            cols = np.concatenate([np.arange(256 * j, 256 * j + 256),
                                   1024 + np.arange(256 * j, 256 * j + 256)])
            wall[n] = tile_kf(w[:, cols], 0, 512)
        elif kind == "pw2":
            wall[n] = tile_kf(inp["w_pw2"][i], j * 512, 512)
        elif kind == "up":
            wall[n] = tile_kf(inp["w_up"][l], j * 512, 512)
        elif kind == "down":
            hf, jj = divmod(j, 4)
            w = inp["w_down"][l][hf * 2048:(hf + 1) * 2048, jj * 256:(jj + 1) * 256]
            wall[n] = np.ascontiguousarray(
                w.reshape(16, 128, 256).transpose(1, 0, 2)).reshape(128, 4096)
    return wall


VOFF = {}
_nv = 0
for _name, _n in [("cond", 16), ("bada", 192), ("ng", 64), ("fg", 8), ("bdw", 16), ("lng", 16),
                  ("lnb", 16), ("wdw", 496), ("gate", 128), ("ctxg", 1), ("flag", 1),
                  ("cmask", 64), ("epsr", 1), ("epsl", 1), ("J", 64), ("ident", 128)]:
    VOFF[_name] = _nv
    _nv += _n
NV = _nv


def fm(v):
    return np.ascontiguousarray(np.asarray(v, np.float32).reshape(8, 128).T)


def row_start(r):
    return int(np.clip(r - 4, 0, 8))


def chunk_rows(c):
    out = []
    for rq in range(16):
        rs = row_start(rq)
        if rs <= 2 * c + 1 and rs + 7 >= 2 * c:
            out.append(rq)
    return out


def build_gtab(is_s):
    g = np.zeros((16, 2048), np.float32)
    for rq in range(16):
        for c in range(8):
            for rl in range(2):
                rk = 2 * c + rl
                if is_s:
                    rs = row_start(rq)
                    ok = rs <= rk < rs + 8
                else:
                    ok = (rk // 4) == (rq // 4)
                g[rq, c * 128 + rl * 64:c * 128 + (rl + 1) * 64] = 0.0 if ok else NEG
        g[rq, 1024 + rq * 64:1024 + (rq + 1) * 64] = 1.0
    return g


def build_vecs(inp, sample_b):
    v = np.zeros((128, NV), np.float32)
    is_s = sample_b is not None
    condA = inp["c"][sample_b] if is_s else inp["c_ctx"]
    cd = np.stack([fm(condA), fm(inp["c_ctx"])], axis=-1)
    v[:, VOFF["cond"]:VOFF["cond"] + 16] = cd.reshape(128, 16)
    for l in range(DEPTH):
        b = np.asarray(inp["b_ada"][l], np.float32).reshape(48, 128).T
        v[:, VOFF["bada"] + l * 48:VOFF["bada"] + (l + 1) * 48] = b
        for s in range(2):
            o = VOFF["ng"] + (l * 2 + s) * 8
            v[:, o:o + 8] = fm(inp["norm_g"][l, s])
    v[:, VOFF["fg"]:VOFF["fg"] + 8] = fm(inp["final_g"])
    for i in range(2):
        v[:, VOFF["bdw"] + i * 8:VOFF["bdw"] + (i + 1) * 8] = fm(inp["b_dw"][i])
        v[:, VOFF["lng"] + i * 8:VOFF["lng"] + (i + 1) * 8] = fm(inp["conv_ln_g"][i])
        v[:, VOFF["lnb"] + i * 8:VOFF["lnb"] + (i + 1) * 8] = fm(inp["conv_ln_b"][i])
        w = np.asarray(inp["w_dw"][i], np.float32)
        wf = w.reshape(31, 8, 128).transpose(2, 1, 0)
        v[:, VOFF["wdw"] + i * 248:VOFF["wdw"] + (i + 1) * 248] = wf.reshape(128, 248)
    g = np.zeros((128, 8, 16), np.float32)
    for c in range(8):
        for rq in range(16):
            for rl in range(2):
                rk = 2 * c + rl
                if is_s:
                    rs = row_start(rq)
                    ok = rs <= rk < rs + 8
                else:
                    ok = (rk // 4) == (rq // 4)
                g[rl * 64:(rl + 1) * 64, c, rq] = 0.0 if ok else NEG
    v[:, VOFF["gate"]:VOFF["gate"] + 128] = g.reshape(128, 128)
    v[:, VOFF["ctxg"]] = 0.0 if is_s else NEG
    v[:, VOFF["flag"]] = 1.0 if is_s else 0.0
    cm = np.zeros((64, 64), np.float32)
    if is_s:
        for cq in range(64):
            ws = int(np.clip(cq - 8, 0, 48))
            for ck in range(64):
                if not (ws <= ck < ws + 16):
                    cm[ck, cq] = NEG
    v[0:64, VOFF["cmask"]:VOFF["cmask"] + 64] = cm
    v[64:128, VOFF["cmask"]:VOFF["cmask"] + 64] = cm
    v[:, VOFF["epsr"]] = 1e-6
    v[:, VOFF["epsl"]] = 1e-5
    v[0:64, VOFF["J"]:VOFF["J"] + 64] = np.eye(64, dtype=np.float32)[::-1]
    v[:, VOFF["ident"]:VOFF["ident"] + 128] = np.eye(128, dtype=np.float32)
    return v


class Sem:
    def __init__(self, h, name):
        self.h = h
        self.name = name
        self.count = 0


class Res:
    __slots__ = ("name", "w", "r")

    def __init__(self, name, init=None):
        self.name = name
        self.w = None
        self.r = dict(init) if init else {}

    def events(self):
        ev = dict(self.r)
        if self.w is not None:
            s, v = self.w
            if ev.get(s, 0) < v:
                ev[s] = v
        return ev


class Eng:
    def __init__(self, name, sem):
        self.name = name
        self.sem = sem
        self.items = []
        self.waited = {}

    def need(self, sem, val):
        if self.waited.get(sem, 0) >= val:
            return
        self.waited[sem] = val
        self.items.append(("wait", sem, val))


class Prog:
    def __init__(self, nc):
        self.nc = nc
        self.sems = []
        self.eng = {}

    def new_sem(self, name):
        s = Sem(None, name)
        self.sems.append(s)
        return s

    def add_engine(self, name):
        self.eng[name] = Eng(name, self.new_sem("e_" + name))

    def _deps(self, eng, reads, writes, extra):
        for r in reads:
            if r.w is not None:
                s, v = r.w
                if s is eng.sem and eng.name == "pe":
                    continue
                eng.need(s, v)
        for w in writes:
            if w.w is not None:
                s, v = w.w
                if not (s is eng.sem and eng.name == "pe"):
                    eng.need(s, v)
            for s, v in w.r.items():
                if s is eng.sem and eng.name == "pe":
                    continue
                eng.need(s, v)
        for s, v in extra:
            if s is eng.sem:
                continue
            eng.need(s, v)

    def op(self, ename, fn, reads=(), writes=(), extra=()):
        eng = self.eng[ename]
        self._deps(eng, reads, writes, extra)
        eng.sem.count += 1
        ev = (eng.sem, eng.sem.count)
        eng.items.append(("op", fn, eng.sem, 1))
        for r in reads:
            if r.r.get(ev[0], 0) < ev[1]:
                r.r[ev[0]] = ev[1]
        for w in writes:
            w.w = ev
            w.r = {}
        return ev

    def dma(self, qname, fn, sem, reads=(), writes=(), extra=()):
        eng = self.eng[qname]
        self._deps(eng, reads, writes, extra)
        sem.count += 16
        ev = (sem, sem.count)
        eng.items.append(("op", fn, sem, 16))
        for r in reads:
            if r.r.get(sem, 0) < ev[1]:
                r.r[sem] = ev[1]
        for w in writes:
            w.w = ev
            w.r = {}
        return ev

    def emit(self):
        nc = self.nc
        from contextlib import ExitStack
        with ExitStack() as st:
            for s in self.sems:
                s.h = st.enter_context(nc.semaphore(s.name))
            block = st.enter_context(nc.Block())

            def runner(eng):
                def body(e):
                    for it in eng.items:
                        if it[0] == "wait":
                            e.wait_ge(it[1].h, it[2])
                        else:
                            ins = it[1](e)
                            ins.then_inc(it[2].h, it[3])
                return body

            block.tensor(runner(self.eng["pe"]))
            block.scalar(runner(self.eng["act"]))
            block.vector(runner(self.eng["dve"]))
            block.gpsimd(runner(self.eng["pool"]))
            block.sync(runner(self.eng["sp"]))


class Region:
    def __init__(self, nc, name, nbytes):
        self.t = nc.alloc_sbuf_tensor(name, [128, nbytes], U8)
        self.nbytes = nbytes
        self.name = name
        self.live = []
        self.inherit = {}
        self.off = 0
        self.n = 0

    def reset(self):
        for r in self.live:
            for s, v in r.events().items():
                if self.inherit.get(s, 0) < v:
                    self.inherit[s] = v
        self.live = []
        self.off = 0

    def view(self, off, nbytes, dtype):
        return self.t[:, off:off + nbytes].bitcast(dtype)

    def alloc(self, nbytes, dtype, name=None):
        nbytes = (nbytes + 31) // 32 * 32
        assert self.off + nbytes <= self.nbytes, (self.name, name, self.off, nbytes, self.nbytes)
        ap = self.t[:, self.off:self.off + nbytes].bitcast(dtype)
        self.last_off = self.off
        self.off += nbytes
        self.n += 1
        r = Res(f"{self.name}_{name}_{self.n}", init=self.inherit)
        self.live.append(r)
        return ap, r


def build_program():
    nc = bass.Bass("TRN2", target_bir_lowering=False)
    xT_d = nc.dram_tensor("xT", [D, TOK], F32, kind="ExternalInput").ap()
    wall_d = nc.dram_tensor("wall", [NPIECE, 128, 4096], F32, kind="ExternalInput").ap()
    vec_d = nc.dram_tensor("vecs", [128, NV], F32, kind="ExternalInput").ap()
    rpb_h = nc.dram_tensor("rpbp", [2, 7696], F32, kind="ExternalInput")
    ckT_d = nc.dram_tensor("ckT", [2, D, 256], F32, kind="ExternalInput").ap()
    cv_d = nc.dram_tensor("cv", [2, 256, D], F32, kind="ExternalInput").ap()
    gt_d = nc.dram_tensor("gtab", [16, 2048], F32, kind="ExternalInput").ap()
    yT_d = nc.dram_tensor("yT", [D, TOK], F32, kind="ExternalOutput").ap()
    kT_d = nc.dram_tensor("kTo", [2, D, TOK], F32, kind="ExternalOutput").ap()
    vo_d = nc.dram_tensor("vo", [2, TOK, D], F32, kind="ExternalOutput").ap()

    P = Prog(nc)
    for e in ("pe", "act", "dve", "pool", "sp"):
        P.add_engine(e)

    xT = nc.alloc_sbuf_tensor("xTs", [128, 8, TOK], F32)
    xT_r = [[Res(f"x{c}_{t}") for t in range(NT)] for c in range(8)]
    hT = nc.alloc_sbuf_tensor("hTs", [128, 8, TOK], BF16)
    hT_r = [Res(f"h{t}") for t in range(NT)]
    vecs = nc.alloc_sbuf_tensor("vecs_s", [128, NV], F32)
    vecs_r = Res("vecs")
    mods = [nc.alloc_sbuf_tensor(f"mod{n}", [128, 48, 2], F32) for n in range(2)]
    mods_r = [[Res(f"mod{n}_{j}") for j in range(12)] for n in range(2)]
    coefs = [nc.alloc_sbuf_tensor(f"coef{n}", [128, 2, 8, 2], F32) for n in range(2)]
    coefs_r = [[Res(f"coef{n}_{k}") for k in range(2)] for n in range(2)]
    cur = {"mod": mods[0], "mod_r": mods_r[0], "coef": coefs[0], "coef_r": coefs_r[0]}
    scond = nc.alloc_sbuf_tensor("scond", [128, 8, 2], BF16)
    scond_r = Res("scond")
    ident_b = nc.alloc_sbuf_tensor("identb", [128, 128], BF16)
    ones_b = nc.alloc_sbuf_tensor("onesb", [128, 128], BF16)
    ones_f = nc.alloc_sbuf_tensor("onesf", [128, 128], F32)
    Jb = nc.alloc_sbuf_tensor("Jb", [128, 64], BF16)
    const_r = Res("const")
    NSLOT = 3
    ring = [nc.alloc_sbuf_tensor(f"ring{i}", [128, 4096], BF16) for i in range(NSLOT)]
    ring_r = [Res(f"ring{i}") for i in range(NSLOT)]
    ring_sem = [P.new_sem(f"ringsem{i}") for i in range(NSLOT)]
    big = Region(nc, "big", 77824)
    scr = Region(nc, "scr", 29184)
    ps = [nc.alloc_psum_tensor(f"ps{i}", [128, 512], F32) for i in range(8)]
    ps_r = [Res(f"ps{i}") for i in range(8)]

    def V(name, n=1, j=0):
        o = VOFF[name] + j
        return vecs[:, o:o + n]

    st = {"next_load": 0, "next_use": 0, "psi": 0}

    def load_piece():
        n = st["next_load"]
        if n >= NPIECE:
            return
        st["next_load"] += 1
        s = n % NSLOT
        P.dma("pool", lambda e, n=n, s=s: e.dma_start(out=ring[s][:, :], in_=wall_d[n],
                                                      max_dma_last_dim=8192),
              ring_sem[s], writes=[ring_r[s]])

    def use_piece(kind, l, j):
        n = st["next_use"]
        PLAN.append((kind, l, j))
        st["next_use"] += 1
        return n % NSLOT

    def done_piece():
        load_piece()

    for _ in range(NSLOT):
        load_piece()

    def next_ps(lo=0, hi=8):
        i = st["psi"]
        if i < lo or i >= hi:
            i = lo
        st["psi"] = i + 1
        return i

    sem_v = P.new_sem("ld_vecs")
    P.dma("sp", lambda e: e.dma_start(out=vecs[:, :], in_=vec_d), sem_v, writes=[vecs_r])
    sem_x = [P.new_sem(f"ld_x{t}") for t in range(NT)]
    for t in range(NT):
        P.dma("sp", lambda e, t=t: e.dma_start(
            out=xT[:, :, t * 512:(t + 1) * 512],
            in_=xT_d.rearrange("(c p) t -> p c t", p=128)[:, :, t * 512:(t + 1) * 512]),
            sem_x[t], writes=[xT_r[c][t] for c in range(8)])

    P.op("dve", lambda e: e.tensor_copy(out=ident_b[:, :], in_=V("ident", 128)),
         reads=[vecs_r], writes=[const_r])
    P.op("dve", lambda e: e.tensor_copy(out=Jb[:, :], in_=V("J", 64)), reads=[vecs_r], writes=[const_r])
    P.op("pool", lambda e: e.memset(ones_b[:, :], 1.0), writes=[const_r])
    P.op("pool", lambda e: e.memset(ones_f[:, :], 1.0), writes=[const_r])
    P.op("act", lambda e: e.activation(out=scond[:, :, :].rearrange("p c s -> p (c s)"),
                                       in_=V("cond", 16), func=AF.Silu),
         reads=[vecs_r], writes=[scond_r])

    out_sems = []

    ada = {}

    ada_q = []

    def adaln_begin(l):
        ada_q.extend((l, j) for j in range(12))

    def adaln_piece():
        if not ada_q:
            return
        l, j = ada_q.pop(0)
        pi = next_ps(0, 4) if st.get("in_attn") else next_ps()
        s = use_piece("ada", l, j)

        def fn(e):
            ins = None
            for fc in range(4):
                col = fc * 2
                for kc in range(8):
                    ins = e.matmul(ps[pi][:, col:col + 2],
                                   lhsT=ring[s][:, kc * 512 + fc * 128:kc * 512 + fc * 128 + 128],
                                   rhs=scond[:, kc, :], start=(kc == 0), stop=(kc == 7))
            return ins
        P.op("pe", fn, reads=[ring_r[s], scond_r], writes=[ps_r[pi]])
        done_piece()
        mod, mod_r = mods[l % 2], mods_r[l % 2]
        coef, coef_r = coefs[l % 2], coefs_r[l % 2]
        P.op("dve", lambda e: e.tensor_tensor(
            out=mod[:, j * 4:(j + 1) * 4, :],
            in0=ps[pi][:, 0:8].rearrange("p (j s) -> p j s", s=2),
            in1=V("bada", 4, l * 48 + j * 4).unsqueeze(2).to_broadcast([128, 4, 2]), op=ALU.add),
            reads=[ps_r[pi], vecs_r], writes=[mod_r[j]])
        for k, jj in ((0, 3), (1, 9)):
            if j == jj:
                P.op("dve", lambda e, k=k: e.scalar_tensor_tensor(
                    out=coef[:, k, :, :], in0=mod[:, 8 + 24 * k:16 + 24 * k, :], scalar=1.0,
                    in1=V("ng", 8, (l * 2 + k) * 8).unsqueeze(2).to_broadcast([128, 8, 2]),
                    op0=ALU.add, op1=ALU.mult),
                    reads=[mod_r[jj - 1], mod_r[jj], vecs_r], writes=[coef_r[k]])

    def set_layer(l):
        cur["mod"], cur["mod_r"] = mods[l % 2], mods_r[l % 2]
        cur["coef"], cur["coef_r"] = coefs[l % 2], coefs_r[l % 2]

    def slot_of(t):
        return 0 if t < 2 else 1

    pend = []

    def tick():
        for it in pend:
            it[0] -= 1
        while pend and pend[0][0] <= 0:
            pend.pop(0)[1]()

    def defer(k, fn):
        pend.append([k, fn])

    def flush_deferred():
        while pend:
            pend.pop(0)[1]()

    def norm_begin(kind, l=None, k=None):
        flush_deferred()
        scr.reset()
        ctx = {"kind": kind, "l": l, "k": k, "n": 0}
        ctx["sqb"], ctx["sqb_r"] = [], []
        for n in range(2):
            sqb, r = scr.alloc(8 * 512 * 2, BF16, f"sq{n}")
            ctx["sqb"].append(sqb.rearrange("p (c t) -> p c t", c=8))
            ctx["sqb_r"].append(r)
        ctx["pendB"] = None
        ctx["rstd"], ctx["rstd_r"] = scr.alloc(2048, F32, "rstd")
        ctx["tmp"] = [scr.alloc(2048, F32, f"tmp{i}") for i in range(2)]
        if kind == "final":
            ctx["stg"] = [scr.alloc(2048, F32, f"stg{i}") for i in range(2)]
            ctx["stg_sem"] = [P.new_sem(f"fin_st{i}") for i in range(2)]
            out_sems.extend(ctx["stg_sem"])
        else:
            ctx["coef"], ctx["coef_r"] = coefs[l % 2], coefs_r[l % 2]
            ctx["mod"], ctx["mod_r"] = mods[l % 2], mods_r[l % 2]
        return ctx

    def norm_A(ctx, t):
        sqb = ctx["sqb"][t % 2]
        P.op("act", lambda e: e.activation(out=sqb[:, :, :], in_=xT[:, :, t * 512:(t + 1) * 512],
                                           func=AF.Square),
             reads=[xT_r[c][t] for c in range(8)], writes=[ctx["sqb_r"][t % 2]])

    def norm_B(ctx, t):
        rps, rps_r = norm_B1(ctx, t)
        norm_B2(ctx, t, rps, rps_r)

    def norm_B1(ctx, t):
        sqb, rstd, rstd_r = ctx["sqb"][t % 2], ctx["rstd"], ctx["rstd_r"]
        sqb_r = ctx["sqb_r"][t % 2]
        pi = next_ps()

        def fn(e):
            ins = None
            for kc in range(8):
                ins = e.matmul(ps[pi][:, :], lhsT=ones_b[:, :], rhs=sqb[:, kc, :],
                               start=(kc == 0), stop=(kc == 7))
            return ins
        P.op("pe", fn, reads=[sqb_r, const_r], writes=[ps_r[pi]])
        P.op("act", lambda e: e.activation(out=rstd[:, :], in_=ps[pi][:, :], func=AF.Ln,
                                           bias=V("epsr"), scale=1.0 / D),
             reads=[ps_r[pi], vecs_r], writes=[rstd_r])
        P.op("act", lambda e: e.activation(out=ps[pi][:, :], in_=rstd[:, :], func=AF.Exp, scale=-0.5),
             reads=[rstd_r], writes=[ps_r[pi]])
        return ps[pi], ps_r[pi]

    def norm_B2(ctx, t, rps, rps_r):
        tsl = slice(t * 512, (t + 1) * 512)
        for c in range(8):
            ta, ta_r = ctx["tmp"][ctx["n"] % 2]
            P.op("dve", lambda e, c=c, ta=ta: e.tensor_tensor(
                out=ta[:, :], in0=xT[:, c, tsl], in1=rps[:, :], op=ALU.mult),
                reads=[xT_r[c][t], rps_r], writes=[ta_r])
            if ctx["kind"] == "final":
                sg, sg_r = ctx["stg"][ctx["n"] % 2]
                ssem = ctx["stg_sem"][ctx["n"] % 2]
                P.op("act", lambda e, c=c, ta=ta, sg=sg: e.activation(
                    out=sg[:, :], in_=ta[:, :], func=AF.Identity, scale=V("fg", 1, c)),
                    reads=[ta_r, vecs_r], writes=[sg_r])
                P.dma("sp", lambda e, c=c, sg=sg: e.dma_start(
                    out=yT_d[c * 128:(c + 1) * 128, tsl], in_=sg[:, :]), ssem, reads=[sg_r])
            else:
                k, s_ = ctx["k"], slot_of(t)
                sh0 = 0 if k == 0 else 24
                cf, md = ctx["coef"], ctx["mod"]
                P.op("act", lambda e, c=c, ta=ta: e.activation(
                    out=hT[:, c, tsl], in_=ta[:, :], func=AF.Identity,
                    scale=cf[:, k, c, s_:s_ + 1], bias=md[:, sh0 + c, s_:s_ + 1]),
                    reads=[ta_r, ctx["coef_r"][k], ctx["mod_r"][(sh0 + c) // 4]], writes=[hT_r[t]])
            ctx["n"] += 1

    def norm_tile_done(ctx, lag=3):
        def cb(t):
            norm_A(ctx, t)
            if ctx["pendB"] is not None:
                tp = ctx["pendB"]
                norm_B(ctx, tp)
            ctx["pendB"] = t
            if t == NT - 1:
                defer(lag, lambda: norm_B(ctx, t))
        return cb

    def norm_mod(l, k):
        ctx = norm_begin("mod", l, k)
        for t in range(NT):
            norm_A(ctx, t)
            norm_B(ctx, t)

    def linear_fm(kind, l, npieces, nk, src, src_r, evac, fc_per_piece=4, j0=0, hook=None, pshi=8,
                  last_tile_done=None):
        for j in range(npieces):
            s = use_piece(kind, l, j0 + j)
            for t in range(NT):
                for fc in range(fc_per_piece):
                    pi = next_ps(0, pshi)

                    def fn(e, s=s, fc=fc, t=t, pi=pi):
                        ins = None
                        w = ring[s][:, :].rearrange("p (k f) -> p k f", k=nk)
                        fw = 4096 // nk // fc_per_piece
                        for kc in range(nk):
                            ins = e.matmul(ps[pi][:, :], lhsT=w[:, kc, fc * fw:(fc + 1) * fw],
                                           rhs=src[:, kc, t * 512:(t + 1) * 512],
                                           start=(kc == 0), stop=(kc == nk - 1))
                        return ins
                    P.op("pe", fn, reads=[ring_r[s], src_r[t]], writes=[ps_r[pi]])
                    evac(j, fc, t, pi)
                    tick()
                if last_tile_done is not None and j == npieces - 1:
                    last_tile_done(t)
            done_piece()
            if hook is not None:
                hook()

    def linear_fm_t(kind, l, npieces, src, src_r, evac, tile_done=None):
        slots = [use_piece(kind, l, j) for j in range(npieces)]
        for t in range(NT):
            for j in range(npieces):
                s = slots[j]
                for fc in range(4):
                    pi = next_ps()

                    def fn(e, s=s, fc=fc, t=t, pi=pi):
                        ins = None
                        for kc in range(8):
                            ins = e.matmul(ps[pi][:, :],
                                           lhsT=ring[s][:, kc * 512 + fc * 128:kc * 512 + fc * 128 + 128],
                                           rhs=src[:, kc, t * 512:(t + 1) * 512],
                                           start=(kc == 0), stop=(kc == 7))
                        return ins
                    P.op("pe", fn, reads=[ring_r[s], src_r[t]], writes=[ps_r[pi]])
                    evac(j, fc, t, pi)
                    tick()
            if tile_done is not None:
                tile_done(t)
        for j in range(npieces):
            done_piece()

    def resid_evac(gate_chunk0):
        def evac(j, fc, t, pi, per_piece=4):
            fch = j * per_piece + fc
            s = slot_of(t)
            md = cur["mod"]
            P.op("dve", lambda e: e.scalar_tensor_tensor(
                out=xT[:, fch, t * 512:(t + 1) * 512], in0=ps[pi][:, :],
                scalar=md[:, gate_chunk0 + fch, s:s + 1], in1=xT[:, fch, t * 512:(t + 1) * 512],
                op0=ALU.mult, op1=ALU.add),
                reads=[ps_r[pi], cur["mod_r"][(gate_chunk0 + fch) // 4]], writes=[xT_r[fch][t]])
        return evac

    def mlp(l):
        big.reset()
        hid, _ = big.alloc(16 * TOK * 2, BF16, "hid")
        hid = hid.rearrange("p (c t) -> p c t", c=16)
        hid_r = [Res(f"hid{t}", init=big.inherit) for t in range(NT)]
        big.live.extend(hid_r)
        rt = [big.alloc(2048, F32, f"relu{i}") for i in range(3)]
        cnt = {"n": 0}
        hook, pshi = adaln_piece, 8
        if l + 1 < DEPTH:
            adaln_begin(l + 1)
        for hf in range(2):
            def up_evac(j, fc, t, pi):
                hc = j * 4 + fc
                ta, ta_r = rt[cnt["n"] % 3]
                cnt["n"] += 1
                P.op("act", lambda e: e.activation(out=ta[:, :], in_=ps[pi][:, :], func=AF.Relu),
                     reads=[ps_r[pi]], writes=[ta_r])
                P.op("dve", lambda e: e.tensor_tensor(out=hid[:, hc, t * 512:(t + 1) * 512],
                                                      in0=ta[:, :], in1=ta[:, :], op=ALU.mult),
                     reads=[ta_r], writes=[hid_r[t]])
            linear_fm("up", l, 4, 8, hT, hT_r, up_evac, j0=hf * 4, hook=hook, pshi=pshi)
            g2 = resid_evac(40)

            def down_evac(j, fc, t, pi):
                g2(j, fc, t, pi, per_piece=2)
            ltd = None
            if hf == 1:
                nctx = norm_begin("mod", l + 1, 0) if l + 1 < DEPTH else norm_begin("final")
                ltd = norm_tile_done(nctx, lag=2)
            linear_fm("down", l, 4, 16, hid, hid_r, down_evac, fc_per_piece=2, j0=hf * 4, hook=hook,
                      pshi=pshi, last_tile_done=ltd)

    def final_norm():
        flush_deferred()

    def attn_layer(l):
        i = l // 2
        big.reset()
        qkT, _ = big.alloc(16 * TOK * 2, BF16, "qkT")
        qkT = qkT.rearrange("p (c t) -> p c t", c=16)
        qk_r = [[Res(f"qk{c}_{t}", init=big.inherit) for t in range(NT)] for c in range(16)]
        Vb, _ = big.alloc(12 * 1024 * 2, BF16, "Vb")
        Vb = Vb.rearrange("p (b f) -> p b f", b=12)
        V_r = [Res(f"V{b}", init=big.inherit) for b in range(12)]
        for c in range(16):
            big.live.extend(qk_r[c])
        big.live.extend(V_r)
        gt, gt_r = big.alloc(2048 * 2, BF16, "gtab")
        sem_gt = P.new_sem(f"gt{l}")
        P.dma("pool", lambda e: e.dma_start(out=gt[0:16, :], in_=gt_d), sem_gt, writes=[gt_r])
        P.dma("pool", lambda e: e.dma_start(out=gt[64:80, :], in_=gt_d), sem_gt, writes=[gt_r])

        scr.reset()
        G = []
        for n in range(4):
            ap, r = scr.alloc(2048, F32, f"G{n}")
            G.append((ap, r, scr.last_off))
        zb = [(G[0][0], G[0][1]), (G[1][0], G[1][1])]
        pT = []
        for n in (2, 3):
            bfv = scr.view(G[n][2], 2048, BF16)
            for hh in range(2):
                r = Res(f"pT{n}_{hh}", init=scr.inherit)
                scr.live.append(r)
                pT.append((bfv[:, hh * 512:(hh + 1) * 512], r))
        stg = [(G[0][0], [G[0][1]]), (G[1][0], [G[1][1]]),
               (G[2][0], [pT[0][1], pT[1][1]]), (G[3][0], [pT[2][1], pT[3][1]])]
        stg_sem = [P.new_sem(f"st{l}_{n}") for n in range(4)]
        out_sems.extend(stg_sem)
        cnt = {"n": 0}
        Tb = [scr.alloc(16 * 64 * 2, BF16, f"T{n}") for n in range(4)]
        Hks = []
        for n in range(2):
            hk, hk_r = scr.alloc(17 * 64 * 2, BF16, f"hank{n}")
            Hks.append((hk.rearrange("p (a b) -> p a b", a=17), hk_r, P.new_sem(f"hk{l}_{n}")))
        ckT, ckT_r = scr.alloc(8 * 256 * 2, BF16, "ckT")
        ckT = ckT.rearrange("p (c k) -> p c k", c=8)
        cvb, cvb_r = scr.alloc(2 * 1024 * 2, BF16, "cvb")
        cvb = cvb.rearrange("p (c f) -> p c f", c=2)
        sem_ck = P.new_sem(f"ck{l}")
        sem_cv = P.new_sem(f"cv{l}")
        P.dma("pool", lambda e: e.dma_start(
            out=ckT[:, :, :], in_=ckT_d[i].rearrange("(c p) k -> p c k", p=128)),
            sem_ck, writes=[ckT_r])
        P.dma("pool", lambda e: e.dma_start(
            out=cvb[:, :, :], in_=cv_d[i].rearrange("(c p) f -> p c f", p=128)),
            sem_cv, writes=[cvb_r])

        cn = {"z": 0, "p": 0}

        def hankel_load(h):
            src = bass.AP(rpb_h, i * 7696 + 128 + (h * 15 - 1) * 31 - 48, [[1, 64], [31, 17], [1, 64]])
            Hk, Hk_r, hk_sem = Hks[h % 2]
            P.dma("pool", lambda e: e.dma_start(out=Hk[0:64, :, :], in_=src), hk_sem, writes=[Hk_r])

        def build_T(h, slot, load=True):
            Tt, Tt_r = Tb[slot]
            Tt = Tt.rearrange("p (o q) -> p o q", o=16)
            Hk, Hk_r, hk_sem = Hks[h % 2]
            if load:
                hankel_load(h)
            for half in range(2):
                pi = next_ps(0, 4)

                def fn(e, half=half, pi=pi):
                    ins = None
                    for oo in range(8):
                        a0 = half * 8 + oo
                        ins = e.matmul(ps[pi][:, (7 - oo) * 64:(8 - oo) * 64],
                                       lhsT=Hk[0:64, a0:a0 + 2, :].rearrange("p a b -> p (a b)"),
                                       rhs=Jb[0:64, :], start=True, stop=True)
                    return ins
                P.op("pe", fn, reads=[Hk_r, const_r], writes=[ps_r[pi]])
                P.op("dve", lambda e, half=half, pi=pi: e.tensor_tensor(
                    out=Tt[:, (1 - half) * 8:(2 - half) * 8, :],
                    in0=ps[pi][:, :].rearrange("p (o q) -> p o q", o=8),
                    in1=V("cmask", 64).unsqueeze(1).to_broadcast([128, 8, 64]), op=ALU.add),
                    reads=[ps_r[pi], vecs_r], writes=[Tt_r])
            return Tt, Tt_r

        hankel_load(0)
        hankel_load(1)

        def qk_evac(j, fc, t, pi):
            fch = j * 4 + fc
            if fch < 8:
                P.op("act", lambda e: e.activation(out=qkT[:, fch, t * 512:(t + 1) * 512],
                                                   in_=ps[pi][:, :], func=AF.Copy, scale=0.125),
                     reads=[ps_r[pi]], writes=[qk_r[fch][t]])
            else:
                n = cnt["n"] % 4
                cnt["n"] += 1
                sg, sg_r = stg[n]
                P.op("dve", lambda e: e.tensor_copy(out=sg[:, :], in_=ps[pi][:, :]),
                     reads=[ps_r[pi]], writes=sg_r)
                P.op("act", lambda e: e.activation(out=qkT[:, fch, t * 512:(t + 1) * 512],
                                                   in_=sg[:, :], func=AF.Copy),
                     reads=sg_r, writes=[qk_r[fch][t]])
                kc = fch - 8
                P.dma("sp", lambda e: e.dma_start(
                    out=kT_d[i, kc * 128:(kc + 1) * 128, t * 512:(t + 1) * 512], in_=sg[:, :]),
                    stg_sem[n], reads=sg_r)
        pshi = 8
        linear_fm("qk", l, 4, 8, hT, hT_r, qk_evac, hook=adaln_piece, pshi=pshi)

        T_first = [build_T(0, 0, load=False), build_T(1, 1, load=False)]

        for j in range(2):
            s = use_piece("v", l, j)
            for b in range(12):
                pi = next_ps(0, pshi)
                t = b // 4

                def fn(e, s=s, b=b, pi=pi):
                    ins = None
                    for kc in range(8):
                        ins = e.matmul(ps[pi][:, :], lhsT=hT[:, kc, b * 128:(b + 1) * 128],
                                       rhs=ring[s][:, kc * 512:(kc + 1) * 512],
                                       start=(kc == 0), stop=(kc == 7))
                    return ins
                P.op("pe", fn, reads=[ring_r[s], hT_r[t]], writes=[ps_r[pi]])
                n = cnt["n"] % 4
                cnt["n"] += 1
                sg, sg_r = stg[n]
                P.op("dve", lambda e, sg=sg, pi=pi: e.tensor_copy(out=sg[:, :], in_=ps[pi][:, :]),
                     reads=[ps_r[pi]], writes=sg_r)
                P.op("act", lambda e, sg=sg, b=b, j=j: e.activation(
                    out=Vb[:, b, j * 512:(j + 1) * 512], in_=sg[:, :], func=AF.Copy),
                    reads=sg_r, writes=[V_r[b]])
                P.dma("sp", lambda e, sg=sg, b=b, j=j: e.dma_start(
                    out=vo_d[i, b * 128:(b + 1) * 128, j * 512:(j + 1) * 512], in_=sg[:, :]),
                    stg_sem[n], reads=sg_r)
            done_piece()
            adaln_piece()

        OA = [4, 5]
        SA = [6, 7]

        def finish_pair(o_i, s_i, hp, col0, ncol, tq):
            rc, rc_r = zb[cn["z"] % 2]
            cn["z"] += 1
            P.op("act", lambda e: e.activation(out=rc[:, 0:ncol], in_=ps[s_i][:, 0:ncol], func=AF.Ln),
                 reads=[ps_r[s_i]], writes=[rc_r])
            P.op("act", lambda e: e.activation(out=rc[:, 0:ncol], in_=rc[:, 0:ncol], func=AF.Exp,
                                               scale=-1.0),
                 reads=[rc_r], writes=[rc_r])
            P.op("dve", lambda e: e.tensor_tensor(
                out=hT[:, hp, col0:col0 + ncol], in0=ps[o_i][:, 0:ncol], in1=rc[:, 0:ncol],
                op=ALU.mult),
                reads=[ps_r[o_i], rc_r], writes=[hT_r[tq]])

        def pv_pair(o_i, s_i, args):
            def fn(e):
                ins = None
                for (pb, vsrc, vsrc_r, ptile, ptile_r, c0, n, first) in args:
                    ins = e.matmul(ps[o_i][pb:pb + 64, c0:c0 + n], lhsT=vsrc, rhs=ptile[:, 0:n],
                                   start=first, stop=True, skip_group_check=True)
                for (pb, vsrc, vsrc_r, ptile, ptile_r, c0, n, first) in args:
                    ins = e.matmul(ps[s_i][pb:pb + 64, c0:c0 + n], lhsT=ones_b[:, 0:64],
                                   rhs=ptile[:, 0:n], start=first, stop=True, skip_group_check=True)
                return ins
            reads = [const_r]
            for a in args:
                reads += [a[2], a[4]]
            P.op("pe", fn, reads=reads, writes=[ps_r[o_i], ps_r[s_i]])

        class Blk:
            pass

        def ctx_block(o_i, s_i, hp, qh, par, lc, first):
            b = Blk()
            pb = 64 * par
            h = 2 * hp + par

            def prep():
                b.pi = next_ps(0, 4)
                b.reads = [ckT_r, qk_r[hp][qh]]
            b.prep = prep
            b.mm_s = lambda e: e.matmul(
                ps[b.pi][:, :], lhsT=ckT[pb:pb + 64, hp, lc * 128:(lc + 1) * 128],
                rhs=qkT[pb:pb + 64, hp, qh * 512:(qh + 1) * 512], start=True, stop=True)
            b.mm_g = None

            def post():
                pi = b.pi
                b.pt, b.pt_r = pT[cn["p"] % 4]
                cn["p"] += 1
                pt = b.pt
                P.op("act", lambda e: e.activation(
                    out=pt[:, :], in_=ps[pi][:, :], func=AF.Exp, bias=V("ctxg")),
                    reads=[ps_r[pi], vecs_r], writes=[b.pt_r])
            b.post = post
            b.pvargs = lambda: (pb, cvb[:, lc, h * 64:(h + 1) * 64], cvb_r, b.pt, b.pt_r, 0, 512, first)
            return b

        def own_block(o_i, s_i, hp, qh, par, c, rows, Tt, Tt_r):
            b = Blk()
            pb = 64 * par
            h = 2 * hp + par
            r0, n = rows[0], len(rows) * 64
            c0 = (r0 - qh * 8) * 64
            tk = c // 4

            def prep():
                b.pi = next_ps(0, 4)
                b.reads = [qk_r[8 + hp][tk], qk_r[hp][qh], gt_r]
            b.prep = prep
            b.mm_s = lambda e: e.matmul(
                ps[b.pi][:, 0:n], lhsT=qkT[pb:pb + 64, 8 + hp, c * 128:(c + 1) * 128],
                rhs=qkT[pb:pb + 64, hp, r0 * 64:r0 * 64 + n], start=True, stop=False,
                skip_group_check=True)
            gneed = []
            for rq in rows:
                rs_ = row_start(rq)
                s_ok = all(rs_ <= 2 * c + rl < rs_ + 8 for rl in range(2))
                gneed.append(not (s_ok and (c // 2) == (rq // 4)))
            gidx = [k_ for k_, x_ in enumerate(gneed) if x_]
            if gidx:
                g0, g1 = gidx[0], gidx[-1] + 1
                assert all(gneed[g0:g1])
                b.mm_g = lambda e: e.matmul(
                    ps[b.pi][:, g0 * 64:g1 * 64], lhsT=gt[pb:pb + 16, c * 128:(c + 1) * 128],
                    rhs=gt[pb:pb + 16, 1024 + (r0 + g0) * 64:1024 + (r0 + g1) * 64], start=False, stop=True,
                    skip_group_check=True)
            else:
                b.mm_g = None

            def post():
                pi = b.pi
                z, z_r = zb[cn["z"] % 2]
                cn["z"] += 1
                k = len(rows)
                o0 = 7 - 2 * c + r0
                assert 0 <= o0 and o0 + k <= 16, (c, r0, k)
                P.op("dve", lambda e: e.tensor_tensor(
                    out=z[:, 0:n], in0=ps[pi][:, 0:n],
                    in1=Tt[:, o0:o0 + k, :].rearrange("p o q -> p (o q)"), op=ALU.add),
                    reads=[ps_r[pi], Tt_r], writes=[z_r])
                b.pt, b.pt_r = pT[cn["p"] % 4]
                cn["p"] += 1
                pt = b.pt
                P.op("act", lambda e: e.activation(out=pt[:, 0:n], in_=z[:, 0:n], func=AF.Exp),
                     reads=[z_r], writes=[b.pt_r])
            b.post = post
            b.pvargs = lambda: (pb, Vb[:, c, h * 64:(h + 1) * 64], V_r[c], b.pt, b.pt_r, c0, n, False)
            return b

        def slotb_block(o_i, s_i, hp, par, sq, kc):
            b = Blk()
            pb = 64 * par
            h = 2 * hp + par
            q0 = 1024 + sq * 256
            k0 = q0 + kc * 128
            vb_i = k0 // 128

            def prep():
                b.pi = next_ps(0, 4)
                b.reads = [qk_r[8 + hp][2], qk_r[hp][2]]
            b.prep = prep
            b.mm_s = lambda e: e.matmul(
                ps[b.pi][:, 0:256], lhsT=qkT[pb:pb + 64, 8 + hp, k0:k0 + 128],
                rhs=qkT[pb:pb + 64, hp, q0:q0 + 256], start=True, stop=True)
            b.mm_g = None

            def post():
                pi = b.pi
                b.pt, b.pt_r = pT[cn["p"] % 4]
                cn["p"] += 1
                pt = b.pt
                P.op("act", lambda e: e.activation(
                    out=pt[:, 0:256], in_=ps[pi][:, 0:256], func=AF.Exp),
                    reads=[ps_r[pi]], writes=[b.pt_r])
            b.post = post
            b.pvargs = lambda: (pb, Vb[:, vb_i, h * 64:(h + 1) * 64], V_r[vb_i], b.pt, b.pt_r,
                                sq * 256, 256, kc == 0)
            return b

        def emit_scores(blks):
            for b in blks:
                b.prep()

            def fn(e):
                ins = None
                for b in blks:
                    ins = b.mm_s(e)
                for b in blks:
                    if b.mm_g is not None:
                        ins = b.mm_g(e)
                return ins
            reads = []
            for b in blks:
                reads += b.reads
            P.op("pe", fn, reads=reads, writes=[ps_r[b.pi] for b in blks])
            for b in blks:
                b.post()

        sched = []
        npair = 0
        for hp in range(8):
            sched.append(("preload", hp))
            for qh in range(2):
                if qh == 1:
                    sched.append(("pre", hp))
                o_i, s_i = OA[npair % 2], SA[npair % 2]
                npair += 1
                for lc in range(2):
                    sched.append(("step", o_i, s_i, [("ctx", o_i, s_i, hp, qh, par, lc, lc == 0)
                                                     for par in range(2)]))
                for c in range(8):
                    rows = [r for r in chunk_rows(c) if qh * 8 <= r < qh * 8 + 8]
                    if rows:
                        sched.append(("step", o_i, s_i, [("own", o_i, s_i, hp, qh, par, c, rows)
                                                         for par in range(2)]))
                sched.append(("post", lambda o_i=o_i, s_i=s_i, hp=hp, qh=qh: finish_pair(
                    o_i, s_i, hp, qh * 512, 512, qh)))
            sched.append(("swap", None))
        for hp in range(8):
            o_i, s_i = OA[npair % 2], SA[npair % 2]
            npair += 1
            for sq in range(2):
                for kc in range(2):
                    sched.append(("step", o_i, s_i, [("sb", o_i, s_i, hp, par, sq, kc) for par in range(2)]))
            sched.append(("post", lambda o_i=o_i, s_i=s_i, hp=hp: finish_pair(o_i, s_i, hp, 1024, 512, 2)))

        LAG = 1
        pending = []
        Tstate = {"cur": T_first, "nxt": None}

        def flush_one():
            ent = pending.pop(0)
            if ent[0] == "step":
                _, o_i, s_i, blks = ent
                pv_pair(o_i, s_i, [b.pvargs() for b in blks])
            else:
                ent[1]()

        def nsteps():
            return sum(1 for e_ in pending if e_[0] == "step")

        for ent in sched:
            kind = ent[0]
            if kind == "preload":
                hpn = ent[1]
                if hpn + 1 < 8:
                    hankel_load(2 * hpn + 2)
                    hankel_load(2 * hpn + 3)
            elif kind == "pre":
                hpn = ent[1]
                if hpn + 1 < 8:
                    sl = ((hpn + 1) % 2) * 2
                    Tstate["nxt"] = [build_T(2 * hpn + 2, sl, load=False),
                                     build_T(2 * hpn + 3, sl + 1, load=False)]
            elif kind == "swap":
                if Tstate["nxt"] is not None:
                    Tstate["cur"] = Tstate["nxt"]
                    Tstate["nxt"] = None
            elif kind == "post":
                pending.append(("post", ent[1]))
            else:
                _, o_i, s_i, specs = ent
                blks = []
                for obj in specs:
                    tag = obj[0]
                    if tag == "ctx":
                        b = ctx_block(*obj[1:])
                    elif tag == "own":
                        _, oo, ss, hp, qh, par, c, rows = obj
                        Tt, Tt_r = Tstate["cur"][par]
                        b = own_block(oo, ss, hp, qh, par, c, rows, Tt, Tt_r)
                    else:
                        b = slotb_block(*obj[1:])
                    blks.append(b)
                emit_scores(blks)
                pending.append(("step", o_i, s_i, blks))
                while nsteps() > LAG:
                    flush_one()
                    while pending and pending[0][0] == "post":
                        flush_one()
        while pending:
            flush_one()

        nctx = norm_begin("mod", l, 1)
        linear_fm_t("o", l, 2, hT, hT_r, resid_evac(16), tile_done=norm_tile_done(nctx, lag=3))

    def conv_layer(l):
        i = l // 2
        big.reset()
        up, up_r0 = big.alloc(8 * 6 * SEGP * 2, BF16, "upad")
        up = up.rearrange("p (c s w) -> p c s w", c=8, s=6)
        up_r = [Res(f"up{c}", init=big.inherit) for c in range(8)]
        big.live.extend(up_r)
        vv, _ = big.alloc(8 * TOK * 4, F32, "v")
        vv = vv.rearrange("p (c t) -> p c t", c=8)
        v_r = [[Res(f"v{c}_{t}", init=big.inherit) for t in range(NT)] for c in range(8)]
        for c in range(8):
            big.live.extend(v_r[c])
        scr.reset()
        dg = [scr.alloc(31 * 128 * 2, BF16, f"dg{n}") for n in range(2)]
        sg = [scr.alloc(2048, F32, f"sig{n}") for n in range(2)]
        for c in range(8):
            P.op("pool", lambda e, c=c: e.memset(up[:, c, :, 0:15], 0.0), writes=[up_r[c]])
            P.op("pool", lambda e, c=c: e.memset(up[:, c, :, 271:286], 0.0), writes=[up_r[c]])
        cnt = {"n": 0}
        gbank = {}

        def pw1_evac(j, fc, t, pi):
            if fc < 2:
                gbank[(fc, t)] = pi
                return
            c = 2 * j + (fc - 2)
            pa = gbank[(fc - 2, t)]
            sgt, sgt_r = sg[cnt["n"] % 2]
            cnt["n"] += 1
            P.op("act", lambda e: e.activation(out=sgt[:, :], in_=ps[pi][:, :], func=AF.Sigmoid),
                 reads=[ps_r[pi]], writes=[sgt_r])
            P.op("dve", lambda e: e.tensor_tensor(
                out=up[:, c, 2 * t:2 * t + 2, 15:271],
                in0=ps[pa][:, :].rearrange("p (s w) -> p s w", s=2),
                in1=sgt[:, :].rearrange("p (s w) -> p s w", s=2), op=ALU.mult),
                reads=[ps_r[pa], sgt_r], writes=[up_r[c]])
        sqc = [scr.alloc(2048, BF16, f"sqc{n}") for n in range(2)]
        mean, mean_r = scr.alloc(2048, F32, "mean")
        rstd, rstd_r = scr.alloc(2048, F32, "rstd")
        tmp = sg
        nd = {"n": 0, "q": 0, "t": 0}
        dgB = [Res("dgB0", init=scr.inherit), Res("dgB1", init=scr.inherit)]
        scr.live.extend(dgB)
        diag_q = []
        pending_tail = []
        pending_stats = []
        NPOOL = 22

        def build_diag(c):
            dgt, dgt_r = dg[nd["n"] % 2]
            dgB_r = dgB[nd["n"] % 2]
            nd["n"] += 1
            dgt = dgt.rearrange("p (j m) -> p j m", j=31)
            P.op("pool", lambda e: e.tensor_tensor(
                out=dgt[:, 0:NPOOL, :],
                in0=ident_b[:, :].unsqueeze(1).to_broadcast([128, NPOOL, 128]),
                in1=V("wdw", NPOOL, i * 248 + c * 31).unsqueeze(2).to_broadcast([128, NPOOL, 128]),
                op=ALU.mult),
                reads=[const_r, vecs_r], writes=[dgt_r])
            P.op("dve", lambda e: e.tensor_tensor(
                out=dgt[:, NPOOL:31, :],
                in0=ident_b[:, :].unsqueeze(1).to_broadcast([128, 31 - NPOOL, 128]),
                in1=V("wdw", 31 - NPOOL, i * 248 + c * 31 + NPOOL).unsqueeze(2).to_broadcast(
                    [128, 31 - NPOOL, 128]),
                op=ALU.mult),
                reads=[const_r, vecs_r], writes=[dgB_r])
            diag_q.append((dgt, dgt_r, dgB_r))

        build_diag(0)
        for j in range(4):
            s = use_piece("pw1", l, j)
            for t in range(NT):
                banks = []
                for fc in range(4):
                    pi = next_ps()
                    banks.append(pi)

                    def fn(e, s=s, fc=fc, t=t, pi=pi):
                        ins = None
                        for kc in range(8):
                            ins = e.matmul(ps[pi][:, :],
                                           lhsT=ring[s][:, kc * 512 + fc * 128:kc * 512 + fc * 128 + 128],
                                           rhs=hT[:, kc, t * 512:(t + 1) * 512],
                                           start=(kc == 0), stop=(kc == 7))
                        return ins
                    P.op("pe", fn, reads=[ring_r[s], hT_r[t]], writes=[ps_r[pi]])
                    pw1_evac(j, fc, t, pi)
            done_piece()
        for c in range(8):
            P.op("dve", lambda e, c=c: e.tensor_scalar(
                out=up[:, c, 1:4, 0:15], in0=up[:, c, 0:3, 256:271], scalar1=V("flag"), scalar2=None,
                op0=ALU.mult), reads=[up_r[c], vecs_r], writes=[up_r[c]])
            P.op("dve", lambda e, c=c: e.tensor_scalar(
                out=up[:, c, 0:3, 271:286], in0=up[:, c, 1:4, 15:30], scalar1=V("flag"), scalar2=None,
                op0=ALU.mult), reads=[up_r[c], vecs_r], writes=[up_r[c]])
        for t in range(NT):
            tsl = slice(t * 512, (t + 1) * 512)
            pm, pq = (4, 5) if t % 2 == 0 else (6, 7)

            def stats(c, t=t, tsl=tsl, pm=pm, pq=pq, sq=None, sq_r=None):
                P.op("pe", lambda e: e.matmul(ps[pm][:, :], lhsT=ones_b[:, :], rhs=sq[:, 512:1024],
                                              start=(c == 0), stop=(c == 7)),
                     reads=[sq_r, const_r], writes=[ps_r[pm]])
                P.op("pe", lambda e: e.matmul(ps[pq][:, :], lhsT=ones_b[:, :], rhs=sq[:, 0:512],
                                              start=(c == 0), stop=(c == 7)),
                     reads=[sq_r, const_r], writes=[ps_r[pq]])
            prev = None
            for c in range(8):
                dgt, dgt_r, dgB_r = diag_q.pop(0)
                if not (t == NT - 1 and c == 7):
                    build_diag((c + 1) % 8)
                pi = next_ps(0, 4)

                def fn(e, c=c, t=t, pi=pi, dgt=dgt):
                    ins = None
                    for sgm in range(2):
                        for jt in range(31):
                            ins = e.matmul(ps[pi][:, sgm * 256:(sgm + 1) * 256], lhsT=dgt[:, jt, :],
                                           rhs=up[:, c, 2 * t + sgm, jt:jt + 256],
                                           start=(jt == 0), stop=(jt == 30))
                    return ins
                P.op("pe", fn, reads=[dgt_r, dgB_r, up_r[c]], writes=[ps_r[pi]])
                P.op("act", lambda e, c=c, tsl=tsl, pi=pi: e.activation(
                    out=vv[:, c, tsl], in_=ps[pi][:, :], func=AF.Identity,
                    bias=V("bdw", 1, i * 8 + c)),
                    reads=[ps_r[pi], vecs_r], writes=[v_r[c][t]])
                sq, sq_r = sqc[nd["q"] % 2]
                nd["q"] += 1
                P.op("act", lambda e, c=c, tsl=tsl, sq=sq: e.activation(
                    out=sq[:, 0:512], in_=vv[:, c, tsl], func=AF.Square),
                    reads=[v_r[c][t]], writes=[sq_r])
                P.op("act", lambda e, c=c, tsl=tsl, sq=sq: e.activation(
                    out=sq[:, 512:1024], in_=vv[:, c, tsl], func=AF.Copy),
                    reads=[v_r[c][t]], writes=[sq_r])
                if prev is not None:
                    stats(*prev[:1], sq=prev[1], sq_r=prev[2])
                prev = (c, sq, sq_r)
                if c == 0 and pending_stats:
                    pending_stats.pop(0)()
                if 1 <= c <= 5 and pending_tail:
                    pending_tail.pop(0)()
            def tail_stats(prev=prev, stats=stats):
                stats(*prev[:1], sq=prev[1], sq_r=prev[2])

            def tail(part, t=t, tsl=tsl, pm=pm, pq=pq):
                if part > 0:
                    tail_chunks(part, t, tsl, pm, pq)
                    return
                ta, ta_r = tmp[0]
                P.op("act", lambda e, pm=pm, ta=ta: e.activation(out=ta[:, :], in_=ps[pm][:, :], func=AF.Square,
                                                                 scale=1.0 / D),
                     reads=[ps_r[pm]], writes=[ta_r])
                P.op("act", lambda e, pm=pm: e.activation(out=ps[pm][:, :], in_=ps[pm][:, :], func=AF.Copy,
                                                          scale=1.0 / D),
                     reads=[ps_r[pm]], writes=[ps_r[pm]])
                P.op("dve", lambda e, ta=ta, pq=pq: e.scalar_tensor_tensor(
                    out=rstd[:, :], in0=ps[pq][:, :], scalar=1.0 / D, in1=ta[:, :],
                    op0=ALU.mult, op1=ALU.subtract),
                    reads=[ps_r[pq], ta_r], writes=[rstd_r])
                P.op("act", lambda e: e.activation(out=rstd[:, :], in_=rstd[:, :], func=AF.Ln,
                                                   bias=V("epsl")),
                     reads=[rstd_r, vecs_r], writes=[rstd_r])
                P.op("act", lambda e, pq=pq: e.activation(out=ps[pq][:, :], in_=rstd[:, :], func=AF.Exp,
                                                          scale=-0.5),
                     reads=[rstd_r], writes=[ps_r[pq]])

            def tail_chunks(part, t, tsl, pm, pq):
                for c in range(2 * (part - 1), 2 * part):
                    ta, ta_r = tmp[nd["t"] % 2]
                    nd["t"] += 1
                    P.op("dve", lambda e, c=c, ta=ta, tsl=tsl, pm=pm: e.tensor_tensor(
                        out=ta[:, :], in0=vv[:, c, tsl], in1=ps[pm][:, :], op=ALU.subtract),
                        reads=[v_r[c][t], ps_r[pm]], writes=[ta_r])
                    P.op("dve", lambda e, ta=ta, pq=pq: e.tensor_tensor(
                        out=ta[:, :], in0=ta[:, :], in1=ps[pq][:, :], op=ALU.mult),
                        reads=[ta_r, ps_r[pq]], writes=[ta_r])
                    P.op("act", lambda e, c=c, ta=ta, tsl=tsl: e.activation(
                        out=hT[:, c, tsl], in_=ta[:, :], func=AF.Silu,
                        scale=V("lng", 1, i * 8 + c), bias=V("lnb", 1, i * 8 + c)),
                        reads=[ta_r, vecs_r], writes=[hT_r[t]])
            pending_stats.append(tail_stats)
            for part in range(5):
                pending_tail.append(lambda part=part, tail=tail: tail(part))
        pending_stats.pop(0)()
        while pending_tail:
            pending_tail.pop(0)()
        nctx = norm_begin("mod", l, 1)
        st["psi"] = 6
        linear_fm_t("pw2", l, 2, hT, hT_r, resid_evac(16), tile_done=norm_tile_done(nctx, lag=3))

    del PLAN[:]
    ctx0 = norm_begin("mod", 0, 0)
    rp0 = []
    for t in range(NT):
        norm_A(ctx0, t)
        rp0.append(norm_B1(ctx0, t))
    adaln_begin(0)
    for _ in range(4):
        adaln_piece()
    for t in range(NT):
        norm_B2(ctx0, t, *rp0[t])
    for l in range(DEPTH):
        set_layer(l)
        flush_deferred()
        if l % 2 == 0:
            attn_layer(l)
        else:
            conv_layer(l)
        mlp(l)
    final_norm()
    assert st["next_use"] == NPIECE, (st, NPIECE)
    sp = P.eng["sp"]
    for s in out_sems:
        if s.count:
            sp.need(s, s.count)
    P.emit()
    return nc


_CACHE = {}


def kernel(**inp):
    inp = {k: np.asarray(v) for k, v in inp.items()}
    if "nc" not in _CACHE:
        _CACHE["nc"] = build_program()
    nc = _CACHE["nc"]
    wall = build_wall(inp)
    xp = inp["x_prompt"].astype(np.float32, copy=False)
    xs = inp["x_sample"].astype(np.float32, copy=False)
    roles = []
    for core in range(8):
        if core < 4:
            roles.append((core, [2 * core, 2 * core + 1], None))
        else:
            base = 8 + (core - 4) * 6
            roles.append((None, [base + 4, base + 5], [base, base + 1, base + 2, base + 3]))
    in_maps = []
    zeros_ck = np.zeros((2, D, 256), np.float32)
    zeros_cv = np.zeros((2, 256, D), np.float32)
    zeros_rp = np.zeros((2, 7696), np.float32)
    for core in range(8):
        sb, pB, pA = roles[core]
        if sb is not None:
            xa = xs[sb]
        else:
            xa = np.concatenate([xp[p] for p in pA], axis=0)
        xb = np.concatenate([xp[p] for p in pB], axis=0)
        x = np.concatenate([xa, xb], axis=0)
        m = {"xT": np.ascontiguousarray(x.T), "wall": wall, "vecs": build_vecs(inp, sb),
             "gtab": build_gtab(sb is not None)}
        if sb is not None:
            ck = inp["cache_k"][sb].reshape(2, 256, D)
            m["ckT"] = np.ascontiguousarray(ck.transpose(0, 2, 1)).astype(np.float32)
            m["cv"] = np.ascontiguousarray(inp["cache_v"][sb].reshape(2, 256, D)).astype(np.float32)
            rp = np.zeros((2, 7696), np.float32)
            rp[:, 128:128 + 7440] = inp["rpb"].reshape(2, 7440)
            m["rpbp"] = rp
        else:
            m["ckT"], m["cv"], m["rpbp"] = zeros_ck, zeros_cv, zeros_rp
        in_maps.append(m)
    res = run_bass_kernel_spmd(nc, in_maps, core_ids=list(range(8)))
    outs = res.results
    y_prompt = np.empty((32, 256, D), np.float32)
    y_sample = np.empty((4, 1024, D), np.float32)
    nk = np.empty((32, 2, 256, 16, 64), np.float32)
    nv = np.empty((32, 2, 256, 16, 64), np.float32)
    for core in range(8):
        sb, pB, pA = roles[core]
        y = outs[core]["yT"].T
        kt = outs[core]["kTo"].transpose(0, 2, 1)
        vo = outs[core]["vo"]
        if sb is not None:
            y_sample[sb] = y[0:1024]
        else:
            for n, p in enumerate(pA):
                y_prompt[p] = y[n * 256:(n + 1) * 256]
                nk[p] = kt[:, n * 256:(n + 1) * 256].reshape(2, 256, 16, 64)
                nv[p] = vo[:, n * 256:(n + 1) * 256].reshape(2, 256, 16, 64)
        for n, p in enumerate(pB):
            y_prompt[p] = y[1024 + n * 256:1024 + (n + 1) * 256]
            nk[p] = kt[:, 1024 + n * 256:1024 + (n + 1) * 256].reshape(2, 256, 16, 64)
            nv[p] = vo[:, 1024 + n * 256:1024 + (n + 1) * 256].reshape(2, 256, 16, 64)
    return (y_prompt, y_sample, nk, nv)
```

```python
import numpy as np
import concourse.bass as bass
import concourse.mybir as mybir
from concourse.bass_utils import run_bass_kernel_spmd

F32 = mybir.dt.float32
F32R = mybir.dt.float32r
BF16 = mybir.dt.bfloat16
U8 = mybir.dt.uint8
AF = mybir.ActivationFunctionType
ALU = mybir.AluOpType

D = 1024
TOK = 1536
NT = 3
DEPTH = 4
NEG = -30000.0
SEGP = 286


def piece_plan():
    plan = []
    for l in range(DEPTH):
        for j in range(12):
            plan.append(("ada", l, j))
        if l % 2 == 0:
            for j in range(4):
                plan.append(("qk", l, j))
            for j in range(2):
                plan.append(("v", l, j))
            for j in range(2):
                plan.append(("o", l, j))
        else:
            for j in range(4):
                plan.append(("pw1", l, j))
            for j in range(2):
                plan.append(("pw2", l, j))
        for hf in range(2):
            for j in range(4):
                plan.append(("up", l, hf * 4 + j))
            for j in range(4):
                plan.append(("down", l, hf * 4 + j))
    return plan


NPIECE = len(piece_plan())
PLAN = []


def tile_kf(w, f0, nf):
    return np.ascontiguousarray(
        w[:, f0:f0 + nf].reshape(8, 128, nf).transpose(1, 0, 2)).reshape(128, 8 * nf)


def build_wall(inp):
    wall = np.empty((NPIECE, 128, 4096), np.float32)
    for n, (kind, l, j) in enumerate(PLAN):
        i = l // 2
        if kind == "ada":
            wall[n] = tile_kf(inp["w_ada"][l], j * 512, 512)
        elif kind == "qk":
            wall[n] = tile_kf(inp["w_qkv"][i], j * 512, 512)
        elif kind == "v":
            wall[n] = tile_kf(inp["w_qkv"][i], 2048 + j * 512, 512)
        elif kind == "o":
            wall[n] = tile_kf(inp["w_o"][i], j * 512, 512)
        elif kind == "pw1":
            w = inp["w_pw1"][i]
            cols = np.concatenate([np.arange(256 * j, 256 * j + 256),
                                   1024 + np.arange(256 * j, 256 * j + 256)])
            wall[n] = tile_kf(w[:, cols], 0, 512)
        elif kind == "pw2":
            wall[n] = tile_kf(inp["w_pw2"][i], j * 512, 512)
        elif kind == "up":
            wall[n] = tile_kf(inp["w_up"][l], j * 512, 512)
        elif kind == "down":
            hf, jj = divmod(j, 4)
            w = inp["w_down"][l][hf * 2048:(hf + 1) * 2048, jj * 256:(jj + 1) * 256]
            wall[n] = np.ascontiguousarray(
                w.reshape(16, 128, 256).transpose(1, 0, 2)).reshape(128, 4096)
    return wall


VOFF = {}
_nv = 0
for _name, _n in [("cond", 16), ("bada", 192), ("ng", 64), ("fg", 8), ("bdw", 16), ("lng", 16),
                  ("lnb", 16), ("wdw", 496), ("gate", 128), ("ctxg", 1), ("flag", 1),
                  ("cmask", 64), ("epsr", 1), ("epsl", 1), ("J", 64), ("ident", 128)]:
    VOFF[_name] = _nv
    _nv += _n
NV = _nv


def fm(v):
    return np.ascontiguousarray(np.asarray(v, np.float32).reshape(8, 128).T)


def row_start(r):
    return int(np.clip(r - 4, 0, 8))


def chunk_rows(c):
    out = []
    for rq in range(16):
        rs = row_start(rq)
        if rs <= 2 * c + 1 and rs + 7 >= 2 * c:
            out.append(rq)
    return out


def build_gtab(is_s):
    g = np.zeros((16, 2048), np.float32)
    for rq in range(16):
        for c in range(8):
            for rl in range(2):
                rk = 2 * c + rl
                if is_s:
                    rs = row_start(rq)
                    ok = rs <= rk < rs + 8
                else:
                    ok = (rk // 4) == (rq // 4)
                g[rq, c * 128 + rl * 64:c * 128 + (rl + 1) * 64] = 0.0 if ok else NEG
        g[rq, 1024 + rq * 64:1024 + (rq + 1) * 64] = 1.0
    return g


def build_vecs(inp, sample_b):
    v = np.zeros((128, NV), np.float32)
    is_s = sample_b is not None
    condA = inp["c"][sample_b] if is_s else inp["c_ctx"]
    cd = np.stack([fm(condA), fm(inp["c_ctx"])], axis=-1)
    v[:, VOFF["cond"]:VOFF["cond"] + 16] = cd.reshape(128, 16)
    for l in range(DEPTH):
        b = np.asarray(inp["b_ada"][l], np.float32).reshape(48, 128).T
        v[:, VOFF["bada"] + l * 48:VOFF["bada"] + (l + 1) * 48] = b
        for s in range(2):
            o = VOFF["ng"] + (l * 2 + s) * 8
            v[:, o:o + 8] = fm(inp["norm_g"][l, s])
    v[:, VOFF["fg"]:VOFF["fg"] + 8] = fm(inp["final_g"])
    for i in range(2):
        v[:, VOFF["bdw"] + i * 8:VOFF["bdw"] + (i + 1) * 8] = fm(inp["b_dw"][i])
        v[:, VOFF["lng"] + i * 8:VOFF["lng"] + (i + 1) * 8] = fm(inp["conv_ln_g"][i])
        v[:, VOFF["lnb"] + i * 8:VOFF["lnb"] + (i + 1) * 8] = fm(inp["conv_ln_b"][i])
        w = np.asarray(inp["w_dw"][i], np.float32)
        wf = w.reshape(31, 8, 128).transpose(2, 1, 0)
        v[:, VOFF["wdw"] + i * 248:VOFF["wdw"] + (i + 1) * 248] = wf.reshape(128, 248)
    g = np.zeros((128, 8, 16), np.float32)
    for c in range(8):
        for rq in range(16):
            for rl in range(2):
                rk = 2 * c + rl
                if is_s:
                    rs = row_start(rq)
                    ok = rs <= rk < rs + 8
                else:
                    ok = (rk // 4) == (rq // 4)
                g[rl * 64:(rl + 1) * 64, c, rq] = 0.0 if ok else NEG
    v[:, VOFF["gate"]:VOFF["gate"] + 128] = g.reshape(128, 128)
    v[:, VOFF["ctxg"]] = 0.0 if is_s else NEG
    v[:, VOFF["flag"]] = 1.0 if is_s else 0.0
    cm = np.zeros((64, 64), np.float32)
    if is_s:
        for cq in range(64):
            ws = int(np.clip(cq - 8, 0, 48))
            for ck in range(64):
                if not (ws <= ck < ws + 16):
                    cm[ck, cq] = NEG
    v[0:64, VOFF["cmask"]:VOFF["cmask"] + 64] = cm
    v[64:128, VOFF["cmask"]:VOFF["cmask"] + 64] = cm
    v[:, VOFF["epsr"]] = 1e-6
    v[:, VOFF["epsl"]] = 1e-5
    v[0:64, VOFF["J"]:VOFF["J"] + 64] = np.eye(64, dtype=np.float32)[::-1]
    v[:, VOFF["ident"]:VOFF["ident"] + 128] = np.eye(128, dtype=np.float32)
    return v


class Sem:
    def __init__(self, h, name):
        self.h = h
        self.name = name
        self.count = 0


class Res:
    __slots__ = ("name", "w", "r")

    def __init__(self, name, init=None):
        self.name = name
        self.w = None
        self.r = dict(init) if init else {}

    def events(self):
        ev = dict(self.r)
        if self.w is not None:
            s, v = self.w
            if ev.get(s, 0) < v:
                ev[s] = v
        return ev


class Eng:
    def __init__(self, name, sem):
        self.name = name
        self.sem = sem
        self.items = []
        self.waited = {}

    def need(self, sem, val):
        if self.waited.get(sem, 0) >= val:
            return
        self.waited[sem] = val
        self.items.append(("wait", sem, val))


class Prog:
    def __init__(self, nc):
        self.nc = nc
        self.sems = []
        self.eng = {}

    def new_sem(self, name):
        s = Sem(None, name)
        self.sems.append(s)
        return s

    def add_engine(self, name):
        self.eng[name] = Eng(name, self.new_sem("e_" + name))

    def _deps(self, eng, reads, writes, extra):
        for r in reads:
            if r.w is not None:
                s, v = r.w
                if s is eng.sem and eng.name == "pe":
                    continue
                eng.need(s, v)
        for w in writes:
            if w.w is not None:
                s, v = w.w
                if not (s is eng.sem and eng.name == "pe"):
                    eng.need(s, v)
            for s, v in w.r.items():
                if s is eng.sem and eng.name == "pe":
                    continue
                eng.need(s, v)
        for s, v in extra:
            if s is eng.sem:
                continue
            eng.need(s, v)

    def op(self, ename, fn, reads=(), writes=(), extra=()):
        eng = self.eng[ename]
        self._deps(eng, reads, writes, extra)
        eng.sem.count += 1
        ev = (eng.sem, eng.sem.count)
        eng.items.append(("op", fn, eng.sem, 1))
        for r in reads:
            if r.r.get(ev[0], 0) < ev[1]:
                r.r[ev[0]] = ev[1]
        for w in writes:
            w.w = ev
            w.r = {}
        return ev

    def dma(self, qname, fn, sem, reads=(), writes=(), extra=()):
        eng = self.eng[qname]
        self._deps(eng, reads, writes, extra)
        sem.count += 16
        ev = (sem, sem.count)
        eng.items.append(("op", fn, sem, 16))
        for r in reads:
            if r.r.get(sem, 0) < ev[1]:
                r.r[sem] = ev[1]
        for w in writes:
            w.w = ev
            w.r = {}
        return ev

    def emit(self):
        nc = self.nc
        from contextlib import ExitStack
        with ExitStack() as st:
            for s in self.sems:
                s.h = st.enter_context(nc.semaphore(s.name))
            block = st.enter_context(nc.Block())

            def runner(eng):
                def body(e):
                    for it in eng.items:
                        if it[0] == "wait":
                            e.wait_ge(it[1].h, it[2])
                        else:
                            ins = it[1](e)
                            ins.then_inc(it[2].h, it[3])
                return body

            block.tensor(runner(self.eng["pe"]))
            block.scalar(runner(self.eng["act"]))
            block.vector(runner(self.eng["dve"]))
            block.gpsimd(runner(self.eng["pool"]))
            block.sync(runner(self.eng["sp"]))


class Region:
    def __init__(self, nc, name, nbytes):
        self.t = nc.alloc_sbuf_tensor(name, [128, nbytes], U8)
        self.nbytes = nbytes
        self.name = name
        self.live = []
        self.inherit = {}
        self.off = 0
        self.n = 0

    def reset(self):
        for r in self.live:
            for s, v in r.events().items():
                if self.inherit.get(s, 0) < v:
                    self.inherit[s] = v
        self.live = []
        self.off = 0

    def view(self, off, nbytes, dtype):
        return self.t[:, off:off + nbytes].bitcast(dtype)

    def alloc(self, nbytes, dtype, name=None):
        nbytes = (nbytes + 31) // 32 * 32
        assert self.off + nbytes <= self.nbytes, (self.name, name, self.off, nbytes, self.nbytes)
        ap = self.t[:, self.off:self.off + nbytes].bitcast(dtype)
        self.last_off = self.off
        self.off += nbytes
        self.n += 1
        r = Res(f"{self.name}_{name}_{self.n}", init=self.inherit)
        self.live.append(r)
        return ap, r


def build_program():
    nc = bass.Bass("TRN2", target_bir_lowering=False)
    xT_d = nc.dram_tensor("xT", [D, TOK], F32, kind="ExternalInput").ap()
    wall_d = nc.dram_tensor("wall", [NPIECE, 128, 4096], F32, kind="ExternalInput").ap()
    vec_d = nc.dram_tensor("vecs", [128, NV], F32, kind="ExternalInput").ap()
    rpb_h = nc.dram_tensor("rpbp", [2, 7696], F32, kind="ExternalInput")
    ckT_d = nc.dram_tensor("ckT", [2, D, 256], F32, kind="ExternalInput").ap()
    cv_d = nc.dram_tensor("cv", [2, 256, D], F32, kind="ExternalInput").ap()
    gt_d = nc.dram_tensor("gtab", [16, 2048], F32, kind="ExternalInput").ap()
    yT_d = nc.dram_tensor("yT", [D, TOK], F32, kind="ExternalOutput").ap()
    kT_d = nc.dram_tensor("kTo", [2, D, TOK], F32, kind="ExternalOutput").ap()
    vo_d = nc.dram_tensor("vo", [2, TOK, D], F32, kind="ExternalOutput").ap()

    P = Prog(nc)
    for e in ("pe", "act", "dve", "pool", "sp"):
        P.add_engine(e)

    xT = nc.alloc_sbuf_tensor("xTs", [128, 8, TOK], F32)
    xT_r = [[Res(f"x{c}_{t}") for t in range(NT)] for c in range(8)]
    hT = nc.alloc_sbuf_tensor("hTs", [128, 8, TOK], BF16)
    hT_r = [Res(f"h{t}") for t in range(NT)]
    vecs = nc.alloc_sbuf_tensor("vecs_s", [128, NV], F32)
    vecs_r = Res("vecs")
    mods = [nc.alloc_sbuf_tensor(f"mod{n}", [128, 48, 2], F32) for n in range(2)]
    mods_r = [[Res(f"mod{n}_{j}") for j in range(12)] for n in range(2)]
    coefs = [nc.alloc_sbuf_tensor(f"coef{n}", [128, 2, 8, 2], F32) for n in range(2)]
    coefs_r = [[Res(f"coef{n}_{k}") for k in range(2)] for n in range(2)]
    cur = {"mod": mods[0], "mod_r": mods_r[0], "coef": coefs[0], "coef_r": coefs_r[0]}
    scond = nc.alloc_sbuf_tensor("scond", [128, 8, 2], BF16)
    scond_r = Res("scond")
    ident_b = nc.alloc_sbuf_tensor("identb", [128, 128], BF16)
    ones_b = nc.alloc_sbuf_tensor("onesb", [128, 128], BF16)
    ones_f = nc.alloc_sbuf_tensor("onesf", [128, 128], F32)
    Jb = nc.alloc_sbuf_tensor("Jb", [128, 64], BF16)
    const_r = Res("const")
    NSLOT = 3
    ring = [nc.alloc_sbuf_tensor(f"ring{i}", [128, 4096], BF16) for i in range(NSLOT)]
    ring_r = [Res(f"ring{i}") for i in range(NSLOT)]
    ring_sem = [P.new_sem(f"ringsem{i}") for i in range(NSLOT)]
    big = Region(nc, "big", 77824)
    scr = Region(nc, "scr", 29184)
    ps = [nc.alloc_psum_tensor(f"ps{i}", [128, 512], F32) for i in range(8)]
    ps_r = [Res(f"ps{i}") for i in range(8)]

    def V(name, n=1, j=0):
        o = VOFF[name] + j
        return vecs[:, o:o + n]

    st = {"next_load": 0, "next_use": 0, "psi": 0}

    def load_piece():
        n = st["next_load"]
        if n >= NPIECE:
            return
        st["next_load"] += 1
        s = n % NSLOT
        P.dma("pool", lambda e, n=n, s=s: e.dma_start(out=ring[s][:, :], in_=wall_d[n],
                                                      max_dma_last_dim=8192),
              ring_sem[s], writes=[ring_r[s]])

    def use_piece(kind, l, j):
        n = st["next_use"]
        PLAN.append((kind, l, j))
        st["next_use"] += 1
        return n % NSLOT

    def done_piece():
        load_piece()

    for _ in range(NSLOT):
        load_piece()

    def next_ps(lo=0, hi=8):
        i = st["psi"]
        if i < lo or i >= hi:
            i = lo
        st["psi"] = i + 1
        return i

    sem_v = P.new_sem("ld_vecs")
    P.dma("sp", lambda e: e.dma_start(out=vecs[:, :], in_=vec_d), sem_v, writes=[vecs_r])
    sem_x = [P.new_sem(f"ld_x{t}") for t in range(NT)]
    for t in range(NT):
        P.dma("sp", lambda e, t=t: e.dma_start(
            out=xT[:, :, t * 512:(t + 1) * 512],
            in_=xT_d.rearrange("(c p) t -> p c t", p=128)[:, :, t * 512:(t + 1) * 512]),
            sem_x[t], writes=[xT_r[c][t] for c in range(8)])

    P.op("dve", lambda e: e.tensor_copy(out=ident_b[:, :], in_=V("ident", 128)),
         reads=[vecs_r], writes=[const_r])
    P.op("dve", lambda e: e.tensor_copy(out=Jb[:, :], in_=V("J", 64)), reads=[vecs_r], writes=[const_r])
    P.op("pool", lambda e: e.memset(ones_b[:, :], 1.0), writes=[const_r])
    P.op("pool", lambda e: e.memset(ones_f[:, :], 1.0), writes=[const_r])
    P.op("act", lambda e: e.activation(out=scond[:, :, :].rearrange("p c s -> p (c s)"),
                                       in_=V("cond", 16), func=AF.Silu),
         reads=[vecs_r], writes=[scond_r])

    out_sems = []

    ada = {}

    ada_q = []

    def adaln_begin(l):
        ada_q.extend((l, j) for j in range(12))

    def adaln_piece():
        if not ada_q:
            return
        l, j = ada_q.pop(0)
        pi = next_ps(0, 4) if st.get("in_attn") else next_ps()
        s = use_piece("ada", l, j)

        def fn(e):
            ins = None
            for fc in range(4):
                col = fc * 2
                for kc in range(8):
                    ins = e.matmul(ps[pi][:, col:col + 2],
                                   lhsT=ring[s][:, kc * 512 + fc * 128:kc * 512 + fc * 128 + 128],
                                   rhs=scond[:, kc, :], start=(kc == 0), stop=(kc == 7))
            return ins
        P.op("pe", fn, reads=[ring_r[s], scond_r], writes=[ps_r[pi]])
        done_piece()
        mod, mod_r = mods[l % 2], mods_r[l % 2]
        coef, coef_r = coefs[l % 2], coefs_r[l % 2]
        P.op("dve", lambda e: e.tensor_tensor(
            out=mod[:, j * 4:(j + 1) * 4, :],
            in0=ps[pi][:, 0:8].rearrange("p (j s) -> p j s", s=2),
            in1=V("bada", 4, l * 48 + j * 4).unsqueeze(2).to_broadcast([128, 4, 2]), op=ALU.add),
            reads=[ps_r[pi], vecs_r], writes=[mod_r[j]])
        for k, jj in ((0, 3), (1, 9)):
            if j == jj:
                P.op("dve", lambda e, k=k: e.scalar_tensor_tensor(
                    out=coef[:, k, :, :], in0=mod[:, 8 + 24 * k:16 + 24 * k, :], scalar=1.0,
                    in1=V("ng", 8, (l * 2 + k) * 8).unsqueeze(2).to_broadcast([128, 8, 2]),
                    op0=ALU.add, op1=ALU.mult),
                    reads=[mod_r[jj - 1], mod_r[jj], vecs_r], writes=[coef_r[k]])

    def set_layer(l):
        cur["mod"], cur["mod_r"] = mods[l % 2], mods_r[l % 2]
        cur["coef"], cur["coef_r"] = coefs[l % 2], coefs_r[l % 2]

    def slot_of(t):
        return 0 if t < 2 else 1

    pend = []

    def tick():
        for it in pend:
            it[0] -= 1
        while pend and pend[0][0] <= 0:
            pend.pop(0)[1]()

    def defer(k, fn):
        pend.append([k, fn])

    def flush_deferred():
        while pend:
            pend.pop(0)[1]()

    def norm_begin(kind, l=None, k=None):
        flush_deferred()
        scr.reset()
        ctx = {"kind": kind, "l": l, "k": k, "n": 0}
        ctx["sqb"], ctx["sqb_r"] = [], []
        for n in range(2):
            sqb, r = scr.alloc(8 * 512 * 2, BF16, f"sq{n}")
            ctx["sqb"].append(sqb.rearrange("p (c t) -> p c t", c=8))
            ctx["sqb_r"].append(r)
        ctx["pendB"] = None
        ctx["rstd"], ctx["rstd_r"] = scr.alloc(2048, F32, "rstd")
        ctx["tmp"] = [scr.alloc(2048, F32, f"tmp{i}") for i in range(2)]
        if kind == "final":
            ctx["stg"] = [scr.alloc(2048, F32, f"stg{i}") for i in range(2)]
            ctx["stg_sem"] = [P.new_sem(f"fin_st{i}") for i in range(2)]
            out_sems.extend(ctx["stg_sem"])
        else:
            ctx["coef"], ctx["coef_r"] = coefs[l % 2], coefs_r[l % 2]
            ctx["mod"], ctx["mod_r"] = mods[l % 2], mods_r[l % 2]
        return ctx

    def norm_A(ctx, t):
        sqb = ctx["sqb"][t % 2]
        P.op("act", lambda e: e.activation(out=sqb[:, :, :], in_=xT[:, :, t * 512:(t + 1) * 512],
                                           func=AF.Square),
             reads=[xT_r[c][t] for c in range(8)], writes=[ctx["sqb_r"][t % 2]])

    def norm_B(ctx, t):
        rps, rps_r = norm_B1(ctx, t)
        norm_B2(ctx, t, rps, rps_r)

    def norm_B1(ctx, t):
        sqb, rstd, rstd_r = ctx["sqb"][t % 2], ctx["rstd"], ctx["rstd_r"]
        sqb_r = ctx["sqb_r"][t % 2]
        pi = next_ps()

        def fn(e):
            ins = None
            for kc in range(8):
                ins = e.matmul(ps[pi][:, :], lhsT=ones_b[:, :], rhs=sqb[:, kc, :],
                               start=(kc == 0), stop=(kc == 7))
            return ins
        P.op("pe", fn, reads=[sqb_r, const_r], writes=[ps_r[pi]])
        P.op("act", lambda e: e.activation(out=rstd[:, :], in_=ps[pi][:, :], func=AF.Ln,
                                           bias=V("epsr"), scale=1.0 / D),
             reads=[ps_r[pi], vecs_r], writes=[rstd_r])
        P.op("act", lambda e: e.activation(out=ps[pi][:, :], in_=rstd[:, :], func=AF.Exp, scale=-0.5),
             reads=[rstd_r], writes=[ps_r[pi]])
        return ps[pi], ps_r[pi]

    def norm_B2(ctx, t, rps, rps_r):
        tsl = slice(t * 512, (t + 1) * 512)
        for c in range(8):
            ta, ta_r = ctx["tmp"][ctx["n"] % 2]
            P.op("dve", lambda e, c=c, ta=ta: e.tensor_tensor(
                out=ta[:, :], in0=xT[:, c, tsl], in1=rps[:, :], op=ALU.mult),
                reads=[xT_r[c][t], rps_r], writes=[ta_r])
            if ctx["kind"] == "final":
                sg, sg_r = ctx["stg"][ctx["n"] % 2]
                ssem = ctx["stg_sem"][ctx["n"] % 2]
                P.op("act", lambda e, c=c, ta=ta, sg=sg: e.activation(
                    out=sg[:, :], in_=ta[:, :], func=AF.Identity, scale=V("fg", 1, c)),
                    reads=[ta_r, vecs_r], writes=[sg_r])
                P.dma("sp", lambda e, c=c, sg=sg: e.dma_start(
                    out=yT_d[c * 128:(c + 1) * 128, tsl], in_=sg[:, :]), ssem, reads=[sg_r])
            else:
                k, s_ = ctx["k"], slot_of(t)
                sh0 = 0 if k == 0 else 24
                cf, md = ctx["coef"], ctx["mod"]
                P.op("act", lambda e, c=c, ta=ta: e.activation(
                    out=hT[:, c, tsl], in_=ta[:, :], func=AF.Identity,
                    scale=cf[:, k, c, s_:s_ + 1], bias=md[:, sh0 + c, s_:s_ + 1]),
                    reads=[ta_r, ctx["coef_r"][k], ctx["mod_r"][(sh0 + c) // 4]], writes=[hT_r[t]])
            ctx["n"] += 1

    def norm_tile_done(ctx, lag=3):
        def cb(t):
            norm_A(ctx, t)
            if ctx["pendB"] is not None:
                tp = ctx["pendB"]
                norm_B(ctx, tp)
            ctx["pendB"] = t
            if t == NT - 1:
                defer(lag, lambda: norm_B(ctx, t))
        return cb

    def norm_mod(l, k):
        ctx = norm_begin("mod", l, k)
        for t in range(NT):
            norm_A(ctx, t)
            norm_B(ctx, t)

    def linear_fm(kind, l, npieces, nk, src, src_r, evac, fc_per_piece=4, j0=0, hook=None, pshi=8,
                  last_tile_done=None):
        for j in range(npieces):
            s = use_piece(kind, l, j0 + j)
            for t in range(NT):
                for fc in range(fc_per_piece):
                    pi = next_ps(0, pshi)

                    def fn(e, s=s, fc=fc, t=t, pi=pi):
                        ins = None
                        w = ring[s][:, :].rearrange("p (k f) -> p k f", k=nk)
                        fw = 4096 // nk // fc_per_piece
                        for kc in range(nk):
                            ins = e.matmul(ps[pi][:, :], lhsT=w[:, kc, fc * fw:(fc + 1) * fw],
                                           rhs=src[:, kc, t * 512:(t + 1) * 512],
                                           start=(kc == 0), stop=(kc == nk - 1))
                        return ins
                    P.op("pe", fn, reads=[ring_r[s], src_r[t]], writes=[ps_r[pi]])
                    evac(j, fc, t, pi)
                    tick()
                if last_tile_done is not None and j == npieces - 1:
                    last_tile_done(t)
            done_piece()
            if hook is not None:
                hook()

    def linear_fm_t(kind, l, npieces, src, src_r, evac, tile_done=None):
        slots = [use_piece(kind, l, j) for j in range(npieces)]
        for t in range(NT):
            for j in range(npieces):
                s = slots[j]
                for fc in range(4):
                    pi = next_ps()

                    def fn(e, s=s, fc=fc, t=t, pi=pi):
                        ins = None
                        for kc in range(8):
                            ins = e.matmul(ps[pi][:, :],
                                           lhsT=ring[s][:, kc * 512 + fc * 128:kc * 512 + fc * 128 + 128],
                                           rhs=src[:, kc, t * 512:(t + 1) * 512],
                                           start=(kc == 0), stop=(kc == 7))
                        return ins
                    P.op("pe", fn, reads=[ring_r[s], src_r[t]], writes=[ps_r[pi]])
                    evac(j, fc, t, pi)
                    tick()
            if tile_done is not None:
                tile_done(t)
        for j in range(npieces):
            done_piece()

    def resid_evac(gate_chunk0):
        def evac(j, fc, t, pi, per_piece=4):
            fch = j * per_piece + fc
            s = slot_of(t)
            md = cur["mod"]
            P.op("dve", lambda e: e.scalar_tensor_tensor(
                out=xT[:, fch, t * 512:(t + 1) * 512], in0=ps[pi][:, :],
                scalar=md[:, gate_chunk0 + fch, s:s + 1], in1=xT[:, fch, t * 512:(t + 1) * 512],
                op0=ALU.mult, op1=ALU.add),
                reads=[ps_r[pi], cur["mod_r"][(gate_chunk0 + fch) // 4]], writes=[xT_r[fch][t]])
        return evac

    def mlp(l):
        big.reset()
        hid, _ = big.alloc(16 * TOK * 2, BF16, "hid")
        hid = hid.rearrange("p (c t) -> p c t", c=16)
        hid_r = [Res(f"hid{t}", init=big.inherit) for t in range(NT)]
        big.live.extend(hid_r)
        rt = [big.alloc(2048, F32, f"relu{i}") for i in range(3)]
        cnt = {"n": 0}
        hook, pshi = adaln_piece, 8
        if l + 1 < DEPTH:
            adaln_begin(l + 1)
        for hf in range(2):
            def up_evac(j, fc, t, pi):
                hc = j * 4 + fc
                ta, ta_r = rt[cnt["n"] % 3]
                cnt["n"] += 1
                P.op("act", lambda e: e.activation(out=ta[:, :], in_=ps[pi][:, :], func=AF.Relu),
                     reads=[ps_r[pi]], writes=[ta_r])
                P.op("dve", lambda e: e.tensor_tensor(out=hid[:, hc, t * 512:(t + 1) * 512],
                                                      in0=ta[:, :], in1=ta[:, :], op=ALU.mult),
                     reads=[ta_r], writes=[hid_r[t]])
            linear_fm("up", l, 4, 8, hT, hT_r, up_evac, j0=hf * 4, hook=hook, pshi=pshi)
            g2 = resid_evac(40)

            def down_evac(j, fc, t, pi):
                g2(j, fc, t, pi, per_piece=2)
            ltd = None
            if hf == 1:
                nctx = norm_begin("mod", l + 1, 0) if l + 1 < DEPTH else norm_begin("final")
                ltd = norm_tile_done(nctx, lag=2)
            linear_fm("down", l, 4, 16, hid, hid_r, down_evac, fc_per_piece=2, j0=hf * 4, hook=hook,
                      pshi=pshi, last_tile_done=ltd)

    def final_norm():
        flush_deferred()

    def attn_layer(l):
        i = l // 2
        big.reset()
        qkT, _ = big.alloc(16 * TOK * 2, BF16, "qkT")
        qkT = qkT.rearrange("p (c t) -> p c t", c=16)
        qk_r = [[Res(f"qk{c}_{t}", init=big.inherit) for t in range(NT)] for c in range(16)]
        Vb, _ = big.alloc(12 * 1024 * 2, BF16, "Vb")
        Vb = Vb.rearrange("p (b f) -> p b f", b=12)
        V_r = [Res(f"V{b}", init=big.inherit) for b in range(12)]
        for c in range(16):
            big.live.extend(qk_r[c])
        big.live.extend(V_r)
        gt, gt_r = big.alloc(2048 * 2, BF16, "gtab")
        sem_gt = P.new_sem(f"gt{l}")
        P.dma("pool", lambda e: e.dma_start(out=gt[0:16, :], in_=gt_d), sem_gt, writes=[gt_r])
        P.dma("pool", lambda e: e.dma_start(out=gt[64:80, :], in_=gt_d), sem_gt, writes=[gt_r])

        scr.reset()
        G = []
        for n in range(4):
            ap, r = scr.alloc(2048, F32, f"G{n}")
            G.append((ap, r, scr.last_off))
        zb = [(G[0][0], G[0][1]), (G[1][0], G[1][1])]
        pT = []
        for n in (2, 3):
            bfv = scr.view(G[n][2], 2048, BF16)
            for hh in range(2):
                r = Res(f"pT{n}_{hh}", init=scr.inherit)
                scr.live.append(r)
                pT.append((bfv[:, hh * 512:(hh + 1) * 512], r))
        stg = [(G[0][0], [G[0][1]]), (G[1][0], [G[1][1]]),
               (G[2][0], [pT[0][1], pT[1][1]]), (G[3][0], [pT[2][1], pT[3][1]])]
        stg_sem = [P.new_sem(f"st{l}_{n}") for n in range(4)]
        out_sems.extend(stg_sem)
        cnt = {"n": 0}
        Tb = [scr.alloc(16 * 64 * 2, BF16, f"T{n}") for n in range(4)]
        Hks = []
        for n in range(2):
            hk, hk_r = scr.alloc(17 * 64 * 2, BF16, f"hank{n}")
            Hks.append((hk.rearrange("p (a b) -> p a b", a=17), hk_r, P.new_sem(f"hk{l}_{n}")))
        ckT, ckT_r = scr.alloc(8 * 256 * 2, BF16, "ckT")
        ckT = ckT.rearrange("p (c k) -> p c k", c=8)
        cvb, cvb_r = scr.alloc(2 * 1024 * 2, BF16, "cvb")
        cvb = cvb.rearrange("p (c f) -> p c f", c=2)
        sem_ck = P.new_sem(f"ck{l}")
        sem_cv = P.new_sem(f"cv{l}")
        P.dma("pool", lambda e: e.dma_start(
            out=ckT[:, :, :], in_=ckT_d[i].rearrange("(c p) k -> p c k", p=128)),
            sem_ck, writes=[ckT_r])
        P.dma("pool", lambda e: e.dma_start(
            out=cvb[:, :, :], in_=cv_d[i].rearrange("(c p) f -> p c f", p=128)),
            sem_cv, writes=[cvb_r])

        cn = {"z": 0, "p": 0}

        def hankel_load(h):
            src = bass.AP(rpb_h, i * 7696 + 128 + (h * 15 - 1) * 31 - 48, [[1, 64], [31, 17], [1, 64]])
            Hk, Hk_r, hk_sem = Hks[h % 2]
            P.dma("pool", lambda e: e.dma_start(out=Hk[0:64, :, :], in_=src), hk_sem, writes=[Hk_r])

        def build_T(h, slot, load=True):
            Tt, Tt_r = Tb[slot]
            Tt = Tt.rearrange("p (o q) -> p o q", o=16)
            Hk, Hk_r, hk_sem = Hks[h % 2]
            if load:
                hankel_load(h)
            for half in range(2):
                pi = next_ps(0, 4)

                def fn(e, half=half, pi=pi):
                    ins = None
                    for oo in range(8):
                        a0 = half * 8 + oo
                        ins = e.matmul(ps[pi][:, (7 - oo) * 64:(8 - oo) * 64],
                                       lhsT=Hk[0:64, a0:a0 + 2, :].rearrange("p a b -> p (a b)"),
                                       rhs=Jb[0:64, :], start=True, stop=True)
                    return ins
                P.op("pe", fn, reads=[Hk_r, const_r], writes=[ps_r[pi]])
                P.op("dve", lambda e, half=half, pi=pi: e.tensor_tensor(
                    out=Tt[:, (1 - half) * 8:(2 - half) * 8, :],
                    in0=ps[pi][:, :].rearrange("p (o q) -> p o q", o=8),
                    in1=V("cmask", 64).unsqueeze(1).to_broadcast([128, 8, 64]), op=ALU.add),
                    reads=[ps_r[pi], vecs_r], writes=[Tt_r])
            return Tt, Tt_r

        hankel_load(0)
        hankel_load(1)

        def qk_evac(j, fc, t, pi):
            fch = j * 4 + fc
            if fch < 8:
                P.op("act", lambda e: e.activation(out=qkT[:, fch, t * 512:(t + 1) * 512],
                                                   in_=ps[pi][:, :], func=AF.Copy, scale=0.125),
                     reads=[ps_r[pi]], writes=[qk_r[fch][t]])
            else:
                n = cnt["n"] % 4
                cnt["n"] += 1
                sg, sg_r = stg[n]
                P.op("dve", lambda e: e.tensor_copy(out=sg[:, :], in_=ps[pi][:, :]),
                     reads=[ps_r[pi]], writes=sg_r)
                P.op("act", lambda e: e.activation(out=qkT[:, fch, t * 512:(t + 1) * 512],
                                                   in_=sg[:, :], func=AF.Copy),
                     reads=sg_r, writes=[qk_r[fch][t]])
                kc = fch - 8
                P.dma("sp", lambda e: e.dma_start(
                    out=kT_d[i, kc * 128:(kc + 1) * 128, t * 512:(t + 1) * 512], in_=sg[:, :]),
                    stg_sem[n], reads=sg_r)
        pshi = 8
        linear_fm("qk", l, 4, 8, hT, hT_r, qk_evac, hook=adaln_piece, pshi=pshi)

        T_first = [build_T(0, 0, load=False), build_T(1, 1, load=False)]

        for j in range(2):
            s = use_piece("v", l, j)
            for b in range(12):
                pi = next_ps(0, pshi)
                t = b // 4

                def fn(e, s=s, b=b, pi=pi):
                    ins = None
                    for kc in range(8):
                        ins = e.matmul(ps[pi][:, :], lhsT=hT[:, kc, b * 128:(b + 1) * 128],
                                       rhs=ring[s][:, kc * 512:(kc + 1) * 512],
                                       start=(kc == 0), stop=(kc == 7))
                    return ins
                P.op("pe", fn, reads=[ring_r[s], hT_r[t]], writes=[ps_r[pi]])
                n = cnt["n"] % 4
                cnt["n"] += 1
                sg, sg_r = stg[n]
                P.op("dve", lambda e, sg=sg, pi=pi: e.tensor_copy(out=sg[:, :], in_=ps[pi][:, :]),
                     reads=[ps_r[pi]], writes=sg_r)
                P.op("act", lambda e, sg=sg, b=b, j=j: e.activation(
                    out=Vb[:, b, j * 512:(j + 1) * 512], in_=sg[:, :], func=AF.Copy),
                    reads=sg_r, writes=[V_r[b]])
                P.dma("sp", lambda e, sg=sg, b=b, j=j: e.dma_start(
                    out=vo_d[i, b * 128:(b + 1) * 128, j * 512:(j + 1) * 512], in_=sg[:, :]),
                    stg_sem[n], reads=sg_r)
            done_piece()
            adaln_piece()

        OA = [4, 5]
        SA = [6, 7]

        def finish_pair(o_i, s_i, hp, col0, ncol, tq):
            rc, rc_r = zb[cn["z"] % 2]
            cn["z"] += 1
            P.op("act", lambda e: e.activation(out=rc[:, 0:ncol], in_=ps[s_i][:, 0:ncol], func=AF.Ln),
                 reads=[ps_r[s_i]], writes=[rc_r])
            P.op("act", lambda e: e.activation(out=rc[:, 0:ncol], in_=rc[:, 0:ncol], func=AF.Exp,
                                               scale=-1.0),
                 reads=[rc_r], writes=[rc_r])
            P.op("dve", lambda e: e.tensor_tensor(
                out=hT[:, hp, col0:col0 + ncol], in0=ps[o_i][:, 0:ncol], in1=rc[:, 0:ncol],
                op=ALU.mult),
                reads=[ps_r[o_i], rc_r], writes=[hT_r[tq]])

        def pv_pair(o_i, s_i, args):
            def fn(e):
                ins = None
                for (pb, vsrc, vsrc_r, ptile, ptile_r, c0, n, first) in args:
                    ins = e.matmul(ps[o_i][pb:pb + 64, c0:c0 + n], lhsT=vsrc, rhs=ptile[:, 0:n],
                                   start=first, stop=True, skip_group_check=True)
                for (pb, vsrc, vsrc_r, ptile, ptile_r, c0, n, first) in args:
                    ins = e.matmul(ps[s_i][pb:pb + 64, c0:c0 + n], lhsT=ones_b[:, 0:64],
                                   rhs=ptile[:, 0:n], start=first, stop=True, skip_group_check=True)
                return ins
            reads = [const_r]
            for a in args:
                reads += [a[2], a[4]]
            P.op("pe", fn, reads=reads, writes=[ps_r[o_i], ps_r[s_i]])

        class Blk:
            pass

        def ctx_block(o_i, s_i, hp, qh, par, lc, first):
            b = Blk()
            pb = 64 * par
            h = 2 * hp + par

            def prep():
                b.pi = next_ps(0, 4)
                b.reads = [ckT_r, qk_r[hp][qh]]
            b.prep = prep
            b.mm_s = lambda e: e.matmul(
                ps[b.pi][:, :], lhsT=ckT[pb:pb + 64, hp, lc * 128:(lc + 1) * 128],
                rhs=qkT[pb:pb + 64, hp, qh * 512:(qh + 1) * 512], start=True, stop=True)
            b.mm_g = None

            def post():
                pi = b.pi
                b.pt, b.pt_r = pT[cn["p"] % 4]
                cn["p"] += 1
                pt = b.pt
                P.op("act", lambda e: e.activation(
                    out=pt[:, :], in_=ps[pi][:, :], func=AF.Exp, bias=V("ctxg")),
                    reads=[ps_r[pi], vecs_r], writes=[b.pt_r])
            b.post = post
            b.pvargs = lambda: (pb, cvb[:, lc, h * 64:(h + 1) * 64], cvb_r, b.pt, b.pt_r, 0, 512, first)
            return b

        def own_block(o_i, s_i, hp, qh, par, c, rows, Tt, Tt_r):
            b = Blk()
            pb = 64 * par
            h = 2 * hp + par
            r0, n = rows[0], len(rows) * 64
            c0 = (r0 - qh * 8) * 64
            tk = c // 4

            def prep():
                b.pi = next_ps(0, 4)
                b.reads = [qk_r[8 + hp][tk], qk_r[hp][qh], gt_r]
            b.prep = prep
            b.mm_s = lambda e: e.matmul(
                ps[b.pi][:, 0:n], lhsT=qkT[pb:pb + 64, 8 + hp, c * 128:(c + 1) * 128],
                rhs=qkT[pb:pb + 64, hp, r0 * 64:r0 * 64 + n], start=True, stop=False,
                skip_group_check=True)
            gneed = []
            for rq in rows:
                rs_ = row_start(rq)
                s_ok = all(rs_ <= 2 * c + rl < rs_ + 8 for rl in range(2))
                gneed.append(not (s_ok and (c // 2) == (rq // 4)))
            gidx = [k_ for k_, x_ in enumerate(gneed) if x_]
            if gidx:
                g0, g1 = gidx[0], gidx[-1] + 1
                assert all(gneed[g0:g1])
                b.mm_g = lambda e: e.matmul(
                    ps[b.pi][:, g0 * 64:g1 * 64], lhsT=gt[pb:pb + 16, c * 128:(c + 1) * 128],
                    rhs=gt[pb:pb + 16, 1024 + (r0 + g0) * 64:1024 + (r0 + g1) * 64], start=False, stop=True,
                    skip_group_check=True)
            else:
                b.mm_g = None

            def post():
                pi = b.pi
                z, z_r = zb[cn["z"] % 2]
                cn["z"] += 1
                k = len(rows)
                o0 = 7 - 2 * c + r0
                assert 0 <= o0 and o0 + k <= 16, (c, r0, k)
                P.op("dve", lambda e: e.tensor_tensor(
                    out=z[:, 0:n], in0=ps[pi][:, 0:n],
                    in1=Tt[:, o0:o0 + k, :].rearrange("p o q -> p (o q)"), op=ALU.add),
                    reads=[ps_r[pi], Tt_r], writes=[z_r])
                b.pt, b.pt_r = pT[cn["p"] % 4]
                cn["p"] += 1
                pt = b.pt
                P.op("act", lambda e: e.activation(out=pt[:, 0:n], in_=z[:, 0:n], func=AF.Exp),
                     reads=[z_r], writes=[b.pt_r])
            b.post = post
            b.pvargs = lambda: (pb, Vb[:, c, h * 64:(h + 1) * 64], V_r[c], b.pt, b.pt_r, c0, n, False)
            return b

        def slotb_block(o_i, s_i, hp, par, sq, kc):
            b = Blk()
            pb = 64 * par
            h = 2 * hp + par
            q0 = 1024 + sq * 256
            k0 = q0 + kc * 128
            vb_i = k0 // 128

            def prep():
                b.pi = next_ps(0, 4)
                b.reads = [qk_r[8 + hp][2], qk_r[hp][2]]
            b.prep = prep
            b.mm_s = lambda e: e.matmul(
                ps[b.pi][:, 0:256], lhsT=qkT[pb:pb + 64, 8 + hp, k0:k0 + 128],
                rhs=qkT[pb:pb + 64, hp, q0:q0 + 256], start=True, stop=True)
            b.mm_g = None

            def post():
                pi = b.pi
                b.pt, b.pt_r = pT[cn["p"] % 4]
                cn["p"] += 1
                pt = b.pt
                P.op("act", lambda e: e.activation(
                    out=pt[:, 0:256], in_=ps[pi][:, 0:256], func=AF.Exp),
                    reads=[ps_r[pi]], writes=[b.pt_r])
            b.post = post
            b.pvargs = lambda: (pb, Vb[:, vb_i, h * 64:(h + 1) * 64], V_r[vb_i], b.pt, b.pt_r,
                                sq * 256, 256, kc == 0)
            return b

        def emit_scores(blks):
            for b in blks:
                b.prep()

            def fn(e):
                ins = None
                for b in blks:
                    ins = b.mm_s(e)
                for b in blks:
                    if b.mm_g is not None:
                        ins = b.mm_g(e)
                return ins
            reads = []
            for b in blks:
                reads += b.reads
            P.op("pe", fn, reads=reads, writes=[ps_r[b.pi] for b in blks])
            for b in blks:
                b.post()

        sched = []
        npair = 0
        for hp in range(8):
            sched.append(("preload", hp))
            for qh in range(2):
                if qh == 1:
                    sched.append(("pre", hp))
                o_i, s_i = OA[npair % 2], SA[npair % 2]
                npair += 1
                for lc in range(2):
                    sched.append(("step", o_i, s_i, [("ctx", o_i, s_i, hp, qh, par, lc, lc == 0)
                                                     for par in range(2)]))
                for c in range(8):
                    rows = [r for r in chunk_rows(c) if qh * 8 <= r < qh * 8 + 8]
                    if rows:
                        sched.append(("step", o_i, s_i, [("own", o_i, s_i, hp, qh, par, c, rows)
                                                         for par in range(2)]))
                sched.append(("post", lambda o_i=o_i, s_i=s_i, hp=hp, qh=qh: finish_pair(
                    o_i, s_i, hp, qh * 512, 512, qh)))
            sched.append(("swap", None))
        for hp in range(8):
            o_i, s_i = OA[npair % 2], SA[npair % 2]
            npair += 1
            for sq in range(2):
                for kc in range(2):
                    sched.append(("step", o_i, s_i, [("sb", o_i, s_i, hp, par, sq, kc) for par in range(2)]))
            sched.append(("post", lambda o_i=o_i, s_i=s_i, hp=hp: finish_pair(o_i, s_i, hp, 1024, 512, 2)))

        LAG = 1
        pending = []
        Tstate = {"cur": T_first, "nxt": None}

        def flush_one():
            ent = pending.pop(0)
            if ent[0] == "step":
                _, o_i, s_i, blks = ent
                pv_pair(o_i, s_i, [b.pvargs() for b in blks])
            else:
                ent[1]()

        def nsteps():
            return sum(1 for e_ in pending if e_[0] == "step")

        for ent in sched:
            kind = ent[0]
            if kind == "preload":
                hpn = ent[1]
                if hpn + 1 < 8:
                    hankel_load(2 * hpn + 2)
                    hankel_load(2 * hpn + 3)
            elif kind == "pre":
                hpn = ent[1]
                if hpn + 1 < 8:
                    sl = ((hpn + 1) % 2) * 2
                    Tstate["nxt"] = [build_T(2 * hpn + 2, sl, load=False),
                                     build_T(2 * hpn + 3, sl + 1, load=False)]
            elif kind == "swap":
                if Tstate["nxt"] is not None:
                    Tstate["cur"] = Tstate["nxt"]
                    Tstate["nxt"] = None
            elif kind == "post":
                pending.append(("post", ent[1]))
            else:
                _, o_i, s_i, specs = ent
                blks = []
                for obj in specs:
                    tag = obj[0]
                    if tag == "ctx":
                        b = ctx_block(*obj[1:])
                    elif tag == "own":
                        _, oo, ss, hp, qh, par, c, rows = obj
                        Tt, Tt_r = Tstate["cur"][par]
                        b = own_block(oo, ss, hp, qh, par, c, rows, Tt, Tt_r)
                    else:
                        b = slotb_block(*obj[1:])
                    blks.append(b)
                emit_scores(blks)
                pending.append(("step", o_i, s_i, blks))
                while nsteps() > LAG:
                    flush_one()
                    while pending and pending[0][0] == "post":
                        flush_one()
        while pending:
            flush_one()

        nctx = norm_begin("mod", l, 1)
        linear_fm_t("o", l, 2, hT, hT_r, resid_evac(16), tile_done=norm_tile_done(nctx, lag=3))

    def conv_layer(l):
        i = l // 2
        big.reset()
        up, up_r0 = big.alloc(8 * 6 * SEGP * 2, BF16, "upad")
        up = up.rearrange("p (c s w) -> p c s w", c=8, s=6)
        up_r = [Res(f"up{c}", init=big.inherit) for c in range(8)]
        big.live.extend(up_r)
        vv, _ = big.alloc(8 * TOK * 4, F32, "v")
        vv = vv.rearrange("p (c t) -> p c t", c=8)
        v_r = [[Res(f"v{c}_{t}", init=big.inherit) for t in range(NT)] for c in range(8)]
        for c in range(8):
            big.live.extend(v_r[c])
        scr.reset()
        dg = [scr.alloc(31 * 128 * 2, BF16, f"dg{n}") for n in range(2)]
        sg = [scr.alloc(2048, F32, f"sig{n}") for n in range(2)]
        for c in range(8):
            P.op("pool", lambda e, c=c: e.memset(up[:, c, :, 0:15], 0.0), writes=[up_r[c]])
            P.op("pool", lambda e, c=c: e.memset(up[:, c, :, 271:286], 0.0), writes=[up_r[c]])
        cnt = {"n": 0}
        gbank = {}

        def pw1_evac(j, fc, t, pi):
            if fc < 2:
                gbank[(fc, t)] = pi
                return
            c = 2 * j + (fc - 2)
            pa = gbank[(fc - 2, t)]
            sgt, sgt_r = sg[cnt["n"] % 2]
            cnt["n"] += 1
            P.op("act", lambda e: e.activation(out=sgt[:, :], in_=ps[pi][:, :], func=AF.Sigmoid),
                 reads=[ps_r[pi]], writes=[sgt_r])
            P.op("dve", lambda e: e.tensor_tensor(
                out=up[:, c, 2 * t:2 * t + 2, 15:271],
                in0=ps[pa][:, :].rearrange("p (s w) -> p s w", s=2),
                in1=sgt[:, :].rearrange("p (s w) -> p s w", s=2), op=ALU.mult),
                reads=[ps_r[pa], sgt_r], writes=[up_r[c]])
        sqc = [scr.alloc(2048, BF16, f"sqc{n}") for n in range(2)]
        mean, mean_r = scr.alloc(2048, F32, "mean")
        rstd, rstd_r = scr.alloc(2048, F32, "rstd")
        tmp = sg
        nd = {"n": 0, "q": 0, "t": 0}
        dgB = [Res("dgB0", init=scr.inherit), Res("dgB1", init=scr.inherit)]
        scr.live.extend(dgB)
        diag_q = []
        pending_tail = []
        pending_stats = []
        NPOOL = 22

        def build_diag(c):
            dgt, dgt_r = dg[nd["n"] % 2]
            dgB_r = dgB[nd["n"] % 2]
            nd["n"] += 1
            dgt = dgt.rearrange("p (j m) -> p j m", j=31)
            P.op("pool", lambda e: e.tensor_tensor(
                out=dgt[:, 0:NPOOL, :],
                in0=ident_b[:, :].unsqueeze(1).to_broadcast([128, NPOOL, 128]),
                in1=V("wdw", NPOOL, i * 248 + c * 31).unsqueeze(2).to_broadcast([128, NPOOL, 128]),
                op=ALU.mult),
                reads=[const_r, vecs_r], writes=[dgt_r])
            P.op("dve", lambda e: e.tensor_tensor(
                out=dgt[:, NPOOL:31, :],
                in0=ident_b[:, :].unsqueeze(1).to_broadcast([128, 31 - NPOOL, 128]),
                in1=V("wdw", 31 - NPOOL, i * 248 + c * 31 + NPOOL).unsqueeze(2).to_broadcast(
                    [128, 31 - NPOOL, 128]),
                op=ALU.mult),
                reads=[const_r, vecs_r], writes=[dgB_r])
            diag_q.append((dgt, dgt_r, dgB_r))

        build_diag(0)
        for j in range(4):
            s = use_piece("pw1", l, j)
            for t in range(NT):
                banks = []
                for fc in range(4):
                    pi = next_ps()
                    banks.append(pi)

                    def fn(e, s=s, fc=fc, t=t, pi=pi):
                        ins = None
                        for kc in range(8):
                            ins = e.matmul(ps[pi][:, :],
                                           lhsT=ring[s][:, kc * 512 + fc * 128:kc * 512 + fc * 128 + 128],
                                           rhs=hT[:, kc, t * 512:(t + 1) * 512],
                                           start=(kc == 0), stop=(kc == 7))
                        return ins
                    P.op("pe", fn, reads=[ring_r[s], hT_r[t]], writes=[ps_r[pi]])
                    pw1_evac(j, fc, t, pi)
            done_piece()
        for c in range(8):
            P.op("dve", lambda e, c=c: e.tensor_scalar(
                out=up[:, c, 1:4, 0:15], in0=up[:, c, 0:3, 256:271], scalar1=V("flag"), scalar2=None,
                op0=ALU.mult), reads=[up_r[c], vecs_r], writes=[up_r[c]])
            P.op("dve", lambda e, c=c: e.tensor_scalar(
                out=up[:, c, 0:3, 271:286], in0=up[:, c, 1:4, 15:30], scalar1=V("flag"), scalar2=None,
                op0=ALU.mult), reads=[up_r[c], vecs_r], writes=[up_r[c]])
        for t in range(NT):
            tsl = slice(t * 512, (t + 1) * 512)
            pm, pq = (4, 5) if t % 2 == 0 else (6, 7)

            def stats(c, t=t, tsl=tsl, pm=pm, pq=pq, sq=None, sq_r=None):
                P.op("pe", lambda e: e.matmul(ps[pm][:, :], lhsT=ones_b[:, :], rhs=sq[:, 512:1024],
                                              start=(c == 0), stop=(c == 7)),
                     reads=[sq_r, const_r], writes=[ps_r[pm]])
                P.op("pe", lambda e: e.matmul(ps[pq][:, :], lhsT=ones_b[:, :], rhs=sq[:, 0:512],
                                              start=(c == 0), stop=(c == 7)),
                     reads=[sq_r, const_r], writes=[ps_r[pq]])
            prev = None
            for c in range(8):
                dgt, dgt_r, dgB_r = diag_q.pop(0)
                if not (t == NT - 1 and c == 7):
                    build_diag((c + 1) % 8)
                pi = next_ps(0, 4)

                def fn(e, c=c, t=t, pi=pi, dgt=dgt):
                    ins = None
                    for sgm in range(2):
                        for jt in range(31):
                            ins = e.matmul(ps[pi][:, sgm * 256:(sgm + 1) * 256], lhsT=dgt[:, jt, :],
                                           rhs=up[:, c, 2 * t + sgm, jt:jt + 256],
                                           start=(jt == 0), stop=(jt == 30))
                    return ins
                P.op("pe", fn, reads=[dgt_r, dgB_r, up_r[c]], writes=[ps_r[pi]])
                P.op("act", lambda e, c=c, tsl=tsl, pi=pi: e.activation(
                    out=vv[:, c, tsl], in_=ps[pi][:, :], func=AF.Identity,
                    bias=V("bdw", 1, i * 8 + c)),
                    reads=[ps_r[pi], vecs_r], writes=[v_r[c][t]])
                sq, sq_r = sqc[nd["q"] % 2]
                nd["q"] += 1
                P.op("act", lambda e, c=c, tsl=tsl, sq=sq: e.activation(
                    out=sq[:, 0:512], in_=vv[:, c, tsl], func=AF.Square),
                    reads=[v_r[c][t]], writes=[sq_r])
                P.op("act", lambda e, c=c, tsl=tsl, sq=sq: e.activation(
                    out=sq[:, 512:1024], in_=vv[:, c, tsl], func=AF.Copy),
                    reads=[v_r[c][t]], writes=[sq_r])
                if prev is not None:
                    stats(*prev[:1], sq=prev[1], sq_r=prev[2])
                prev = (c, sq, sq_r)
                if c == 0 and pending_stats:
                    pending_stats.pop(0)()
                if 1 <= c <= 5 and pending_tail:
                    pending_tail.pop(0)()
            def tail_stats(prev=prev, stats=stats):
                stats(*prev[:1], sq=prev[1], sq_r=prev[2])

            def tail(part, t=t, tsl=tsl, pm=pm, pq=pq):
                if part > 0:
                    tail_chunks(part, t, tsl, pm, pq)
                    return
                ta, ta_r = tmp[0]
                P.op("act", lambda e, pm=pm, ta=ta: e.activation(out=ta[:, :], in_=ps[pm][:, :], func=AF.Square,
                                                                 scale=1.0 / D),
                     reads=[ps_r[pm]], writes=[ta_r])
                P.op("act", lambda e, pm=pm: e.activation(out=ps[pm][:, :], in_=ps[pm][:, :], func=AF.Copy,
                                                          scale=1.0 / D),
                     reads=[ps_r[pm]], writes=[ps_r[pm]])
                P.op("dve", lambda e, ta=ta, pq=pq: e.scalar_tensor_tensor(
                    out=rstd[:, :], in0=ps[pq][:, :], scalar=1.0 / D, in1=ta[:, :],
                    op0=ALU.mult, op1=ALU.subtract),
                    reads=[ps_r[pq], ta_r], writes=[rstd_r])
                P.op("act", lambda e: e.activation(out=rstd[:, :], in_=rstd[:, :], func=AF.Ln,
                                                   bias=V("epsl")),
                     reads=[rstd_r, vecs_r], writes=[rstd_r])
                P.op("act", lambda e, pq=pq: e.activation(out=ps[pq][:, :], in_=rstd[:, :], func=AF.Exp,
                                                          scale=-0.5),
                     reads=[rstd_r], writes=[ps_r[pq]])

            def tail_chunks(part, t, tsl, pm, pq):
                for c in range(2 * (part - 1), 2 * part):
                    ta, ta_r = tmp[nd["t"] % 2]
                    nd["t"] += 1
                    P.op("dve", lambda e, c=c, ta=ta, tsl=tsl, pm=pm: e.tensor_tensor(
                        out=ta[:, :], in0=vv[:, c, tsl], in1=ps[pm][:, :], op=ALU.subtract),
                        reads=[v_r[c][t], ps_r[pm]], writes=[ta_r])
                    P.op("dve", lambda e, ta=ta, pq=pq: e.tensor_tensor(
                        out=ta[:, :], in0=ta[:, :], in1=ps[pq][:, :], op=ALU.mult),
                        reads=[ta_r, ps_r[pq]], writes=[ta_r])
                    P.op("act", lambda e, c=c, ta=ta, tsl=tsl: e.activation(
                        out=hT[:, c, tsl], in_=ta[:, :], func=AF.Silu,
                        scale=V("lng", 1, i * 8 + c), bias=V("lnb", 1, i * 8 + c)),
                        reads=[ta_r, vecs_r], writes=[hT_r[t]])
            pending_stats.append(tail_stats)
            for part in range(5):
                pending_tail.append(lambda part=part, tail=tail: tail(part))
        pending_stats.pop(0)()
        while pending_tail:
            pending_tail.pop(0)()
        nctx = norm_begin("mod", l, 1)
        st["psi"] = 6
        linear_fm_t("pw2", l, 2, hT, hT_r, resid_evac(16), tile_done=norm_tile_done(nctx, lag=3))

    del PLAN[:]
    ctx0 = norm_begin("mod", 0, 0)
    rp0 = []
    adaln_begin(0)
    for t in range(NT):
        adaln_piece()
        norm_A(ctx0, t)
        rp0.append(norm_B1(ctx0, t))
    adaln_piece()
    for t in range(NT):
        norm_B2(ctx0, t, *rp0[t])
    for l in range(DEPTH):
        set_layer(l)
        flush_deferred()
        if l % 2 == 0:
            attn_layer(l)
        else:
            conv_layer(l)
        mlp(l)
    final_norm()
    assert st["next_use"] == NPIECE, (st, NPIECE)
    sp = P.eng["sp"]
    for s in out_sems:
        if s.count:
            sp.need(s, s.count)
    P.emit()
    return nc


_CACHE = {}


def kernel(**inp):
    inp = {k: np.asarray(v) for k, v in inp.items()}
    if "nc" not in _CACHE:
        _CACHE["nc"] = build_program()
    nc = _CACHE["nc"]
    wall = build_wall(inp)
    xp = inp["x_prompt"].astype(np.float32, copy=False)
    xs = inp["x_sample"].astype(np.float32, copy=False)
    roles = []
    for core in range(8):
        if core < 4:
            roles.append((core, [2 * core, 2 * core + 1], None))
        else:
            base = 8 + (core - 4) * 6
            roles.append((None, [base + 4, base + 5], [base, base + 1, base + 2, base + 3]))
    in_maps = []
    zeros_ck = np.zeros((2, D, 256), np.float32)
    zeros_cv = np.zeros((2, 256, D), np.float32)
    zeros_rp = np.zeros((2, 7696), np.float32)
    for core in range(8):
        sb, pB, pA = roles[core]
        if sb is not None:
            xa = xs[sb]
        else:
            xa = np.concatenate([xp[p] for p in pA], axis=0)
        xb = np.concatenate([xp[p] for p in pB], axis=0)
        x = np.concatenate([xa, xb], axis=0)
        m = {"xT": np.ascontiguousarray(x.T), "wall": wall, "vecs": build_vecs(inp, sb),
             "gtab": build_gtab(sb is not None)}
        if sb is not None:
            ck = inp["cache_k"][sb].reshape(2, 256, D)
            m["ckT"] = np.ascontiguousarray(ck.transpose(0, 2, 1)).astype(np.float32)
            m["cv"] = np.ascontiguousarray(inp["cache_v"][sb].reshape(2, 256, D)).astype(np.float32)
            rp = np.zeros((2, 7696), np.float32)
            rp[:, 128:128 + 7440] = inp["rpb"].reshape(2, 7440)
            m["rpbp"] = rp
        else:
            m["ckT"], m["cv"], m["rpbp"] = zeros_ck, zeros_cv, zeros_rp
        in_maps.append(m)
    res = run_bass_kernel_spmd(nc, in_maps, core_ids=list(range(8)))
    outs = res.results
    y_prompt = np.empty((32, 256, D), np.float32)
    y_sample = np.empty((4, 1024, D), np.float32)
    nk = np.empty((32, 2, 256, 16, 64), np.float32)
    nv = np.empty((32, 2, 256, 16, 64), np.float32)
    for core in range(8):
        sb, pB, pA = roles[core]
        y = outs[core]["yT"].T
        kt = outs[core]["kTo"].transpose(0, 2, 1)
        vo = outs[core]["vo"]
        if sb is not None:
            y_sample[sb] = y[0:1024]
        else:
            for n, p in enumerate(pA):
                y_prompt[p] = y[n * 256:(n + 1) * 256]
                nk[p] = kt[:, n * 256:(n + 1) * 256].reshape(2, 256, 16, 64)
                nv[p] = vo[:, n * 256:(n + 1) * 256].reshape(2, 256, 16, 64)
        for n, p in enumerate(pB):
            y_prompt[p] = y[1024 + n * 256:1024 + (n + 1) * 256]
            nk[p] = kt[:, 1024 + n * 256:1024 + (n + 1) * 256].reshape(2, 256, 16, 64)
            nv[p] = vo[:, 1024 + n * 256:1024 + (n + 1) * 256].reshape(2, 256, 16, 64)
    return (y_prompt, y_sample, nk, nv)
```
